# Optimizing a Trainium2 kernel written in Bass

```python
import jax, jax.numpy as jnp
from jax import lax
import numpy as np

D_MODEL = 2048
BATCH = 32
SEQ = 256
DEPTH = 4
DEC_BATCH = 2
DEC_SEQ = 1024
PAST_LEN = 512

GRID_W = 64
CHUNK = 128
QBLOCK = 128
A_WIDTH = 1024
A_GROUPS = 8
A_GDIM = A_WIDTH // A_GROUPS
NA_HEADS = 16
NA_HEAD_DIM = 64
NA_WIDTH = NA_HEADS * NA_HEAD_DIM
NA_KH_MAX = 8
NA_KW = 16
D_FF = 4 * D_MODEL
ROPE_THETA = 10000.0
EPS = 1e-6
N_MOD = 6
IN_COLS = 2 * A_WIDTH + 3 * NA_WIDTH + 2 * D_MODEL
SPLITS = [int(s) for s in np.cumsum([A_WIDTH, A_WIDTH, NA_WIDTH, NA_WIDTH, NA_WIDTH, D_MODEL])]

kernel_name = 'hybrid_gmlp_natten_prefix_diffusion_step'


def _rmsnorm(x, g):
    xf = x.astype(jnp.float32)
    y = xf * lax.rsqrt(jnp.mean(xf * xf, axis=-1, keepdims=True) + EPS)
    return (y * g.astype(jnp.float32)).astype(x.dtype)


def _layernorm(x, g, b):
    xf = x.astype(jnp.float32)
    mu = jnp.mean(xf, axis=-1, keepdims=True)
    var = jnp.mean(jnp.square(xf - mu), axis=-1, keepdims=True)
    y = (xf - mu) * lax.rsqrt(var + EPS) * g.astype(jnp.float32) + b.astype(jnp.float32)
    return y.astype(x.dtype)


def _modulation(cvec, w_mod, b_mod):
    m = jax.nn.silu(cvec) @ w_mod + b_mod
    return jnp.split(m[:, None, :], N_MOD, axis=-1)


def _spatial_gating(au, av, ln_g, ln_b, w_s, b_s):
    bsz, seq, _ = au.shape
    u = jax.nn.gelu(au)
    v = _layernorm(jax.nn.gelu(av), ln_g, ln_b)
    v = v.reshape(bsz, seq // CHUNK, CHUNK, A_GROUPS, A_GDIM)
    s = jnp.einsum('gij,bnjgd->bnigd', w_s, v) + b_s.T[None, None, :, :, None]
    return u * s.reshape(bsz, seq, A_WIDTH)


def _heads(x):
    b, l, _ = x.shape
    return x.reshape(b, l, NA_HEADS, NA_HEAD_DIM).transpose(0, 2, 1, 3)


def _merge_heads(x):
    b, h, l, d = x.shape
    return x.transpose(0, 2, 1, 3).reshape(b, l, h * d)


def _rope_axis(x, pos):
    half = x.shape[-1] // 2
    freqs = ROPE_THETA ** (-jnp.arange(half, dtype=jnp.float32) / half)
    ang = pos.astype(jnp.float32)[:, None] * freqs[None, :]
    cos, sin = jnp.cos(ang), jnp.sin(ang)
    xf = x.astype(jnp.float32)
    x1, x2 = xf[..., :half], xf[..., half:]
    return jnp.concatenate([x1 * cos - x2 * sin, x2 * cos + x1 * sin], axis=-1).astype(x.dtype)


def _rope_2d(x):
    t = jnp.arange(x.shape[-2])
    hd = NA_HEAD_DIM // 2
    return jnp.concatenate([_rope_axis(x[..., :hd], t // GRID_W), _rope_axis(x[..., hd:], t % GRID_W)], axis=-1)


def _context_attention(q, k, v):
    b, h, l, d = q.shape
    nb = l // QBLOCK
    scale = d ** -0.5
    qb = q.reshape(b, h, nb, QBLOCK, d).transpose(2, 0, 1, 3, 4)

    def block(qi):
        s = jnp.einsum('bhqd,bhkd->bhqk', qi, k).astype(jnp.float32) * scale
        p = jax.nn.softmax(s, axis=-1).astype(v.dtype)
        return jnp.einsum('bhqk,bhkd->bhqd', p, v)

    o = lax.map(block, qb)
    return o.transpose(1, 2, 0, 3, 4).reshape(b, h, l, d)


def _neighbourhood_attention(q, k, v, k_ctx, v_ctx, rpb):
    b, h, t, d = q.shape
    rows = t // GRID_W
    kh = min(NA_KH_MAX, rows)
    scale = d ** -0.5
    qg = q.reshape(b, h, rows, GRID_W, d)
    kg = k.reshape(b, h, rows, GRID_W, d)
    vg = v.reshape(b, h, rows, GRID_W, d)
    cols = np.arange(GRID_W)
    cstart = np.clip(cols - NA_KW // 2, 0, GRID_W - NA_KW)
    col_mask = jnp.asarray((cols[None, :] >= cstart[:, None]) & (cols[None, :] < cstart[:, None] + NA_KW))
    col_idx = jnp.asarray(np.clip(cols[None, :] - cols[:, None] + NA_KW - 1, 0, 2 * NA_KW - 2))

    def row_block(r):
        rs = jnp.clip(r - kh // 2, 0, rows - kh)
        k_rows = lax.dynamic_slice_in_dim(kg, rs, kh, axis=2)
        v_rows = lax.dynamic_slice_in_dim(vg, rs, kh, axis=2)
        q_row = lax.dynamic_index_in_dim(qg, r, axis=2, keepdims=False)
        row_idx = rs + jnp.arange(kh) - r + NA_KH_MAX - 1
        bias = rpb[:, row_idx][:, :, col_idx].transpose(0, 2, 1, 3)
        s_win = jnp.einsum('bhqd,bhiwd->bhqiw', q_row, k_rows).astype(jnp.float32) * scale + bias.astype(jnp.float32)
        s_win = jnp.where(col_mask[:, None, :], s_win, -jnp.inf)
        s_ctx = jnp.einsum('bhqd,bhkd->bhqk', q_row, k_ctx).astype(jnp.float32) * scale
        s = jnp.concatenate([s_win.reshape(b, h, GRID_W, kh * GRID_W), s_ctx], axis=-1)
        p = jax.nn.softmax(s, axis=-1).astype(v.dtype)
        p_win = p[..., :kh * GRID_W].reshape(b, h, GRID_W, kh, GRID_W)
        p_ctx = p[..., kh * GRID_W:]
        return jnp.einsum('bhqiw,bhiwd->bhqd', p_win, v_rows) + jnp.einsum('bhqk,bhkd->bhqd', p_ctx, v_ctx)

    o = lax.map(row_block, jnp.arange(rows))
    return o.transpose(1, 2, 0, 3, 4).reshape(b, h, t, d)


def _layer(x, cvec, p, attend):
    shift1, scale1, gate1, shift2, scale2, gate2 = _modulation(cvec, p['w_mod'], p['b_mod'])
    h = _rmsnorm(x, p['g_pre_mix']) * (1 + scale1) + shift1
    au, av, q, k, v, ga, gb = jnp.split(h @ p['w_in'], SPLITS, axis=-1)
    ya = _spatial_gating(au, av, p['sgu_ln_g'], p['sgu_ln_b'], p['sgu_w'], p['sgu_b'])
    yb, extra = attend(_heads(q), _heads(k), _heads(v))
    merged = jax.nn.sigmoid(ga) * (ya @ p['w_pa']) + jax.nn.sigmoid(gb) * (_merge_heads(yb) @ p['w_pb'])
    x = x + gate1 * _rmsnorm(merged @ p['w_out'], p['g_post_mix'])
    h = _rmsnorm(x, p['g_pre_ffn']) * (1 + scale2) + shift2
    f = jnp.square(jax.nn.relu(h @ p['w_ff1'])) @ p['w_ff2']
    x = x + gate2 * _rmsnorm(f, p['g_post_ffn'])
    return x, extra


def setup_inputs(seed: int = 0) -> dict:
    key = jax.random.key(seed)
    ks = jax.random.split(key, 24)
    f32 = jnp.float32
    nrm = lambda k, shape, s: jax.random.normal(k, shape, f32) * s
    return {
        'x_prompt': nrm(ks[0], (BATCH, SEQ, D_MODEL), 1.0),
        'x_sample': nrm(ks[1], (DEC_BATCH, DEC_SEQ, D_MODEL), 1.0),
        'cache_ctx_k': nrm(ks[2], (DEC_BATCH, DEPTH, NA_HEADS, PAST_LEN, NA_HEAD_DIM), 1.0),
        'cache_ctx_v': nrm(ks[3], (DEC_BATCH, DEPTH, NA_HEADS, PAST_LEN, NA_HEAD_DIM), 1.0),
        'c': nrm(ks[4], (DEC_BATCH, D_MODEL), 1.0),
        'c_ctx': nrm(ks[5], (D_MODEL,), 1.0),
        'w_mod': nrm(ks[6], (DEPTH, D_MODEL, N_MOD * D_MODEL), 0.5 * D_MODEL ** -0.5),
        'b_mod': nrm(ks[7], (DEPTH, N_MOD * D_MODEL), 0.01),
        'g_pre_mix': 1.0 + nrm(ks[8], (DEPTH, D_MODEL), 0.01),
        'g_post_mix': 1.0 + nrm(ks[9], (DEPTH, D_MODEL), 0.01),
        'g_pre_ffn': 1.0 + nrm(ks[10], (DEPTH, D_MODEL), 0.01),
        'g_post_ffn': 1.0 + nrm(ks[11], (DEPTH, D_MODEL), 0.01),
        'w_in': nrm(ks[12], (DEPTH, D_MODEL, IN_COLS), D_MODEL ** -0.5),
        'sgu_ln_g': 1.0 + nrm(ks[13], (DEPTH, A_WIDTH), 0.01),
        'sgu_ln_b': nrm(ks[14], (DEPTH, A_WIDTH), 0.01),
        'sgu_w': nrm(ks[15], (DEPTH, A_GROUPS, CHUNK, CHUNK), CHUNK ** -0.5),
        'sgu_b': 1.0 + nrm(ks[16], (DEPTH, A_GROUPS, CHUNK), 0.01),
        'na_rpb': nrm(ks[17], (DEPTH, NA_HEADS, 2 * NA_KH_MAX - 1, 2 * NA_KW - 1), 0.1),
        'w_pa': nrm(ks[18], (DEPTH, A_WIDTH, D_MODEL), A_WIDTH ** -0.5),
        'w_pb': nrm(ks[19], (DEPTH, NA_WIDTH, D_MODEL), NA_WIDTH ** -0.5),
        'w_out': nrm(ks[20], (DEPTH, D_MODEL, D_MODEL), D_MODEL ** -0.5),
        'w_ff1': nrm(ks[21], (DEPTH, D_MODEL, D_FF), D_MODEL ** -0.5),
        'w_ff2': nrm(ks[22], (DEPTH, D_FF, D_MODEL), D_FF ** -0.5),
    }


def reference(x_prompt, x_sample, cache_ctx_k, cache_ctx_v, c, c_ctx, w_mod, b_mod, g_pre_mix, g_post_mix,
              g_pre_ffn, g_post_ffn, w_in, sgu_ln_g, sgu_ln_b, sgu_w, sgu_b, na_rpb, w_pa, w_pb, w_out,
              w_ff1, w_ff2):
    xc = x_prompt
    xs = x_sample
    ks_out, vs_out = [], []
    for l in range(DEPTH):
        p = {'w_mod': w_mod[l], 'b_mod': b_mod[l], 'g_pre_mix': g_pre_mix[l], 'g_post_mix': g_post_mix[l],
             'g_pre_ffn': g_pre_ffn[l], 'g_post_ffn': g_post_ffn[l], 'w_in': w_in[l], 'sgu_ln_g': sgu_ln_g[l],
             'sgu_ln_b': sgu_ln_b[l], 'sgu_w': sgu_w[l], 'sgu_b': sgu_b[l], 'w_pa': w_pa[l], 'w_pb': w_pb[l],
             'w_out': w_out[l], 'w_ff1': w_ff1[l], 'w_ff2': w_ff2[l]}
        rpb = na_rpb[l]
        xc, (k_l, v_l) = _layer(xc, c_ctx[None, :], p, lambda q, k, v: (_context_attention(q, k, v), (k, v)))
        ks_out.append(k_l)
        vs_out.append(v_l)
        ck = cache_ctx_k[:, l]
        cv = cache_ctx_v[:, l]
        xs, _ = _layer(xs, c, p, lambda q, k, v, ck=ck, cv=cv, rpb=rpb: (
            _neighbourhood_attention(_rope_2d(q), _rope_2d(k), v, ck, cv, rpb), None))
    state_ctx_k = jnp.stack(ks_out, axis=1)
    state_ctx_v = jnp.stack(vs_out, axis=1)
    return (xc, xs, state_ctx_k, state_ctx_v)
```

```python
import numpy as np
from contextlib import ExitStack
import concourse.bass as bass
import concourse.mybir as mybir
from concourse.bass_utils import run_bass_kernel_spmd

F32 = mybir.dt.float32
BF16 = mybir.dt.bfloat16
AF = mybir.ActivationFunctionType
ALU = mybir.AluOpType

D = 2048
NKC = 16
DEPTH = 4
NH = 16
AW = 1024
DFF = 8192
INC = 9216
EPS = 1e-6
NEG = -30000.0
PK = 94
LROW = 64 + 64 * PK + 64
LROW2 = 64 + 32 * PK + 64
import os
DBG = bool(os.environ.get("KDBG"))
NSLOT = 80
NWB = 3
WBE = 4096


class Op:
    __slots__ = ("eng", "fn", "deps", "signal", "token", "dma_sem", "inc")

    def __init__(self, eng, fn, dma_sem):
        self.eng = eng
        self.fn = fn
        self.deps = set()
        self.signal = False
        self.token = None
        self.dma_sem = dma_sem
        self.inc = 16


class Prog:
    ENGS = ("pe", "act", "dve", "pool", "sp")

    def __init__(self, nc, stack):
        self.nc = nc
        self.stack = stack
        self.ops = []
        self.lastw = {}
        self.readers = {}
        self.dma_cnt = {}
        self.last_barrier = -1
        self.nsem = 0
        self.esem = {e: self.new_sem("e_" + e) for e in self.ENGS}

    def new_sem(self, name):
        self.nsem += 1
        return self.stack.enter_context(self.nc.semaphore(name))

    def add(self, eng, fn, reads=(), writes=(), dma_sem=None, inc=16):
        i = len(self.ops)
        op = Op(eng, fn, dma_sem)
        op.inc = inc
        ops = self.ops
        for r in reads:
            w = self.lastw.get(r)
            if w is not None:
                wo = ops[w]
                if not (wo.eng == eng and eng == "pe" and wo.dma_sem is None):
                    op.deps.add(w)
            if isinstance(r, tuple) and r[0] == "ps":
                for rd in self.readers.get(r, ()):
                    if ops[rd].eng != eng:
                        op.deps.add(rd)
        for r in writes:
            w = self.lastw.get(r)
            if w is not None:
                wo = ops[w]
                if wo.dma_sem is not None or wo.eng != eng or dma_sem is not None:
                    op.deps.add(w)
            for rd in self.readers.get(r, ()):
                ro = ops[rd]
                if ro.dma_sem is not None or ro.eng != eng or dma_sem is not None:
                    op.deps.add(rd)
        for r in reads:
            self.readers.setdefault(r, []).append(i)
        for r in writes:
            self.lastw[r] = i
            self.readers[r] = []
        for d in op.deps:
            ops[d].signal = True
        if dma_sem is not None:
            c = self.dma_cnt.get(dma_sem, 0) + 1
            self.dma_cnt[dma_sem] = c
            op.token = (dma_sem, inc * c)
            op.signal = True
        ops.append(op)
        return i

    def barrier(self):
        last = {}
        dmas = []
        for i, op in enumerate(self.ops):
            if op.dma_sem is not None:
                if i > self.last_barrier:
                    dmas.append(i)
            elif op.fn is not None:
                last[op.eng] = i
        deps = set(last.values()) | set(dmas)
        for e in self.ENGS:
            op = Op(e, None, None)
            op.deps = set(d for d in deps if not (self.ops[d].eng == e and self.ops[d].dma_sem is None))
            for d in op.deps:
                self.ops[d].signal = True
            self.ops.append(op)
        self.last_barrier = len(self.ops)
        self.lastw = {k: v for k, v in self.lastw.items() if isinstance(k, tuple) and k[0] in ("W", "ps")
                      or k in ("ones", "scb", "cf", "bmod", "gv") or (isinstance(k, tuple) and k[0] in ("mod", "vecs"))}
        self.readers = {k: v for k, v in self.readers.items() if k in self.lastw}

    def emit(self, final_waits):
        nc = self.nc
        import os as _os
        _n = int(_os.environ.get("KSTOP", "0"))
        if _n:
            self.ops = self.ops[:_n]
            final_waits = [i for i in final_waits if i < _n]
            print("KSTOP", _n, "nsem", self.nsem)
        cnt = {e: 0 for e in self.ENGS}
        for op in self.ops:
            if op.dma_sem is None and op.signal:
                cnt[op.eng] += 1
                op.token = (self.esem[op.eng], cnt[op.eng])
        per = {e: [] for e in self.ENGS}
        for op in self.ops:
            per[op.eng].append(op)
        ops = self.ops

        def run(engname, eng):
            waited = {}
            for op in per[engname]:
                need = {}
                for d in op.deps:
                    sem, val = ops[d].token
                    if need.get(sem, (0, 0))[1] < val:
                        need[sem] = (d, val)
                for sem, (d, val) in sorted(need.items(), key=lambda kv: kv[1][0]):
                    if waited.get(sem, 0) < val:
                        eng.wait_ge(sem, val)
                        waited[sem] = val
                inst = op.fn(eng) if op.fn is not None else None
                if inst is not None and op.signal:
                    if op.dma_sem is not None and op.inc == 1:
                        inst.then_inc(op.dma_sem)
                    elif op.dma_sem is not None:
                        inst.then_inc(op.dma_sem, 16)
                    else:
                        inst.then_inc(self.esem[engname], 1)
            if engname == "sp":
                for i in final_waits:
                    sem, val = ops[i].token
                    eng.wait_ge(sem, val)

        with nc.Block() as block:
            @block.tensor
            def _(e):
                run("pe", e)

            @block.scalar
            def _(e):
                run("act", e)

            @block.vector
            def _(e):
                run("dve", e)

            @block.gpsimd
            def _(e):
                run("pool", e)

            @block.sync
            def _(e):
                run("sp", e)


def build(n_layers=DEPTH, passes=(("S", 0, 256), ("A", 256, 512), ("B", 768, 512)),
          n_tok=1280, group=4, n_cores=8):
    nc = bass.Bass("TRN2", target_bir_lowering=False)
    L = n_layers

    def din(name, shape, dt=F32):
        return nc.dram_tensor(name, list(shape), dt, kind="ExternalInput")

    xin = din("xin", [D, n_tok]).ap()
    cvT = din("cvT", [D, 2]).ap()
    w_mod = din("w_mod", [L, D, 6 * D]).ap()
    b_modT = din("b_modT", [L, 128, 96]).ap()
    gvec = din("gvec", [L, 128, 4, 16]).ap()
    w_in = din("w_in", [L, D, INC]).ap()
    w_pa = din("w_pa", [L, AW, D]).ap()
    w_pb = din("w_pb", [L, AW, D]).ap()
    w_out = din("w_out", [L, D, D]).ap()
    w_ff1 = din("w_ff1", [L, D, DFF]).ap()
    w_ff2 = din("w_ff2", [L, DFF, D]).ap()
    lng = din("lng", [L, AW]).ap()
    lnb = din("lnb", [L, AW]).ap()
    sgu_wT = din("sgu_wT", [L, 128, 8 * 128]).ap()
    sgu_b = din("sgu_b", [L, 1, AW]).ap()
    ropeC = din("ropeC", [128, 256]).ap()
    ropeS = din("ropeS", [128, 256]).ap()
    permT = din("permT", [128, 128]).ap()
    Emat = din("Emat", [16, 64]).ap()
    cmt = din("cmt", [128, 256]).ap()
    rpb = din("rpb", [L, 16, NH, 31]).ap()
    ckT = din("ckT", [L, 1024, 512]).ap()
    cvv = din("cvv", [L, 512, 1024]).ap()
    ident2 = din("ident2", [128, 128]).ap()
    selmat = din("selmat", [128, 256]).ap()

    yout = nc.dram_tensor("yout", [D, n_tok], F32, kind="ExternalOutput").ap()
    kst = nc.dram_tensor("kst", [L, 1024, 1024], F32, kind="ExternalOutput").ap()
    vst = nc.dram_tensor("vst", [L, 1024, 1024], F32, kind="ExternalOutput").ap()

    has_S = any(p[0] == "S" for p in passes)
    kv_loc = nc.dram_tensor("kv_loc", [128, 4096], BF16)
    kv_all = nc.dram_tensor("kv_all", [group * 128, 4096], BF16)
    d3 = nc.dram_tensor("d3", [2, 68 * 32 * PK], BF16)

    with ExitStack() as st:
        P = Prog(nc, st)

        def sb(name, shape, dt):
            return st.enter_context(nc.sbuf_tensor(name, list(shape), dt))

        def dsem(name):
            return P.new_sem(name)

        ARENA = 88064
        arena = sb("arena", [128, ARENA], BF16)
        Wb = sb("W", [128, NWB * WBE], BF16)
        ones = sb("ones", [128, 128], BF16)
        mod = sb("mod", [128, L, 96, 2], F32)
        vecs = sb("vecs", [128, L, 6, 16, 2], F32)
        bmod = sb("bmod", [128, L, 96], F32)
        gv = sb("gv", [128, L, 4, 16], F32)
        scb = sb("scb", [128, NKC, 2], BF16)
        cf = sb("cf", [128, NKC, 2], F32)
        lnst = sb("lnst", [128, 4, 16], F32)
        ps = [st.enter_context(nc.psum_tensor("ps%d" % i, [128, 512], F32)) for i in range(8)]

        sems = {}

        def ksem(k):
            if k not in sems:
                sems[k] = P.new_sem("d%d" % len(sems))
            return sems[k]

        def dma(eng, out, in_, reads=(), writes=(), key=None):
            k = key if key is not None else writes[0]
            return P.add(eng, lambda e: e.dma_start(out=out, in_=in_), reads, writes, dma_sem=ksem(k))

        state = {"nbank": 8, "bank": 0, "wb": 0, "sq": 0, "tmp": 0, "kst": 0, "vst": 0, "relu": 0, "pT": 0}
        out_dmas = []

        def nxt(key, n):
            v = state[key]
            state[key] = (v + 1) % n
            return v

        P.add("pool", lambda e: e.memset(ones[:], 1.0), writes=["ones"])
        dma("sp", cf[:], cvT.rearrange("(c p) j -> p c j", p=128), writes=["cf"])
        dma("sp", bmod[:], b_modT.rearrange("l p n -> p l n"), writes=["bmod"])
        dma("sp", gv[:], gvec.rearrange("l p a c -> p l a c"), writes=["gv"])
        P.add("act", lambda e: e.activation(out=scb[:], in_=cf[:], func=AF.Silu), reads=["cf"], writes=["scb"])

        def load_w(src_ap, nk, nc_):
            b = nxt("wb", NWB)
            dst = Wb[:, b * WBE:b * WBE + nk * nc_].rearrange("p (k n) -> p k n", k=nk)
            dma("pool", dst, src_ap, writes=[("W", b)])
            return dst, ("W", b)

        def wview(w2d, r0, nk, c0, nc_):
            return w2d[r0:r0 + nk * 128, c0:c0 + nc_].rearrange("(k p) n -> p k n", p=128)

        for l in range(L):
            bank = nxt("bank", state["nbank"])
            pm = ps[bank][:, 0:192].rearrange("p (n j) -> p n j", j=2)
            for blk in range(48):
                wb_, wres = load_w(wview(w_mod[l], 0, NKC, blk * 256, 256), NKC, 256)

                def f(e, wb_=wb_, blk=blk, pm=pm):
                    inst = None
                    for nn in range(2):
                        for kc in range(NKC):
                            inst = e.matmul(pm[:, blk * 2 + nn, :], lhsT=wb_[:, kc, nn * 128:(nn + 1) * 128],
                                            rhs=scb[:, kc, :], start=(kc == 0), stop=(kc == NKC - 1))
                    return inst
                P.add("pe", f, reads=[wres, "scb"], writes=[("ps", bank)])
            P.add("dve", lambda e, l=l, pm=pm: e.tensor_tensor(
                out=mod[:, l], in0=pm, in1=bmod[:, l].unsqueeze(2).to_broadcast([128, 96, 2]), op=ALU.add),
                reads=[("ps", bank), "bmod"], writes=[("mod", l)])
            def gb(l, a):
                return gv[:, l, a].unsqueeze(2).to_broadcast([128, 16, 2])
            P.add("dve", lambda e, l=l: e.scalar_tensor_tensor(
                out=vecs[:, l, 0], in0=mod[:, l, 16:32, :], scalar=1.0, in1=gb(l, 0), op0=ALU.add, op1=ALU.mult),
                reads=[("mod", l), "gv"], writes=[("vecs", l, 0)])
            P.add("dve", lambda e, l=l: e.tensor_copy(out=vecs[:, l, 1], in_=mod[:, l, 0:16, :]),
                  reads=[("mod", l)], writes=[("vecs", l, 1)])
            P.add("dve", lambda e, l=l: e.tensor_tensor(out=vecs[:, l, 2], in0=mod[:, l, 32:48, :], in1=gb(l, 1),
                                                        op=ALU.mult),
                  reads=[("mod", l), "gv"], writes=[("vecs", l, 2)])
            P.add("dve", lambda e, l=l: e.scalar_tensor_tensor(
                out=vecs[:, l, 3], in0=mod[:, l, 64:80, :], scalar=1.0, in1=gb(l, 2), op0=ALU.add, op1=ALU.mult),
                reads=[("mod", l), "gv"], writes=[("vecs", l, 3)])
            P.add("dve", lambda e, l=l: e.tensor_copy(out=vecs[:, l, 4], in_=mod[:, l, 48:64, :]),
                  reads=[("mod", l)], writes=[("vecs", l, 4)])
            P.add("dve", lambda e, l=l: e.tensor_tensor(out=vecs[:, l, 5], in0=mod[:, l, 80:96, :], in1=gb(l, 3),
                                                        op=ALU.mult),
                  reads=[("mod", l), "gv"], writes=[("vecs", l, 5)])

        def vres(l):
            return [("vecs", l, a) for a in range(6)]

        def run_pass(pname, col0, T):
            isS = pname == "S"
            SW = T
            cur = [0]

            def carve(nel, dt=BF16, parts=128):
                nb = nel * (2 if dt == F32 else 1)
                nb = (nb + 15) // 16 * 16
                v = arena[0:parts, cur[0]:cur[0] + nb]
                cur[0] += nb
                assert cur[0] <= ARENA, (pname, cur[0])
                return v.bitcast(F32) if dt == F32 else v
            x = carve(NKC * SW, F32).rearrange("p (c t) -> p c t", c=NKC)
            hb = carve(NKC * SW)
            R = carve(64 * SW)
            F1r = carve(8 * SW, F32)
            lngb = carve(AW, F32)
            lnbb = carve(AW, F32)
            wsT = carve(8 * 128)
            bsrow = carve(AW, BF16, 1)
            sq = [carve(SW) for i in range(2)]
            rstd = carve(SW, F32)
            tmpf = [carve(512, F32) for i in range(3)]
            relu_t = [carve(SW) for i in range(2)]
            if not isS:
                pT = [carve(512) for i in range(2)]
                kstg = [carve(512, F32) for i in range(2)]
                vstg = [carve(256, F32) for i in range(2)]
            else:
                pT = [carve(12 * 256) for i in range(1)]
                ropeCs = carve(256, F32)
                ropeSs = carve(256, F32)
                permTs = carve(128)
                Es = carve(64, F32, 16)
                cmts = carve(256, BF16, 128)
                id2 = carve(128, BF16, 128)
                sel2 = carve(256, BF16, 128)
                rpbx = carve(NH * 31, F32, 16).rearrange("p (h c) -> p h c", c=31)
                EP = [carve(32 * 31, BF16, 16)]
                RSp = [carve(32 * PK, BF16, 64)]
                Bpad = [carve(32 * PK, BF16, 128) for i in range(2)]
                Kall = carve(8 * group * 256).rearrange("p (c k) -> p c k", c=8)
                Vall = carve(group * 2 * 1024).rearrange("p (t f) -> p t f", f=1024)
                Kc = carve(8 * 512).rearrange("p (c k) -> p c k", c=8)
                Vc = carve(4 * 1024).rearrange("p (t f) -> p t f", f=1024)
                qb16 = [carve(256) for i in range(2)]
            NPT = len(pT)

            def Rs(slot, n=1):
                return R[:, slot * SW:(slot + n) * SW]

            def Rres(slot, n=1):
                return [("R", s_) for s_ in range(slot, slot + n)]

            def hview(T):
                return hb[:, :].rearrange("p (c t) -> p c t", c=NKC)[:, :, :T]

            P.barrier()
            dma("sp", x[:, :, :T], xin[:, col0:col0 + T].rearrange("(c p) t -> p c t", p=128), writes=["x"])
            if isS:
                dma("sp", ropeCs[:], ropeC, writes=["ropeC"])
                dma("sp", ropeSs[:], ropeS, writes=["ropeS"])
                dma("sp", Es[:], Emat, writes=["Es"])
                dma("pool", permTs[:], permT, writes=["permT"])
                dma("pool", cmts[:], cmt, writes=["cmt"])
                dma("pool", id2[:], ident2, writes=["id2"])
                dma("pool", sel2[:], selmat, writes=["sel2"])
                P.add("pool", lambda e: e.memset(RSp[0][:], 0.0), writes=[("RSp", 0)])
                for i in range(2):
                    P.add("pool", lambda e, i=i: e.memset(Bpad[i][:], 0.0), writes=[("Bpad", i)])
                d3a = d3.ap()
                for i in range(2):
                    dma("sp", d3a[i, :].rearrange("(p m) -> p m", p=68)[0:64, :], Bpad[0][0:64, :],
                        reads=[("Bpad", 0)], writes=[("d3", i)], key=("d3z", i))
                    dma("sp", d3a[i, :].rearrange("(p m) -> p m", p=68)[64:68, :], Bpad[0][0:4, :],
                        reads=[("Bpad", 0)], writes=[("d3", i)], key=("d3z", i))

            def rms_stats(src_fn, src_res, T):
                bank = nxt("bank", state["nbank"])
                for kc in range(NKC):
                    s = nxt("sq", 2)
                    P.add("act", lambda e, kc=kc, s=s: e.activation(out=sq[s][:, :T], in_=src_fn(kc), func=AF.Square),
                          reads=src_res(kc), writes=[("sq", s)])
                    P.add("pe", lambda e, kc=kc, s=s: e.matmul(ps[bank][:, :T], lhsT=ones[:], rhs=sq[s][:, :T],
                                                               start=(kc == 0), stop=(kc == NKC - 1)),
                          reads=[("sq", s), "ones"], writes=[("ps", bank)])
                P.add("act", lambda e: e.activation(out=rstd[:, :T], in_=ps[bank][:, :T], func=AF.Sqrt,
                                                    scale=1.0 / D, bias=EPS),
                      reads=[("ps", bank)], writes=["rstd"])
                P.add("dve", lambda e: e.reciprocal(out=rstd[:, :T], in_=rstd[:, :T]), reads=["rstd"], writes=["rstd"])

            def norm_mod(l, cv, T, ai, bi):
                rms_stats(lambda kc: x[:, kc, :T], lambda kc: ["x"], T)
                hv = hview(T)
                for kc in range(NKC):
                    t = nxt("tmp", 3)
                    P.add("dve", lambda e, kc=kc, t=t: e.scalar_tensor_tensor(
                        out=tmpf[t][:, :T], in0=x[:, kc, :T], scalar=vecs[:, l, ai, kc, cv:cv + 1], in1=rstd[:, :T],
                        op0=ALU.mult, op1=ALU.mult),
                        reads=["x", "rstd"] + vres(l), writes=[("tmp", t)])
                    P.add("act", lambda e, kc=kc, t=t: e.activation(
                        out=hv[:, kc, :], in_=tmpf[t][:, :T], func=AF.Identity, bias=vecs[:, l, bi, kc, cv:cv + 1],
                        scale=1.0),
                        reads=[("tmp", t)] + vres(l), writes=["h"])

            def proj_fm(w2d, r0, nk, c0, ncols, rhs_fn, rhs_res, T, evac, nb0=0):
                bc = WBE // nk
                bc = min(bc, ncols)
                for b in range(ncols // bc):
                    wb_, wres = load_w(wview(w2d, r0, nk, c0 + b * bc, bc), nk, bc)
                    for nn in range(bc // 128):
                        bank = nxt("bank", state["nbank"])

                        def f(e, wb_=wb_, nn=nn, bank=bank):
                            inst = None
                            for kc in range(nk):
                                inst = e.matmul(ps[bank][:, :T], lhsT=wb_[:, kc, nn * 128:(nn + 1) * 128],
                                                rhs=rhs_fn(kc), start=(kc == 0), stop=(kc == nk - 1))
                            return inst
                        P.add("pe", f, reads=[wres] + rhs_res, writes=[("ps", bank)])
                        evac(nb0 + b * (bc // 128) + nn, bank)

            def proj_tm(w2d, c0, ncols, T, evac):
                hv = hview(T)
                for b in range(ncols // 256):
                    wb_, wres = load_w(wview(w2d, 0, NKC, c0 + b * 256, 256), NKC, 256)
                    for tb in range(T // 128):
                        bank = nxt("bank", state["nbank"])

                        def f(e, wb_=wb_, tb=tb, bank=bank):
                            inst = None
                            for kc in range(NKC):
                                inst = e.matmul(ps[bank][:, :256], lhsT=hv[:, kc, tb * 128:(tb + 1) * 128],
                                                rhs=wb_[:, kc, :], start=(kc == 0), stop=(kc == NKC - 1))
                            return inst
                        P.add("pe", f, reads=[wres, "h"], writes=[("ps", bank)])
                        evac(b, tb, bank)

            SL_U, SL_Q, SL_K, SL_V, SL_YB, SL_SGA, SL_VT, SL_SGB = 0, 8, 16, 24, 32, 40, 56, 8

            def layer(l, pname, col0, T):
                cv = 1 if pname == "S" else 0
                PO = "dve" if pname == "S" else "pool"
                NTB = T // 128
                hv = hview(T)
                dma("sp", lngb[:], lng[l:l + 1, :].partition_broadcast(128), writes=["lngb"])
                dma("sp", lnbb[:], lnb[l:l + 1, :].partition_broadcast(128), writes=["lnbb"])
                dma("pool", wsT[:], sgu_wT[l], writes=["wsT"])
                dma("pool", bsrow[:], sgu_b[l], writes=["bsrow"])

                norm_mod(l, cv, T, 0, 1)
                hres = ["h"]
                wl = w_in[l]

                DBG and print("MARK", len(P.ops), '# ---- au -> u = gelu ----')
                def ev_u(n, bank):
                    P.add("act", lambda e: e.activation(out=Rs(SL_U + n)[:, :T], in_=ps[bank][:, :T],
                                                        func=AF.Gelu_apprx_tanh),
                          reads=[("ps", bank)], writes=Rres(SL_U + n))
                proj_fm(wl, 0, NKC, 0, 1024, lambda kc: hv[:, kc, :], hres, T, ev_u)

                DBG and print("MARK", len(P.ops), '# ---- av -> token-major gelu -> VT (bf16) ----')
                VT = Rs(SL_VT, 8).rearrange("p (t f) -> p t f", f=1024)

                def ev_av(b, tb, bank):
                    P.add("act", lambda e: e.activation(out=VT[:, tb, b * 256:(b + 1) * 256], in_=ps[bank][:, :256],
                                                        func=AF.Gelu_apprx_tanh),
                          reads=[("ps", bank)], writes=[("VT", tb, b)])
                proj_tm(wl, 1024, 1024, T, ev_av)

                DBG and print("MARK", len(P.ops), '# ---- LayerNorm over features (token-major) ----')
                for tb in range(NTB):
                    allvt = [("VT", tb, b) for b in range(4)]
                    for c2 in range(2):
                        P.add("dve", lambda e, tb=tb, c2=c2: e.bn_stats(lnst[:, tb, c2 * 6:(c2 + 1) * 6],
                                                                         VT[:, tb, c2 * 512:(c2 + 1) * 512]),
                              reads=allvt, writes=[("lnst", tb, c2)])
                    P.add("dve", lambda e, tb=tb: e.bn_aggr(lnst[:, tb, 12:14], lnst[:, tb, 0:12]),
                          reads=[("lnst", tb, 0), ("lnst", tb, 1)], writes=[("lnmv", tb)])
                    P.add("act", lambda e, tb=tb: e.activation(out=lnst[:, tb, 14:15], in_=lnst[:, tb, 13:14],
                                                               func=AF.Sqrt, scale=1.0, bias=EPS),
                          reads=[("lnmv", tb)], writes=[("lnr", tb)])
                    P.add("dve", lambda e, tb=tb: e.reciprocal(out=lnst[:, tb, 15:16], in_=lnst[:, tb, 14:15]),
                          reads=[("lnr", tb)], writes=[("lnr2", tb)])
                    t = nxt("tmp", 3)
                    t2 = nxt("tmp", 3)
                    for c2 in range(2):
                        sl = slice(c2 * 512, (c2 + 1) * 512)
                        tt = t if c2 == 0 else t2
                        P.add("dve", lambda e, tb=tb, sl=sl, tt=tt: e.tensor_scalar(
                            out=tmpf[tt][:], in0=VT[:, tb, sl], scalar1=lnst[:, tb, 12:13], scalar2=lnst[:, tb, 15:16],
                            op0=ALU.subtract, op1=ALU.mult),
                            reads=allvt + [("lnmv", tb), ("lnr2", tb)], writes=[("tmp", tt)])
                        P.add("pool", lambda e, sl=sl, tt=tt: e.tensor_tensor(out=tmpf[tt][:], in0=tmpf[tt][:],
                                                                               in1=lngb[:, sl], op=ALU.mult),
                              reads=[("tmp", tt), "lngb"], writes=[("tmp", tt)])
                        P.add("pool", lambda e, tb=tb, sl=sl, tt=tt: e.tensor_tensor(out=VT[:, tb, sl], in0=tmpf[tt][:],
                                                                                      in1=lnbb[:, sl], op=ALU.add),
                              reads=[("tmp", tt), "lnbb"], writes=[("VTn", tb, c2)] + [("VT", tb, 2 * c2), ("VT", tb, 2 * c2 + 1)])

                DBG and print("MARK", len(P.ops), '# ---- q, k (feature-major) ----')
                if pname == "S":
                    def ev_rope(dst_slot):
                        def ev(n, bank):
                            qi = nxt("relu", 2)
                            P.add("act", lambda e: e.activation(out=qb16[qi][:, :T], in_=ps[bank][:, :T], func=AF.Identity),
                                  reads=[("ps", bank)], writes=[("qb16", qi)])
                            bank2 = nxt("bank", state["nbank"])
                            P.add("pe", lambda e: e.matmul(ps[bank2][:, :T], lhsT=permTs[:], rhs=qb16[qi][:, :T],
                                                           start=True, stop=True),
                                  reads=[("qb16", qi), "permT"], writes=[("ps", bank2)])
                            t = nxt("tmp", 3)
                            t2 = nxt("tmp", 3)
                            P.add("dve", lambda e: e.tensor_tensor(out=tmpf[t][:, :T], in0=ps[bank][:, :T],
                                                                   in1=ropeCs[:, :T], op=ALU.mult),
                                  reads=[("ps", bank), "ropeC"], writes=[("tmp", t)])
                            P.add("dve", lambda e: e.tensor_tensor(out=tmpf[t2][:, :T], in0=ps[bank2][:, :T],
                                                                   in1=ropeSs[:, :T], op=ALU.mult),
                                  reads=[("ps", bank2), "ropeS"], writes=[("tmp", t2)])
                            P.add("dve", lambda e: e.tensor_tensor(out=Rs(dst_slot + n)[:, :T], in0=tmpf[t][:, :T],
                                                                    in1=tmpf[t2][:, :T], op=ALU.add),
                                  reads=[("tmp", t), ("tmp", t2)], writes=Rres(dst_slot + n))
                        return ev
                    proj_fm(wl, 0, NKC, 2048, 1024, lambda kc: hv[:, kc, :], hres, T, ev_rope(SL_Q))
                    proj_fm(wl, 0, NKC, 3072, 1024, lambda kc: hv[:, kc, :], hres, T, ev_rope(SL_K))
                else:
                    def ev_q(n, bank):
                        P.add("act", lambda e: e.activation(out=Rs(SL_Q + n)[:, :T], in_=ps[bank][:, :T], func=AF.Identity),
                              reads=[("ps", bank)], writes=Rres(SL_Q + n))
                    proj_fm(wl, 0, NKC, 2048, 1024, lambda kc: hv[:, kc, :], hres, T, ev_q)

                    def ev_k(n, bank):
                        kb_ = nxt("kst", 2)
                        P.add("act", lambda e: e.activation(out=kstg[kb_][:, :T], in_=ps[bank][:, :T], func=AF.Identity),
                              reads=[("ps", bank)], writes=[("kstg", kb_)])
                        P.add("dve", lambda e: e.tensor_copy(out=Rs(SL_K + n)[:, :T], in_=ps[bank][:, :T]),
                              reads=[("ps", bank)], writes=Rres(SL_K + n))
                        i = dma("sp", kst[l, n * 128:(n + 1) * 128, col0 - 256:col0 - 256 + T], kstg[kb_][:, :T],
                                reads=[("kstg", kb_)], writes=[("kstg_out", kb_)])
                        out_dmas.append(i)
                    proj_fm(wl, 0, NKC, 3072, 1024, lambda kc: hv[:, kc, :], hres, T, ev_k)

                DBG and print("MARK", len(P.ops), '# ---- v (token-major) ----')
                Vt = Rs(SL_V, 8).rearrange("p (t f) -> p t f", f=1024)

                def ev_v(b, tb, bank):
                    if pname == "S":
                        P.add("act", lambda e: e.activation(out=Vt[:, tb, b * 256:(b + 1) * 256], in_=ps[bank][:, :256],
                                                            func=AF.Identity),
                              reads=[("ps", bank)], writes=[("Vt", tb, b)])
                        return
                    vb_ = nxt("vst", 2)
                    P.add("act", lambda e: e.activation(out=vstg[vb_][:], in_=ps[bank][:, :256], func=AF.Identity),
                          reads=[("ps", bank)], writes=[("vstg", vb_)])
                    P.add("dve", lambda e: e.tensor_copy(out=Vt[:, tb, b * 256:(b + 1) * 256], in_=ps[bank][:, :256]),
                          reads=[("ps", bank)], writes=[("Vt", tb, b)])
                    r0 = col0 - 256 + tb * 128
                    i = dma("sp", vst[l, r0:r0 + 128, b * 256:(b + 1) * 256], vstg[vb_][:],
                            reads=[("vstg", vb_)], writes=[("vstg_out", vb_)])
                    out_dmas.append(i)
                proj_tm(wl, 4096, 1024, T, ev_v)

                DBG and print("MARK", len(P.ops), '# ---- gate A (sigmoid) ----')
                def ev_sga(n, bank):
                    P.add("act", lambda e: e.activation(out=Rs(SL_SGA + n)[:, :T], in_=ps[bank][:, :T], func=AF.Sigmoid),
                          reads=[("ps", bank)], writes=Rres(SL_SGA + n))
                proj_fm(wl, 0, NKC, 5120, 2048, lambda kc: hv[:, kc, :], hres, T, ev_sga)

                DBG and print("MARK", len(P.ops), '# ---- SGU spatial mixing')
                for g in range(8):
                    bank = nxt("bank", state["nbank"])

                    def f(e, g=g, bank=bank):
                        inst = None
                        for tb in range(NTB):
                            e.matmul(ps[bank][:, tb * 128:(tb + 1) * 128], lhsT=VT[:, tb, g * 128:(g + 1) * 128],
                                     rhs=wsT[:, g * 128:(g + 1) * 128], start=True, stop=False)
                            inst = e.matmul(ps[bank][:, tb * 128:(tb + 1) * 128], lhsT=ones[0:1, :],
                                            rhs=bsrow[0:1, g * 128:(g + 1) * 128], start=False, stop=True)
                        return inst
                    P.add("pe", f, reads=[("VTn", tb, c2) for tb in range(NTB) for c2 in range(2)] + ["wsT", "bsrow", "ones"],
                          writes=[("ps", bank)])
                    P.add("dve", lambda e, g=g, bank=bank: e.tensor_tensor(out=Rs(SL_U + g)[:, :T], in0=ps[bank][:, :T],
                                                                          in1=Rs(SL_U + g)[:, :T], op=ALU.mult),
                          reads=[("ps", bank)] + Rres(SL_U + g), writes=Rres(SL_U + g))

                DBG and print("MARK", len(P.ops), '# ---- attention ----')
                if pname != "S":
                    for s in range(T // 256):
                        for pr in range(8):
                            obank = nxt("bank", state["nbank"])
                            dbank = nxt("bank", state["nbank"])
                            for hh in range(2):
                                pb = hh * 64
                                sbank = nxt("bank", state["nbank"])
                                pi = nxt("pT", 2) % NPT

                                def fqk(e, s=s, pr=pr, pb=pb, sbank=sbank):
                                    inst = None
                                    for kb in range(2):
                                        inst = e.matmul(
                                            ps[sbank][:, kb * 256:(kb + 1) * 256],
                                            lhsT=Rs(SL_K + pr)[pb:pb + 64, s * 256 + kb * 128:s * 256 + (kb + 1) * 128],
                                            rhs=Rs(SL_Q + pr)[pb:pb + 64, s * 256:(s + 1) * 256], start=True, stop=True)
                                    return inst
                                P.add("pe", fqk, reads=Rres(SL_K + pr) + Rres(SL_Q + pr), writes=[("ps", sbank)])
                                P.add("act", lambda e, sbank=sbank, pi=pi: e.activation(
                                    out=pT[pi][:, 0:512], in_=ps[sbank][:, :], func=AF.Exp, scale=0.125),
                                    reads=[("ps", sbank)], writes=[("pT", pi, 0)])

                                def fpv(e, s=s, pr=pr, hh=hh, pi=pi, obank=obank, dbank=dbank):
                                    inst = None
                                    for kb in range(2):
                                        tbk = s * 2 + kb
                                        e.matmul(ps[obank][:, hh * 256:(hh + 1) * 256],
                                                 lhsT=Vt[:, tbk, pr * 128:(pr + 1) * 128],
                                                 rhs=pT[pi][:, kb * 256:(kb + 1) * 256], start=(kb == 0), stop=(kb == 1))
                                        inst = e.matmul(ps[dbank][:, hh * 256:(hh + 1) * 256], lhsT=ones[:],
                                                        rhs=pT[pi][:, kb * 256:(kb + 1) * 256], start=(kb == 0),
                                                        stop=(kb == 1))
                                    return inst
                                P.add("pe", fpv,
                                      reads=[("pT", pi, 0), "ones"] + [("Vt", s * 2 + kb, b) for kb in range(2) for b in range(4)],
                                      writes=[("ps", obank), ("ps", dbank)])
                            t = nxt("tmp", 3)
                            P.add("dve", lambda e, dbank=dbank, t=t: e.reciprocal(out=tmpf[t][:], in_=ps[dbank][:]),
                                  reads=[("ps", dbank)], writes=[("tmp", t)])
                            for hh in range(2):
                                pb = hh * 64
                                P.add("dve", lambda e, s=s, pr=pr, hh=hh, pb=pb, obank=obank, t=t: e.tensor_tensor(
                                    out=Rs(SL_YB + pr)[pb:pb + 64, s * 256:(s + 1) * 256],
                                    in0=ps[obank][pb:pb + 64, hh * 256:(hh + 1) * 256],
                                    in1=tmpf[t][pb:pb + 64, hh * 256:(hh + 1) * 256], op=ALU.mult),
                                    reads=[("ps", obank), ("tmp", t)], writes=[("YBh", pr, hh, s)])
                    yb_res = lambda pr: [("YBh", pr, hh, s) for hh in range(2) for s in range(T // 256)]
                else:
                    sample_attention(l, T, Vt)
                    state["nbank"] = 8
                    yb_res = lambda pr: [("YBh", pr, hh, 0) for hh in range(2)]

                DBG and print("MARK", len(P.ops), '# ---- gate B (sigmoid) into Q/K slots ----')
                def ev_sgb(n, bank):
                    P.add("act", lambda e: e.activation(out=Rs(SL_SGB + n)[:, :T], in_=ps[bank][:, :T], func=AF.Sigmoid),
                          reads=[("ps", bank)], writes=Rres(SL_SGB + n) + ([("YBdummy",)] if False else []))
                proj_fm(wl, 0, NKC, 7168, 2048, lambda kc: hv[:, kc, :], hres, T, ev_sgb)

                DBG and print("MARK", len(P.ops), '# ---- pa / pb projections')
                for qt in range(4):
                    wa_, war_ = load_w(wview(w_pa[l], 0, 8, qt * 512, 512), 8, 512)
                    wb2_, wbr_ = load_w(wview(w_pb[l], 0, 8, qt * 512, 512), 8, 512)
                    for nn in range(4):
                        n = qt * 4 + nn
                        c = nn * 128
                        ba = nxt("bank", state["nbank"])
                        bb = nxt("bank", state["nbank"])

                        def fa(e, wa_=wa_, c=c, ba=ba):
                            inst = None
                            for kc in range(8):
                                inst = e.matmul(ps[ba][:, :T], lhsT=wa_[:, kc, c:c + 128], rhs=Rs(SL_U + kc)[:, :T],
                                                start=(kc == 0), stop=(kc == 7))
                            return inst
                        P.add("pe", fa, reads=[war_] + Rres(SL_U, 8), writes=[("ps", ba)])

                        def fb(e, wb2_=wb2_, c=c, bb=bb):
                            inst = None
                            for kc in range(8):
                                inst = e.matmul(ps[bb][:, :T], lhsT=wb2_[:, kc, c:c + 128], rhs=Rs(SL_YB + kc)[:, :T],
                                                start=(kc == 0), stop=(kc == 7))
                            return inst
                        P.add("pe", fb, reads=[wbr_] + [r for pr in range(8) for r in yb_res(pr)], writes=[("ps", bb)])
                        t = nxt("tmp", 3)
                        t2 = nxt("tmp", 3)
                        P.add("dve", lambda e, n=n, ba=ba, t=t: e.tensor_tensor(
                            out=tmpf[t][:, :T], in0=ps[ba][:, :T], in1=Rs(SL_SGA + n)[:, :T], op=ALU.mult),
                            reads=[("ps", ba)] + Rres(SL_SGA + n), writes=[("tmp", t)])
                        P.add("dve", lambda e, n=n, bb=bb, t2=t2: e.tensor_tensor(
                            out=tmpf[t2][:, :T], in0=ps[bb][:, :T], in1=Rs(SL_SGB + n)[:, :T], op=ALU.mult),
                            reads=[("ps", bb)] + Rres(SL_SGB + n), writes=[("tmp", t2)])
                        P.add(PO, lambda e, n=n, t=t, t2=t2: e.tensor_tensor(
                            out=Rs(SL_SGA + n)[:, :T], in0=tmpf[t][:, :T], in1=tmpf[t2][:, :T], op=ALU.add),
                            reads=[("tmp", t), ("tmp", t2)], writes=Rres(SL_SGA + n))

                DBG and print("MARK", len(P.ops), '# ---- w_out -> MO')
                MO = Rs(0, 32).bitcast(F32).rearrange("p (c t) -> p c t", c=NKC)
                mo_res = lambda n: Rres(2 * n, 2)

                def ev_mo(n, bank):
                    P.add("act", lambda e: e.activation(out=MO[:, n, :T], in_=ps[bank][:, :T], func=AF.Identity),
                          reads=[("ps", bank)], writes=mo_res(n))
                proj_fm(w_out[l], 0, NKC, 0, D, lambda kc: Rs(SL_SGA + kc)[:, :T], Rres(SL_SGA, 16), T, ev_mo)
                rms_stats(lambda kc: MO[:, kc, :T], mo_res, T)
                for n in range(NKC):
                    t = nxt("tmp", 3)
                    P.add("dve", lambda e, n=n, t=t: e.scalar_tensor_tensor(
                        out=tmpf[t][:, :T], in0=MO[:, n, :T], scalar=vecs[:, l, 2, n, cv:cv + 1], in1=rstd[:, :T],
                        op0=ALU.mult, op1=ALU.mult),
                        reads=mo_res(n) + ["rstd"] + vres(l), writes=[("tmp", t)])
                    P.add(PO, lambda e, n=n, t=t: e.tensor_tensor(out=x[:, n, :T], in0=x[:, n, :T], in1=tmpf[t][:, :T],
                                                                      op=ALU.add),
                          reads=[("tmp", t), "x"], writes=["x"])

                DBG and print("MARK", len(P.ops), '# ---- FFN ----')
                norm_mod(l, cv, T, 3, 4)

                def ev_ff1(n, bank):
                    r = nxt("relu", 2)
                    P.add("act", lambda e: e.activation(out=relu_t[r][:, :T], in_=ps[bank][:, :T], func=AF.Relu),
                          reads=[("ps", bank)], writes=[("relu", r)])
                    P.add(PO, lambda e: e.tensor_tensor(out=Rs(n)[:, :T], in0=relu_t[r][:, :T], in1=relu_t[r][:, :T],
                                                            op=ALU.mult),
                          reads=[("relu", r)], writes=Rres(n))
                proj_fm(w_ff1[l], 0, NKC, 0, DFF, lambda kc: hv[:, kc, :], hres, T, ev_ff1)

                F0 = hb[:, :].bitcast(F32).rearrange("p (c t) -> p c t", c=8)
                F1 = F1r[:, :].rearrange("p (c t) -> p c t", c=8)

                def Fv(n):
                    return F0[:, n, :T] if n < 8 else F1[:, n - 8, :T]

                def Fres(n):
                    return ["h"] if n < 8 else [("F1", n)]
                for c in range(8):
                    banks = [nxt("bank", state["nbank"]), nxt("bank", state["nbank"])]
                    for g in range(4):
                        wb_, wres = load_w(wview(w_ff2[l], g * 2048, NKC, c * 256, 256), NKC, 256)
                        for nn in range(2):
                            def f(e, wb_=wb_, nn=nn, g=g, bank=banks[nn]):
                                inst = None
                                for kc in range(NKC):
                                    inst = e.matmul(ps[bank][:, :T], lhsT=wb_[:, kc, nn * 128:(nn + 1) * 128],
                                                    rhs=Rs(g * 16 + kc)[:, :T], start=(g == 0 and kc == 0),
                                                    stop=(g == 3 and kc == NKC - 1))
                                return inst
                            P.add("pe", f, reads=[wres] + Rres(g * 16, 16), writes=[("ps", banks[nn])])
                    for nn in range(2):
                        n = c * 2 + nn
                        P.add("act", lambda e, n=n, bank=banks[nn]: e.activation(out=Fv(n), in_=ps[bank][:, :T],
                                                                                 func=AF.Identity),
                              reads=[("ps", banks[nn])], writes=Fres(n) + [("F", n)])
                rms_stats(lambda kc: Fv(kc), lambda kc: [("F", kc)], T)
                for n in range(NKC):
                    t = nxt("tmp", 3)
                    P.add("dve", lambda e, n=n, t=t: e.scalar_tensor_tensor(
                        out=tmpf[t][:, :T], in0=Fv(n), scalar=vecs[:, l, 5, n, cv:cv + 1], in1=rstd[:, :T],
                        op0=ALU.mult, op1=ALU.mult),
                        reads=[("F", n), "rstd"] + vres(l), writes=[("tmp", t)])
                    P.add(PO, lambda e, n=n, t=t: e.tensor_tensor(out=x[:, n, :T], in0=x[:, n, :T], in1=tmpf[t][:, :T],
                                                                      op=ALU.add),
                          reads=[("tmp", t), "x"] + Fres(n), writes=["x"])

            def sample_attention(l, T, Vt):
                G4 = group
                kvl = kv_loc.ap()
                kvl_res = [("kvl", i) for i in range(9)]
                for pr in range(8):
                    dma("sp", kvl[:, pr * 256:(pr + 1) * 256], Rs(SL_K + pr)[:, :256],
                        reads=Rres(SL_K + pr), writes=[("kvl", pr)])
                dma("sp", kvl[:, 2048:4096], Rs(SL_V, 8)[:, 0:2048],
                    reads=[("Vt", tb, b) for tb in range(2) for b in range(4)], writes=[("kvl", 8)])
                if G4 > 1:
                    groups = [list(range(g0, g0 + G4)) for g0 in range(0, n_cores, G4)]
                    P.add("pool", lambda e: e.collective_compute("AllGather", ALU.bypass, groups,
                                                                 [kv_loc.ap().opt()], [kv_all.ap().opt()]),
                          reads=kvl_res, writes=["kv_all"], dma_sem=ksem("cc"), inc=1)
                    src = kv_all.ap()
                    src_res = ["kv_all"]
                else:
                    src = kv_loc.ap()
                    src_res = kvl_res
                kva = src.rearrange("(r p) c -> p r c", p=128)
                kall_res = [("Kall", r) for r in range(G4)]
                vall_res = [("Vall", r) for r in range(G4)]
                for r in range(G4):
                    dma("sp", Kall[:, :, r * 256:(r + 1) * 256], kva[:, r, 0:2048].rearrange("p (c t) -> p c t", c=8),
                        reads=src_res, writes=[("Kall", r)], key=("kall", r))
                    dma("sp", Vall[:, 2 * r:2 * r + 2, :], kva[:, r, 2048:4096].rearrange("p (t f) -> p t f", t=2),
                        reads=src_res, writes=[("Vall", r)], key=("vall", r))
                kva_res = kall_res + vall_res
                dma("pool", Kc[:], ckT[l].rearrange("(c p) k -> p c k", p=128), writes=["Kc"])
                dma("pool", Vc[:], cvv[l].rearrange("(kb p) f -> p kb f", p=128), writes=["Vc"])
                dma("sp", rpbx[:, :, :], rpb[l], writes=["rpbx"])
                state["nbank"] = 6
                state["bank"] = 0
                NKB = 2 * G4
                nblk = NKB + 4
                for hd in range(NH):
                    pr, hh = hd // 2, hd % 2
                    pb = hh * 64
                    Bvs = []
                    for hf in range(2):
                        EPv = EP[0][:, :].rearrange("p (x c) -> p x c", c=31)
                        P.add("dve", lambda e, hd=hd, hf=hf, EPv=EPv: e.tensor_tensor(
                            out=EPv, in0=Es[:, hf * 32:(hf + 1) * 32].unsqueeze(2).to_broadcast([16, 32, 31]),
                            in1=rpbx[:, hd, :].unsqueeze(1).to_broadcast([16, 32, 31]), op=ALU.mult),
                            reads=["Es", "rpbx"], writes=[("EP", 0)])
                        RSv = RSp[0][:, :].rearrange("p (x m) -> p x m", m=PK)
                        for q4 in range(2):
                            bank = nxt("bank", state["nbank"])
                            P.add("pe", lambda e, q4=q4, bank=bank: e.matmul(
                                ps[bank][0:64, 0:496], lhsT=ones[0:16, 0:64], rhs=EP[0][:, q4 * 496:(q4 + 1) * 496],
                                start=True, stop=True),
                                reads=[("EP", 0), "ones"], writes=[("ps", bank)])
                            P.add("act", lambda e, q4=q4, bank=bank, RSv=RSv: e.activation(
                                out=RSv[:, q4 * 16:(q4 + 1) * 16, 0:31],
                                in_=ps[bank][0:64, 0:496].rearrange("p (x c) -> p x c", c=31), func=AF.Identity),
                                reads=[("ps", bank)], writes=[("RSp", 0)])
                        d3h = d3.ap()[hf]
                        wr = bass.AP(d3h.tensor, d3h.offset + 64, [[LROW2 + 1, 64], [1, 32 * PK]])
                        dma("sp", wr, RSp[0][:, :], reads=[("RSp", 0)], writes=[("d3", hf)])
                        rd = bass.AP(d3h.tensor, d3h.offset + 64 + 15, [[LROW2, 64], [1, 32 * PK]])
                        dma("sp", Bpad[hf][pb:pb + 64, :], rd, reads=[("d3", hf)], writes=[("Bpad", hf)])
                        Bvs.append(Bpad[hf][pb:pb + 64, :].rearrange("p (x m) -> p x m", m=PK))
                    pi = nxt("pT", 2) % NPT
                    for kb2 in range(nblk // 2):
                        sbank = nxt("bank", state["nbank"])

                        def fs(e, kb2=kb2, sbank=sbank, pr=pr, pb=pb, Bvs=Bvs):
                            inst = None
                            for j in range(2):
                                kb = kb2 * 2 + j
                                o = ps[sbank][:, j * 256:(j + 1) * 256]
                                if kb < NKB:
                                    e.matmul(o, lhsT=Kall[pb:pb + 64, pr, kb * 128:(kb + 1) * 128],
                                             rhs=Rs(SL_Q + pr)[pb:pb + 64, 0:256], start=True, stop=False)
                                    e.matmul(o, lhsT=id2[pb:pb + 64, :], rhs=cmts[pb:pb + 64, :], start=False, stop=False)
                                    for ql in range(4):
                                        for k2 in range(2):
                                            x0 = (ql % 2) * 16 + 2 * kb + k2
                                            inst = e.matmul(ps[sbank][:, j * 256 + ql * 64:j * 256 + (ql + 1) * 64],
                                                            lhsT=sel2[pb:pb + 64, k2 * 128:(k2 + 1) * 128],
                                                            rhs=Bvs[ql // 2][:, x0, 0:64],
                                                            start=False, stop=(ql == 3 and k2 == 1))
                                else:
                                    kc_ = kb - NKB
                                    inst = e.matmul(o, lhsT=Kc[pb:pb + 64, pr, kc_ * 128:(kc_ + 1) * 128],
                                                    rhs=Rs(SL_Q + pr)[pb:pb + 64, 0:256], start=True, stop=True)
                            return inst
                        P.add("pe", fs, reads=kall_res + ["Kc", "id2", "sel2", "cmt", ("Bpad", 0), ("Bpad", 1)] + Rres(SL_Q + pr),
                              writes=[("ps", sbank)])
                        P.add("act", lambda e, kb2=kb2, sbank=sbank, pi=pi: e.activation(
                            out=pT[pi][:, kb2 * 512:(kb2 + 1) * 512], in_=ps[sbank][:, :], func=AF.Exp, scale=0.125),
                            reads=[("ps", sbank)], writes=[("pT", pi, kb2)])
                    obank, dbank = 6, 7

                    def fpv(e, pr=pr, hh=hh, pi=pi, obank=obank, dbank=dbank):
                        inst = None
                        for kb in range(nblk):
                            lhs = Vall[:, kb, pr * 128:(pr + 1) * 128] if kb < NKB else Vc[:, kb - NKB, pr * 128:(pr + 1) * 128]
                            e.matmul(ps[obank][:, hh * 256:(hh + 1) * 256], lhsT=lhs,
                                     rhs=pT[pi][:, kb * 256:(kb + 1) * 256], start=(kb == 0), stop=(kb == nblk - 1))
                            inst = e.matmul(ps[dbank][:, hh * 256:(hh + 1) * 256], lhsT=ones[:],
                                            rhs=pT[pi][:, kb * 256:(kb + 1) * 256], start=(kb == 0), stop=(kb == nblk - 1))
                        return inst
                    P.add("pe", fpv, reads=[("pT", pi, k) for k in range(nblk // 2)] + vall_res + ["Vc", "ones"],
                          writes=[("ps", obank), ("ps", dbank)])
                    if hh == 1:
                        t = nxt("tmp", 3)
                        P.add("dve", lambda e, dbank=dbank, t=t: e.reciprocal(out=tmpf[t][:], in_=ps[dbank][:]),
                              reads=[("ps", dbank)], writes=[("tmp", t)])
                        for h2 in range(2):
                            p2 = h2 * 64
                            P.add("dve", lambda e, pr=pr, h2=h2, p2=p2, obank=obank, t=t: e.tensor_tensor(
                                out=Rs(SL_YB + pr)[p2:p2 + 64, 0:256],
                                in0=ps[obank][p2:p2 + 64, h2 * 256:(h2 + 1) * 256],
                                in1=tmpf[t][p2:p2 + 64, h2 * 256:(h2 + 1) * 256], op=ALU.mult),
                                reads=[("ps", obank), ("tmp", t)], writes=[("YBh", pr, h2, 0)])


            for l in range(L):
                layer(l, pname, col0, T)
            i = dma("sp", yout[:, col0:col0 + T].rearrange("(c p) t -> p c t", p=128), x[:, :, :T],
                    reads=["x"], writes=[("yout", pname)])
            out_dmas.append(i)

        for (pname, col0, T) in passes:
            run_pass(pname, col0, T)

        P.emit(out_dmas)
    return nc


def _rope_tables(row0):
    half = 16
    freqs = (10000.0 ** (-np.arange(half, dtype=np.float32) / half)).astype(np.float32)
    t = np.arange(256)
    prow = (row0 + t // 64).astype(np.float32)
    pcol = (t % 64).astype(np.float32)
    C = np.zeros((128, 256), np.float32)
    S = np.zeros((128, 256), np.float32)
    for p in range(128):
        d = p % 64
        pos = prow if d < 32 else pcol
        ang = pos * freqs[d % 16]
        C[p] = np.cos(ang.astype(np.float32))
        S[p] = np.sin(ang.astype(np.float32))
    return C, S


def _perm_T():
    M = np.zeros((128, 128), np.float32)
    for m in range(128):
        if m % 32 < 16:
            M[m + 16, m] = -1.0
        else:
            M[m - 16, m] = 1.0
    return M


def _E_table(row0, rows=16, kh=8):
    E = np.zeros((16, 64), np.float32)
    for ql in range(4):
        qr = row0 + ql
        rs = int(np.clip(qr - kh // 2, 0, rows - kh))
        for kr in range(16):
            xi = ql * 16 + kr
            if rs <= kr < rs + kh:
                E[kr - qr + 7, xi] = 8.0
            else:
                E[15, xi] = 8.0
    return E


def _cmt():
    cols = np.arange(64)
    cstart = np.clip(cols - 8, 0, 48)
    m = np.full((64, 4, 64), NEG, np.float32)
    for qc in range(64):
        m[cstart[qc]:cstart[qc] + 16, :, qc] = 0.0
    return np.concatenate([m.reshape(64, 256)] * 2, 0)


def make_inputs(inp, core, n_layers=DEPTH, n_cores=8):
    f = lambda a: np.ascontiguousarray(np.asarray(a, dtype=np.float32))
    L = n_layers
    b = core // 4
    r = core % 4
    xs = np.asarray(inp["x_sample"])[b, r * 256:(r + 1) * 256]
    xp = np.asarray(inp["x_prompt"])[4 * core:4 * core + 4].reshape(1024, D)
    xin = f(np.concatenate([xs, xp], 0).T)
    cvT = f(np.stack([np.asarray(inp["c_ctx"]), np.asarray(inp["c"])[b]], 1))
    gvec = np.stack([np.asarray(inp[k])[:L].reshape(L, 16, 128) for k in
                     ("g_pre_mix", "g_post_mix", "g_pre_ffn", "g_post_ffn")], 1)
    C, S = _rope_tables(4 * r)
    m = {
        "xin": xin, "cvT": cvT,
        "w_mod": inp["w_mod"][:L], "b_modT": f(np.asarray(inp["b_mod"])[:L].reshape(L, 96, 128).transpose(0, 2, 1)),
        "gvec": f(gvec.transpose(0, 3, 1, 2)),
        "w_in": inp["w_in"][:L], "w_pa": inp["w_pa"][:L], "w_pb": inp["w_pb"][:L], "w_out": inp["w_out"][:L],
        "w_ff1": inp["w_ff1"][:L], "w_ff2": inp["w_ff2"][:L],
        "lng": f(np.asarray(inp["sgu_ln_g"])[:L]), "lnb": f(np.asarray(inp["sgu_ln_b"])[:L]),
        "sgu_wT": f(np.asarray(inp["sgu_w"])[:L].transpose(0, 3, 1, 2).reshape(L, 128, 1024)),
        "sgu_b": f(np.asarray(inp["sgu_b"])[:L].reshape(L, 1, 1024)),
        "ropeC": C, "ropeS": S, "permT": _perm_T(), "Emat": _E_table(4 * r), "cmt": _cmt(),
        "rpb": f(np.concatenate([np.asarray(inp["na_rpb"])[:L][..., ::-1].transpose(0, 2, 1, 3),
                                 np.full((L, 1, NH, 31), NEG / 8.0, np.float32)], 1)),
        "ckT": f(np.asarray(inp["cache_ctx_k"])[b, :L].transpose(0, 1, 3, 2).reshape(L, 1024, 512)),
        "cvv": f(np.asarray(inp["cache_ctx_v"])[b, :L].transpose(0, 2, 1, 3).reshape(L, 512, 1024)),
        "ident2": f(np.concatenate([np.concatenate([np.eye(64), np.eye(64)], 1)] * 2, 0)),
        "selmat": f(np.concatenate([np.concatenate([np.eye(64), np.zeros((64, 64)), np.zeros((64, 64)), np.eye(64)], 1)] * 2, 0)),
    }
    return m


_NC_CACHE = {}


def kernel(**inputs):
    inp = {k: np.asarray(v) for k, v in inputs.items()}
    if "full" not in _NC_CACHE:
        _NC_CACHE["full"] = build()
    nc = _NC_CACHE["full"]
    in_maps = [make_inputs(inp, c) for c in range(8)]
    res = run_bass_kernel_spmd(nc, in_maps, core_ids=list(range(8)))
    B, SEQ = 32, 256
    y_prompt = np.zeros((B, SEQ, D), np.float32)
    y_sample = np.zeros((2, 1024, D), np.float32)
    sk = np.zeros((B, DEPTH, NH, SEQ, 64), np.float32)
    sv = np.zeros((B, DEPTH, NH, SEQ, 64), np.float32)
    for c in range(8):
        r = res.results[c]
        yo = np.asarray(r["yout"]).T
        y_sample[c // 4, (c % 4) * 256:(c % 4 + 1) * 256] = yo[:256]
        y_prompt[4 * c:4 * c + 4] = yo[256:].reshape(4, SEQ, D)
        k = np.asarray(r["kst"])
        sk[4 * c:4 * c + 4] = k.reshape(DEPTH, NH, 64, 4, SEQ).transpose(3, 0, 1, 4, 2)
        v = np.asarray(r["vst"])
        sv[4 * c:4 * c + 4] = v.reshape(DEPTH, 4, SEQ, NH, 64).transpose(1, 0, 3, 2, 4)
    return (y_prompt, y_sample, sk, sv)
```

```python
import numpy as np
from contextlib import ExitStack
import concourse.bass as bass
import concourse.mybir as mybir
from concourse.bass_utils import run_bass_kernel_spmd

F32 = mybir.dt.float32
BF16 = mybir.dt.bfloat16
AF = mybir.ActivationFunctionType
ALU = mybir.AluOpType

D = 2048
NKC = 16
DEPTH = 4
NH = 16
AW = 1024
DFF = 8192
INC = 9216
EPS = 1e-6
NEG = -30000.0
PK = 94
LROW = 64 + 64 * PK + 64
LROW2 = 64 + 32 * PK + 64
import os
DBG = bool(os.environ.get("KDBG"))
NSLOT = 80
NWB = 3
WBE = 4096


class Op:
    __slots__ = ("eng", "fn", "deps", "signal", "token", "dma_sem", "inc")

    def __init__(self, eng, fn, dma_sem):
        self.eng = eng
        self.fn = fn
        self.deps = set()
        self.signal = False
        self.token = None
        self.dma_sem = dma_sem
        self.inc = 16


class Prog:
    ENGS = ("pe", "act", "dve", "pool", "sp")

    def __init__(self, nc, stack):
        self.nc = nc
        self.stack = stack
        self.ops = []
        self.lastw = {}
        self.readers = {}
        self.dma_cnt = {}
        self.last_barrier = -1
        self.nsem = 0
        self.esem = {e: self.new_sem("e_" + e) for e in self.ENGS}

    def new_sem(self, name):
        self.nsem += 1
        return self.stack.enter_context(self.nc.semaphore(name))

    def add(self, eng, fn, reads=(), writes=(), dma_sem=None, inc=16):
        i = len(self.ops)
        op = Op(eng, fn, dma_sem)
        op.inc = inc
        ops = self.ops
        for r in reads:
            w = self.lastw.get(r)
            if w is not None:
                wo = ops[w]
                if not (wo.eng == eng and eng == "pe" and wo.dma_sem is None):
                    op.deps.add(w)
            if isinstance(r, tuple) and r[0] == "ps":
                for rd in self.readers.get(r, ()):
                    if ops[rd].eng != eng:
                        op.deps.add(rd)
        for r in writes:
            w = self.lastw.get(r)
            if w is not None:
                wo = ops[w]
                if wo.dma_sem is not None or wo.eng != eng or dma_sem is not None:
                    op.deps.add(w)
            for rd in self.readers.get(r, ()):
                ro = ops[rd]
                if ro.dma_sem is not None or ro.eng != eng or dma_sem is not None:
                    op.deps.add(rd)
        for r in reads:
            self.readers.setdefault(r, []).append(i)
        for r in writes:
            self.lastw[r] = i
            self.readers[r] = []
        for d in op.deps:
            ops[d].signal = True
        if dma_sem is not None:
            c = self.dma_cnt.get(dma_sem, 0) + 1
            self.dma_cnt[dma_sem] = c
            op.token = (dma_sem, inc * c)
            op.signal = True
        ops.append(op)
        return i

    def barrier(self):
        last = {}
        dmas = []
        for i, op in enumerate(self.ops):
            if op.dma_sem is not None:
                if i > self.last_barrier:
                    dmas.append(i)
            elif op.fn is not None:
                last[op.eng] = i
        deps = set(last.values()) | set(dmas)
        for e in self.ENGS:
            op = Op(e, None, None)
            op.deps = set(d for d in deps if not (self.ops[d].eng == e and self.ops[d].dma_sem is None))
            for d in op.deps:
                self.ops[d].signal = True
            self.ops.append(op)
        self.last_barrier = len(self.ops)
        self.lastw = {k: v for k, v in self.lastw.items() if isinstance(k, tuple) and k[0] in ("W", "ps")
                      or k in ("ones", "scb", "cf", "bmod", "gv") or (isinstance(k, tuple) and k[0] in ("mod", "vecs"))}
        self.readers = {k: v for k, v in self.readers.items() if k in self.lastw}

    def emit(self, final_waits):
        nc = self.nc
        import os as _os
        _n = int(_os.environ.get("KSTOP", "0"))
        if _n:
            self.ops = self.ops[:_n]
            final_waits = [i for i in final_waits if i < _n]
            print("KSTOP", _n, "nsem", self.nsem)
        cnt = {e: 0 for e in self.ENGS}
        for op in self.ops:
            if op.dma_sem is None and op.signal:
                cnt[op.eng] += 1
                op.token = (self.esem[op.eng], cnt[op.eng])
        per = {e: [] for e in self.ENGS}
        for op in self.ops:
            per[op.eng].append(op)
        ops = self.ops

        def run(engname, eng):
            waited = {}
            for op in per[engname]:
                need = {}
                for d in op.deps:
                    sem, val = ops[d].token
                    if need.get(sem, (0, 0))[1] < val:
                        need[sem] = (d, val)
                for sem, (d, val) in sorted(need.items(), key=lambda kv: kv[1][0]):
                    if waited.get(sem, 0) < val:
                        eng.wait_ge(sem, val)
                        waited[sem] = val
                inst = op.fn(eng) if op.fn is not None else None
                if inst is not None and op.signal:
                    if op.dma_sem is not None and op.inc == 1:
                        inst.then_inc(op.dma_sem)
                    elif op.dma_sem is not None:
                        inst.then_inc(op.dma_sem, 16)
                    else:
                        inst.then_inc(self.esem[engname], 1)
            if engname == "sp":
                for i in final_waits:
                    sem, val = ops[i].token
                    eng.wait_ge(sem, val)

        with nc.Block() as block:
            @block.tensor
            def _(e):
                run("pe", e)

            @block.scalar
            def _(e):
                run("act", e)

            @block.vector
            def _(e):
                run("dve", e)

            @block.gpsimd
            def _(e):
                run("pool", e)

            @block.sync
            def _(e):
                run("sp", e)


def build(n_layers=DEPTH, passes=(("S", 0, 256), ("A", 256, 512), ("B", 768, 512)),
          n_tok=1280, group=4, n_cores=8):
    nc = bass.Bass("TRN2", target_bir_lowering=False)
    L = n_layers

    def din(name, shape, dt=F32):
        return nc.dram_tensor(name, list(shape), dt, kind="ExternalInput")

    xin = din("xin", [D, n_tok]).ap()
    cvT = din("cvT", [D, 2]).ap()
    w_mod = din("w_mod", [L, D, 6 * D]).ap()
    b_modT = din("b_modT", [L, 128, 96]).ap()
    gvec = din("gvec", [L, 128, 4, 16]).ap()
    w_in = din("w_in", [L, D, INC]).ap()
    w_pa = din("w_pa", [L, AW, D]).ap()
    w_pb = din("w_pb", [L, AW, D]).ap()
    w_out = din("w_out", [L, D, D]).ap()
    w_ff1 = din("w_ff1", [L, D, DFF]).ap()
    w_ff2 = din("w_ff2", [L, DFF, D]).ap()
    lng = din("lng", [L, AW]).ap()
    lnb = din("lnb", [L, AW]).ap()
    sgu_wT = din("sgu_wT", [L, 128, 8 * 128]).ap()
    sgu_b = din("sgu_b", [L, 1, AW]).ap()
    ropeC = din("ropeC", [128, 256]).ap()
    ropeS = din("ropeS", [128, 256]).ap()
    permT = din("permT", [128, 128]).ap()
    Emat = din("Emat", [16, 64]).ap()
    cmt = din("cmt", [128, 256]).ap()
    rpb = din("rpb", [L, 16, NH, 31]).ap()
    ckT = din("ckT", [L, 1024, 512]).ap()
    cvv = din("cvv", [L, 512, 1024]).ap()
    ident2 = din("ident2", [128, 128]).ap()
    selmat = din("selmat", [128, 256]).ap()

    yout = nc.dram_tensor("yout", [D, n_tok], F32, kind="ExternalOutput").ap()
    kst = nc.dram_tensor("kst", [L, 1024, 1024], F32, kind="ExternalOutput").ap()
    vst = nc.dram_tensor("vst", [L, 1024, 1024], F32, kind="ExternalOutput").ap()

    has_S = any(p[0] == "S" for p in passes)
    kv_loc = nc.dram_tensor("kv_loc", [128, 4096], BF16)
    kv_all = nc.dram_tensor("kv_all", [group * 128, 4096], BF16)
    d3 = nc.dram_tensor("d3", [2, 68 * 32 * PK], BF16)

    with ExitStack() as st:
        P = Prog(nc, st)

        def sb(name, shape, dt):
            return st.enter_context(nc.sbuf_tensor(name, list(shape), dt))

        def dsem(name):
            return P.new_sem(name)

        ARENA = 88064
        arena = sb("arena", [128, ARENA], BF16)
        Wb = sb("W", [128, NWB * WBE], BF16)
        ones = sb("ones", [128, 128], BF16)
        mod = sb("mod", [128, L, 96, 2], F32)
        vecs = sb("vecs", [128, L, 6, 16, 2], F32)
        bmod = sb("bmod", [128, L, 96], F32)
        gv = sb("gv", [128, L, 4, 16], F32)
        scb = sb("scb", [128, NKC, 2], BF16)
        cf = sb("cf", [128, NKC, 2], F32)
        lnst = sb("lnst", [128, 4, 16], F32)
        ps = [st.enter_context(nc.psum_tensor("ps%d" % i, [128, 512], F32)) for i in range(8)]

        sems = {}

        def ksem(k):
            if k not in sems:
                sems[k] = P.new_sem("d%d" % len(sems))
            return sems[k]

        def dma(eng, out, in_, reads=(), writes=(), key=None):
            k = key if key is not None else writes[0]
            return P.add(eng, lambda e: e.dma_start(out=out, in_=in_), reads, writes, dma_sem=ksem(k))

        state = {"nbank": 8, "bank": 0, "wb": 0, "sq": 0, "tmp": 0, "kst": 0, "vst": 0, "relu": 0, "pT": 0}
        out_dmas = []

        def nxt(key, n):
            v = state[key]
            state[key] = (v + 1) % n
            return v

        P.add("pool", lambda e: e.memset(ones[:], 1.0), writes=["ones"])
        dma("sp", cf[:], cvT.rearrange("(c p) j -> p c j", p=128), writes=["cf"])
        dma("sp", bmod[:], b_modT.rearrange("l p n -> p l n"), writes=["bmod"])
        dma("sp", gv[:], gvec.rearrange("l p a c -> p l a c"), writes=["gv"])
        P.add("act", lambda e: e.activation(out=scb[:], in_=cf[:], func=AF.Silu), reads=["cf"], writes=["scb"])

        def load_w(src_ap, nk, nc_):
            b = nxt("wb", NWB)
            dst = Wb[:, b * WBE:b * WBE + nk * nc_].rearrange("p (k n) -> p k n", k=nk)
            dma("pool", dst, src_ap, writes=[("W", b)])
            return dst, ("W", b)

        def wview(w2d, r0, nk, c0, nc_):
            return w2d[r0:r0 + nk * 128, c0:c0 + nc_].rearrange("(k p) n -> p k n", p=128)

        for l in range(L):
            bank = nxt("bank", state["nbank"])
            pm = ps[bank][:, 0:192].rearrange("p (n j) -> p n j", j=2)
            for blk in range(48):
                wb_, wres = load_w(wview(w_mod[l], 0, NKC, blk * 256, 256), NKC, 256)

                def f(e, wb_=wb_, blk=blk, pm=pm):
                    inst = None
                    for nn in range(2):
                        for kc in range(NKC):
                            inst = e.matmul(pm[:, blk * 2 + nn, :], lhsT=wb_[:, kc, nn * 128:(nn + 1) * 128],
                                            rhs=scb[:, kc, :], start=(kc == 0), stop=(kc == NKC - 1))
                    return inst
                P.add("pe", f, reads=[wres, "scb"], writes=[("ps", bank)])
            P.add("dve", lambda e, l=l, pm=pm: e.tensor_tensor(
                out=mod[:, l], in0=pm, in1=bmod[:, l].unsqueeze(2).to_broadcast([128, 96, 2]), op=ALU.add),
                reads=[("ps", bank), "bmod"], writes=[("mod", l)])
            def gb(l, a):
                return gv[:, l, a].unsqueeze(2).to_broadcast([128, 16, 2])
            P.add("dve", lambda e, l=l: e.scalar_tensor_tensor(
                out=vecs[:, l, 0], in0=mod[:, l, 16:32, :], scalar=1.0, in1=gb(l, 0), op0=ALU.add, op1=ALU.mult),
                reads=[("mod", l), "gv"], writes=[("vecs", l, 0)])
            P.add("dve", lambda e, l=l: e.tensor_copy(out=vecs[:, l, 1], in_=mod[:, l, 0:16, :]),
                  reads=[("mod", l)], writes=[("vecs", l, 1)])
            P.add("dve", lambda e, l=l: e.tensor_tensor(out=vecs[:, l, 2], in0=mod[:, l, 32:48, :], in1=gb(l, 1),
                                                        op=ALU.mult),
                  reads=[("mod", l), "gv"], writes=[("vecs", l, 2)])
            P.add("dve", lambda e, l=l: e.scalar_tensor_tensor(
                out=vecs[:, l, 3], in0=mod[:, l, 64:80, :], scalar=1.0, in1=gb(l, 2), op0=ALU.add, op1=ALU.mult),
                reads=[("mod", l), "gv"], writes=[("vecs", l, 3)])
            P.add("dve", lambda e, l=l: e.tensor_copy(out=vecs[:, l, 4], in_=mod[:, l, 48:64, :]),
                  reads=[("mod", l)], writes=[("vecs", l, 4)])
            P.add("dve", lambda e, l=l: e.tensor_tensor(out=vecs[:, l, 5], in0=mod[:, l, 80:96, :], in1=gb(l, 3),
                                                        op=ALU.mult),
                  reads=[("mod", l), "gv"], writes=[("vecs", l, 5)])

        def vres(l):
            return [("vecs", l, a) for a in range(6)]

        def run_pass(pname, col0, T):
            isS = pname == "S"
            SW = T
            cur = [0]

            def carve(nel, dt=BF16, parts=128):
                nb = nel * (2 if dt == F32 else 1)
                nb = (nb + 15) // 16 * 16
                v = arena[0:parts, cur[0]:cur[0] + nb]
                cur[0] += nb
                assert cur[0] <= ARENA, (pname, cur[0])
                return v.bitcast(F32) if dt == F32 else v
            x = carve(NKC * SW, F32).rearrange("p (c t) -> p c t", c=NKC)
            hb = carve(NKC * SW)
            R = carve(64 * SW)
            F1r = carve(8 * SW, F32)
            lngb = carve(AW, F32)
            lnbb = carve(AW, F32)
            wsT = carve(8 * 128)
            bsrow = carve(AW, BF16, 1)
            sq = [carve(SW) for i in range(2)]
            rstd = carve(SW, F32)
            tmpf = [carve(512, F32) for i in range(3)]
            relu_t = [carve(SW) for i in range(2)]
            if not isS:
                pT = [carve(512) for i in range(2)]
                kstg = [carve(512, F32) for i in range(2)]
                vstg = [carve(256, F32) for i in range(2)]
            else:
                pT = [carve(12 * 256) for i in range(1)]
                ropeCs = carve(256, F32)
                ropeSs = carve(256, F32)
                permTs = carve(128)
                Es = carve(64, F32, 16)
                cmts = carve(256, BF16, 128)
                id2 = carve(128, BF16, 128)
                sel2 = carve(256, BF16, 128)
                rpbx = carve(NH * 31, F32, 16).rearrange("p (h c) -> p h c", c=31)
                EP = [carve(32 * 31, BF16, 16)]
                RSp = [carve(32 * PK, BF16, 64)]
                Bpad = [carve(32 * PK, BF16, 128) for i in range(2)]
                Kall = carve(8 * group * 256).rearrange("p (c k) -> p c k", c=8)
                Vall = carve(group * 2 * 1024).rearrange("p (t f) -> p t f", f=1024)
                Kc = carve(8 * 512).rearrange("p (c k) -> p c k", c=8)
                Vc = carve(4 * 1024).rearrange("p (t f) -> p t f", f=1024)
                qb16 = [carve(256) for i in range(2)]
            NPT = len(pT)

            def Rs(slot, n=1):
                return R[:, slot * SW:(slot + n) * SW]

            def Rres(slot, n=1):
                return [("R", s_) for s_ in range(slot, slot + n)]

            def hview(T):
                return hb[:, :].rearrange("p (c t) -> p c t", c=NKC)[:, :, :T]

            P.barrier()
            dma("sp", x[:, :, :T], xin[:, col0:col0 + T].rearrange("(c p) t -> p c t", p=128), writes=["x"])
            if isS:
                dma("sp", ropeCs[:], ropeC, writes=["ropeC"])
                dma("sp", ropeSs[:], ropeS, writes=["ropeS"])
                dma("sp", Es[:], Emat, writes=["Es"])
                dma("pool", permTs[:], permT, writes=["permT"])
                dma("pool", cmts[:], cmt, writes=["cmt"])
                dma("pool", id2[:], ident2, writes=["id2"])
                dma("pool", sel2[:], selmat, writes=["sel2"])
                P.add("pool", lambda e: e.memset(RSp[0][:], 0.0), writes=[("RSp", 0)])
                for i in range(2):
                    P.add("pool", lambda e, i=i: e.memset(Bpad[i][:], 0.0), writes=[("Bpad", i)])
                d3a = d3.ap()
                for i in range(2):
                    dma("sp", d3a[i, :].rearrange("(p m) -> p m", p=68)[0:64, :], Bpad[0][0:64, :],
                        reads=[("Bpad", 0)], writes=[("d3", i)], key=("d3z", i))
                    dma("sp", d3a[i, :].rearrange("(p m) -> p m", p=68)[64:68, :], Bpad[0][0:4, :],
                        reads=[("Bpad", 0)], writes=[("d3", i)], key=("d3z", i))

            def rms_stats(src_fn, src_res, T):
                bank = nxt("bank", state["nbank"])
                for kc in range(NKC):
                    s = nxt("sq", 2)
                    P.add("act", lambda e, kc=kc, s=s: e.activation(out=sq[s][:, :T], in_=src_fn(kc), func=AF.Square),
                          reads=src_res(kc), writes=[("sq", s)])
                    P.add("pe", lambda e, kc=kc, s=s: e.matmul(ps[bank][:, :T], lhsT=ones[:], rhs=sq[s][:, :T],
                                                               start=(kc == 0), stop=(kc == NKC - 1)),
                          reads=[("sq", s), "ones"], writes=[("ps", bank)])
                P.add("act", lambda e: e.activation(out=rstd[:, :T], in_=ps[bank][:, :T], func=AF.Sqrt,
                                                    scale=1.0 / D, bias=EPS),
                      reads=[("ps", bank)], writes=["rstd"])
                P.add("dve", lambda e: e.reciprocal(out=rstd[:, :T], in_=rstd[:, :T]), reads=["rstd"], writes=["rstd"])

            def norm_mod(l, cv, T, ai, bi):
                rms_stats(lambda kc: x[:, kc, :T], lambda kc: ["x"], T)
                hv = hview(T)
                for kc in range(NKC):
                    t = nxt("tmp", 3)
                    P.add("dve", lambda e, kc=kc, t=t: e.scalar_tensor_tensor(
                        out=tmpf[t][:, :T], in0=x[:, kc, :T], scalar=vecs[:, l, ai, kc, cv:cv + 1], in1=rstd[:, :T],
                        op0=ALU.mult, op1=ALU.mult),
                        reads=["x", "rstd"] + vres(l), writes=[("tmp", t)])
                    P.add("act", lambda e, kc=kc, t=t: e.activation(
                        out=hv[:, kc, :], in_=tmpf[t][:, :T], func=AF.Identity, bias=vecs[:, l, bi, kc, cv:cv + 1],
                        scale=1.0),
                        reads=[("tmp", t)] + vres(l), writes=["h"])

            def proj_fm(w2d, r0, nk, c0, ncols, rhs_fn, rhs_res, T, evac, nb0=0):
                bc = WBE // nk
                bc = min(bc, ncols)
                for b in range(ncols // bc):
                    wb_, wres = load_w(wview(w2d, r0, nk, c0 + b * bc, bc), nk, bc)
                    for nn in range(bc // 128):
                        bank = nxt("bank", state["nbank"])

                        def f(e, wb_=wb_, nn=nn, bank=bank):
                            inst = None
                            for kc in range(nk):
                                inst = e.matmul(ps[bank][:, :T], lhsT=wb_[:, kc, nn * 128:(nn + 1) * 128],
                                                rhs=rhs_fn(kc), start=(kc == 0), stop=(kc == nk - 1))
                            return inst
                        P.add("pe", f, reads=[wres] + rhs_res, writes=[("ps", bank)])
                        evac(nb0 + b * (bc // 128) + nn, bank)

            def proj_tm(w2d, c0, ncols, T, evac):
                hv = hview(T)
                for b in range(ncols // 256):
                    wb_, wres = load_w(wview(w2d, 0, NKC, c0 + b * 256, 256), NKC, 256)
                    for tb in range(T // 128):
                        bank = nxt("bank", state["nbank"])

                        def f(e, wb_=wb_, tb=tb, bank=bank):
                            inst = None
                            for kc in range(NKC):
                                inst = e.matmul(ps[bank][:, :256], lhsT=hv[:, kc, tb * 128:(tb + 1) * 128],
                                                rhs=wb_[:, kc, :], start=(kc == 0), stop=(kc == NKC - 1))
                            return inst
                        P.add("pe", f, reads=[wres, "h"], writes=[("ps", bank)])
                        evac(b, tb, bank)

            SL_U, SL_Q, SL_K, SL_V, SL_YB, SL_SGA, SL_VT, SL_SGB = 0, 8, 16, 24, 32, 40, 56, 8

            def layer(l, pname, col0, T):
                cv = 1 if pname == "S" else 0
                PO = "dve"
                NTB = T // 128
                hv = hview(T)
                dma("sp", lngb[:], lng[l:l + 1, :].partition_broadcast(128), writes=["lngb"])
                dma("sp", lnbb[:], lnb[l:l + 1, :].partition_broadcast(128), writes=["lnbb"])
                dma("pool", wsT[:], sgu_wT[l], writes=["wsT"])
                dma("pool", bsrow[:], sgu_b[l], writes=["bsrow"])

                norm_mod(l, cv, T, 0, 1)
                hres = ["h"]
                wl = w_in[l]

                DBG and print("MARK", len(P.ops), '# ---- au -> u = gelu ----')
                def ev_u(n, bank):
                    P.add("act", lambda e: e.activation(out=Rs(SL_U + n)[:, :T], in_=ps[bank][:, :T],
                                                        func=AF.Gelu_apprx_tanh),
                          reads=[("ps", bank)], writes=Rres(SL_U + n))
                proj_fm(wl, 0, NKC, 0, 1024, lambda kc: hv[:, kc, :], hres, T, ev_u)

                DBG and print("MARK", len(P.ops), '# ---- av -> token-major gelu -> VT (bf16) ----')
                VT = Rs(SL_VT, 8).rearrange("p (t f) -> p t f", f=1024)

                def ev_av(b, tb, bank):
                    P.add("act", lambda e: e.activation(out=VT[:, tb, b * 256:(b + 1) * 256], in_=ps[bank][:, :256],
                                                        func=AF.Gelu_apprx_tanh),
                          reads=[("ps", bank)], writes=[("VT", tb, b)])
                proj_tm(wl, 1024, 1024, T, ev_av)

                DBG and print("MARK", len(P.ops), '# ---- LayerNorm over features (token-major) ----')
                for tb in range(NTB):
                    allvt = [("VT", tb, b) for b in range(4)]
                    for c2 in range(2):
                        P.add("dve", lambda e, tb=tb, c2=c2: e.bn_stats(lnst[:, tb, c2 * 6:(c2 + 1) * 6],
                                                                         VT[:, tb, c2 * 512:(c2 + 1) * 512]),
                              reads=allvt, writes=[("lnst", tb, c2)])
                    P.add("dve", lambda e, tb=tb: e.bn_aggr(lnst[:, tb, 12:14], lnst[:, tb, 0:12]),
                          reads=[("lnst", tb, 0), ("lnst", tb, 1)], writes=[("lnmv", tb)])
                    P.add("act", lambda e, tb=tb: e.activation(out=lnst[:, tb, 14:15], in_=lnst[:, tb, 13:14],
                                                               func=AF.Sqrt, scale=1.0, bias=EPS),
                          reads=[("lnmv", tb)], writes=[("lnr", tb)])
                    P.add("dve", lambda e, tb=tb: e.reciprocal(out=lnst[:, tb, 15:16], in_=lnst[:, tb, 14:15]),
                          reads=[("lnr", tb)], writes=[("lnr2", tb)])
                    t = nxt("tmp", 3)
                    t2 = nxt("tmp", 3)
                    for c2 in range(2):
                        sl = slice(c2 * 512, (c2 + 1) * 512)
                        tt = t if c2 == 0 else t2
                        P.add("dve", lambda e, tb=tb, sl=sl, tt=tt: e.tensor_scalar(
                            out=tmpf[tt][:], in0=VT[:, tb, sl], scalar1=lnst[:, tb, 12:13], scalar2=lnst[:, tb, 15:16],
                            op0=ALU.subtract, op1=ALU.mult),
                            reads=allvt + [("lnmv", tb), ("lnr2", tb)], writes=[("tmp", tt)])
                        P.add("dve", lambda e, sl=sl, tt=tt: e.tensor_tensor(out=tmpf[tt][:], in0=tmpf[tt][:],
                                                                               in1=lngb[:, sl], op=ALU.mult),
                              reads=[("tmp", tt), "lngb"], writes=[("tmp", tt)])
                        P.add("dve", lambda e, tb=tb, sl=sl, tt=tt: e.tensor_tensor(out=VT[:, tb, sl], in0=tmpf[tt][:],
                                                                                      in1=lnbb[:, sl], op=ALU.add),
                              reads=[("tmp", tt), "lnbb"], writes=[("VTn", tb, c2)] + [("VT", tb, 2 * c2), ("VT", tb, 2 * c2 + 1)])

                DBG and print("MARK", len(P.ops), '# ---- q, k (feature-major) ----')
                if pname == "S":
                    def ev_rope(dst_slot):
                        def ev(n, bank):
                            qi = nxt("relu", 2)
                            P.add("act", lambda e: e.activation(out=qb16[qi][:, :T], in_=ps[bank][:, :T], func=AF.Identity),
                                  reads=[("ps", bank)], writes=[("qb16", qi)])
                            bank2 = nxt("bank", state["nbank"])
                            P.add("pe", lambda e: e.matmul(ps[bank2][:, :T], lhsT=permTs[:], rhs=qb16[qi][:, :T],
                                                           start=True, stop=True),
                                  reads=[("qb16", qi), "permT"], writes=[("ps", bank2)])
                            t = nxt("tmp", 3)
                            t2 = nxt("tmp", 3)
                            P.add("dve", lambda e: e.tensor_tensor(out=tmpf[t][:, :T], in0=ps[bank][:, :T],
                                                                   in1=ropeCs[:, :T], op=ALU.mult),
                                  reads=[("ps", bank), "ropeC"], writes=[("tmp", t)])
                            P.add("dve", lambda e: e.tensor_tensor(out=tmpf[t2][:, :T], in0=ps[bank2][:, :T],
                                                                   in1=ropeSs[:, :T], op=ALU.mult),
                                  reads=[("ps", bank2), "ropeS"], writes=[("tmp", t2)])
                            P.add("dve", lambda e: e.tensor_tensor(out=Rs(dst_slot + n)[:, :T], in0=tmpf[t][:, :T],
                                                                    in1=tmpf[t2][:, :T], op=ALU.add),
                                  reads=[("tmp", t), ("tmp", t2)], writes=Rres(dst_slot + n))
                        return ev
                    proj_fm(wl, 0, NKC, 2048, 1024, lambda kc: hv[:, kc, :], hres, T, ev_rope(SL_Q))
                    proj_fm(wl, 0, NKC, 3072, 1024, lambda kc: hv[:, kc, :], hres, T, ev_rope(SL_K))
                else:
                    def ev_q(n, bank):
                        P.add("act", lambda e: e.activation(out=Rs(SL_Q + n)[:, :T], in_=ps[bank][:, :T], func=AF.Identity),
                              reads=[("ps", bank)], writes=Rres(SL_Q + n))
                    proj_fm(wl, 0, NKC, 2048, 1024, lambda kc: hv[:, kc, :], hres, T, ev_q)

                    def ev_k(n, bank):
                        kb_ = nxt("kst", 2)
                        P.add("act", lambda e: e.activation(out=kstg[kb_][:, :T], in_=ps[bank][:, :T], func=AF.Identity),
                              reads=[("ps", bank)], writes=[("kstg", kb_)])
                        P.add("dve", lambda e: e.tensor_copy(out=Rs(SL_K + n)[:, :T], in_=ps[bank][:, :T]),
                              reads=[("ps", bank)], writes=Rres(SL_K + n))
                        i = dma("sp", kst[l, n * 128:(n + 1) * 128, col0 - 256:col0 - 256 + T], kstg[kb_][:, :T],
                                reads=[("kstg", kb_)], writes=[("kstg_out", kb_)])
                        out_dmas.append(i)
                    proj_fm(wl, 0, NKC, 3072, 1024, lambda kc: hv[:, kc, :], hres, T, ev_k)

                DBG and print("MARK", len(P.ops), '# ---- v (token-major) ----')
                Vt = Rs(SL_V, 8).rearrange("p (t f) -> p t f", f=1024)

                def ev_v(b, tb, bank):
                    if pname == "S":
                        P.add("act", lambda e: e.activation(out=Vt[:, tb, b * 256:(b + 1) * 256], in_=ps[bank][:, :256],
                                                            func=AF.Identity),
                              reads=[("ps", bank)], writes=[("Vt", tb, b)])
                        return
                    vb_ = nxt("vst", 2)
                    P.add("act", lambda e: e.activation(out=vstg[vb_][:], in_=ps[bank][:, :256], func=AF.Identity),
                          reads=[("ps", bank)], writes=[("vstg", vb_)])
                    P.add("dve", lambda e: e.tensor_copy(out=Vt[:, tb, b * 256:(b + 1) * 256], in_=ps[bank][:, :256]),
                          reads=[("ps", bank)], writes=[("Vt", tb, b)])
                    r0 = col0 - 256 + tb * 128
                    i = dma("sp", vst[l, r0:r0 + 128, b * 256:(b + 1) * 256], vstg[vb_][:],
                            reads=[("vstg", vb_)], writes=[("vstg_out", vb_)])
                    out_dmas.append(i)
                proj_tm(wl, 4096, 1024, T, ev_v)

                DBG and print("MARK", len(P.ops), '# ---- gate A (sigmoid) ----')
                def ev_sga(n, bank):
                    P.add("act", lambda e: e.activation(out=Rs(SL_SGA + n)[:, :T], in_=ps[bank][:, :T], func=AF.Sigmoid),
                          reads=[("ps", bank)], writes=Rres(SL_SGA + n))
                proj_fm(wl, 0, NKC, 5120, 2048, lambda kc: hv[:, kc, :], hres, T, ev_sga)

                DBG and print("MARK", len(P.ops), '# ---- SGU spatial mixing')
                for g in range(8):
                    bank = nxt("bank", state["nbank"])

                    def f(e, g=g, bank=bank):
                        inst = None
                        for tb in range(NTB):
                            e.matmul(ps[bank][:, tb * 128:(tb + 1) * 128], lhsT=VT[:, tb, g * 128:(g + 1) * 128],
                                     rhs=wsT[:, g * 128:(g + 1) * 128], start=True, stop=False)
                            inst = e.matmul(ps[bank][:, tb * 128:(tb + 1) * 128], lhsT=ones[0:1, :],
                                            rhs=bsrow[0:1, g * 128:(g + 1) * 128], start=False, stop=True)
                        return inst
                    P.add("pe", f, reads=[("VTn", tb, c2) for tb in range(NTB) for c2 in range(2)] + ["wsT", "bsrow", "ones"],
                          writes=[("ps", bank)])
                    P.add("dve", lambda e, g=g, bank=bank: e.tensor_tensor(out=Rs(SL_U + g)[:, :T], in0=ps[bank][:, :T],
                                                                          in1=Rs(SL_U + g)[:, :T], op=ALU.mult),
                          reads=[("ps", bank)] + Rres(SL_U + g), writes=Rres(SL_U + g))

                DBG and print("MARK", len(P.ops), '# ---- attention ----')
                if pname != "S":
                    for s in range(T // 256):
                        for pr in range(8):
                            obank = nxt("bank", state["nbank"])
                            dbank = nxt("bank", state["nbank"])
                            for hh in range(2):
                                pb = hh * 64
                                sbank = nxt("bank", state["nbank"])
                                pi = nxt("pT", 2) % NPT

                                def fqk(e, s=s, pr=pr, pb=pb, sbank=sbank):
                                    inst = None
                                    for kb in range(2):
                                        inst = e.matmul(
                                            ps[sbank][:, kb * 256:(kb + 1) * 256],
                                            lhsT=Rs(SL_K + pr)[pb:pb + 64, s * 256 + kb * 128:s * 256 + (kb + 1) * 128],
                                            rhs=Rs(SL_Q + pr)[pb:pb + 64, s * 256:(s + 1) * 256], start=True, stop=True)
                                    return inst
                                P.add("pe", fqk, reads=Rres(SL_K + pr) + Rres(SL_Q + pr), writes=[("ps", sbank)])
                                P.add("act", lambda e, sbank=sbank, pi=pi: e.activation(
                                    out=pT[pi][:, 0:512], in_=ps[sbank][:, :], func=AF.Exp, scale=0.125),
                                    reads=[("ps", sbank)], writes=[("pT", pi, 0)])

                                def fpv(e, s=s, pr=pr, hh=hh, pi=pi, obank=obank, dbank=dbank):
                                    inst = None
                                    for kb in range(2):
                                        tbk = s * 2 + kb
                                        e.matmul(ps[obank][:, hh * 256:(hh + 1) * 256],
                                                 lhsT=Vt[:, tbk, pr * 128:(pr + 1) * 128],
                                                 rhs=pT[pi][:, kb * 256:(kb + 1) * 256], start=(kb == 0), stop=(kb == 1))
                                        inst = e.matmul(ps[dbank][:, hh * 256:(hh + 1) * 256], lhsT=ones[:],
                                                        rhs=pT[pi][:, kb * 256:(kb + 1) * 256], start=(kb == 0),
                                                        stop=(kb == 1))
                                    return inst
                                P.add("pe", fpv,
                                      reads=[("pT", pi, 0), "ones"] + [("Vt", s * 2 + kb, b) for kb in range(2) for b in range(4)],
                                      writes=[("ps", obank), ("ps", dbank)])
                            t = nxt("tmp", 3)
                            P.add("dve", lambda e, dbank=dbank, t=t: e.reciprocal(out=tmpf[t][:], in_=ps[dbank][:]),
                                  reads=[("ps", dbank)], writes=[("tmp", t)])
                            for hh in range(2):
                                pb = hh * 64
                                P.add("dve", lambda e, s=s, pr=pr, hh=hh, pb=pb, obank=obank, t=t: e.tensor_tensor(
                                    out=Rs(SL_YB + pr)[pb:pb + 64, s * 256:(s + 1) * 256],
                                    in0=ps[obank][pb:pb + 64, hh * 256:(hh + 1) * 256],
                                    in1=tmpf[t][pb:pb + 64, hh * 256:(hh + 1) * 256], op=ALU.mult),
                                    reads=[("ps", obank), ("tmp", t)], writes=[("YBh", pr, hh, s)])
                    yb_res = lambda pr: [("YBh", pr, hh, s) for hh in range(2) for s in range(T // 256)]
                else:
                    sample_attention(l, T, Vt)
                    state["nbank"] = 8
                    yb_res = lambda pr: [("YBh", pr, hh, 0) for hh in range(2)]

                DBG and print("MARK", len(P.ops), '# ---- gate B (sigmoid) into Q/K slots ----')
                def ev_sgb(n, bank):
                    P.add("act", lambda e: e.activation(out=Rs(SL_SGB + n)[:, :T], in_=ps[bank][:, :T], func=AF.Sigmoid),
                          reads=[("ps", bank)], writes=Rres(SL_SGB + n) + ([("YBdummy",)] if False else []))
                proj_fm(wl, 0, NKC, 7168, 2048, lambda kc: hv[:, kc, :], hres, T, ev_sgb)

                DBG and print("MARK", len(P.ops), '# ---- pa / pb projections')
                for qt in range(4):
                    wa_, war_ = load_w(wview(w_pa[l], 0, 8, qt * 512, 512), 8, 512)
                    wb2_, wbr_ = load_w(wview(w_pb[l], 0, 8, qt * 512, 512), 8, 512)
                    for nn in range(4):
                        n = qt * 4 + nn
                        c = nn * 128
                        ba = nxt("bank", state["nbank"])
                        bb = nxt("bank", state["nbank"])

                        def fa(e, wa_=wa_, c=c, ba=ba):
                            inst = None
                            for kc in range(8):
                                inst = e.matmul(ps[ba][:, :T], lhsT=wa_[:, kc, c:c + 128], rhs=Rs(SL_U + kc)[:, :T],
                                                start=(kc == 0), stop=(kc == 7))
                            return inst
                        P.add("pe", fa, reads=[war_] + Rres(SL_U, 8), writes=[("ps", ba)])

                        def fb(e, wb2_=wb2_, c=c, bb=bb):
                            inst = None
                            for kc in range(8):
                                inst = e.matmul(ps[bb][:, :T], lhsT=wb2_[:, kc, c:c + 128], rhs=Rs(SL_YB + kc)[:, :T],
                                                start=(kc == 0), stop=(kc == 7))
                            return inst
                        P.add("pe", fb, reads=[wbr_] + [r for pr in range(8) for r in yb_res(pr)], writes=[("ps", bb)])
                        t = nxt("tmp", 3)
                        t2 = nxt("tmp", 3)
                        P.add("dve", lambda e, n=n, ba=ba, t=t: e.tensor_tensor(
                            out=tmpf[t][:, :T], in0=ps[ba][:, :T], in1=Rs(SL_SGA + n)[:, :T], op=ALU.mult),
                            reads=[("ps", ba)] + Rres(SL_SGA + n), writes=[("tmp", t)])
                        P.add("dve", lambda e, n=n, bb=bb, t2=t2: e.tensor_tensor(
                            out=tmpf[t2][:, :T], in0=ps[bb][:, :T], in1=Rs(SL_SGB + n)[:, :T], op=ALU.mult),
                            reads=[("ps", bb)] + Rres(SL_SGB + n), writes=[("tmp", t2)])
                        P.add(PO, lambda e, n=n, t=t, t2=t2: e.tensor_tensor(
                            out=Rs(SL_SGA + n)[:, :T], in0=tmpf[t][:, :T], in1=tmpf[t2][:, :T], op=ALU.add),
                            reads=[("tmp", t), ("tmp", t2)], writes=Rres(SL_SGA + n))

                DBG and print("MARK", len(P.ops), '# ---- w_out -> MO')
                MO = Rs(0, 32).bitcast(F32).rearrange("p (c t) -> p c t", c=NKC)
                mo_res = lambda n: Rres(2 * n, 2)

                def ev_mo(n, bank):
                    P.add("act", lambda e: e.activation(out=MO[:, n, :T], in_=ps[bank][:, :T], func=AF.Identity),
                          reads=[("ps", bank)], writes=mo_res(n))
                proj_fm(w_out[l], 0, NKC, 0, D, lambda kc: Rs(SL_SGA + kc)[:, :T], Rres(SL_SGA, 16), T, ev_mo)
                rms_stats(lambda kc: MO[:, kc, :T], mo_res, T)
                for n in range(NKC):
                    t = nxt("tmp", 3)
                    P.add("dve", lambda e, n=n, t=t: e.scalar_tensor_tensor(
                        out=tmpf[t][:, :T], in0=MO[:, n, :T], scalar=vecs[:, l, 2, n, cv:cv + 1], in1=rstd[:, :T],
                        op0=ALU.mult, op1=ALU.mult),
                        reads=mo_res(n) + ["rstd"] + vres(l), writes=[("tmp", t)])
                    P.add(PO, lambda e, n=n, t=t: e.tensor_tensor(out=x[:, n, :T], in0=x[:, n, :T], in1=tmpf[t][:, :T],
                                                                      op=ALU.add),
                          reads=[("tmp", t), "x"], writes=["x"])

                DBG and print("MARK", len(P.ops), '# ---- FFN ----')
                norm_mod(l, cv, T, 3, 4)

                def ev_ff1(n, bank):
                    r = nxt("relu", 2)
                    P.add("act", lambda e: e.activation(out=relu_t[r][:, :T], in_=ps[bank][:, :T], func=AF.Relu),
                          reads=[("ps", bank)], writes=[("relu", r)])
                    P.add(PO, lambda e: e.tensor_tensor(out=Rs(n)[:, :T], in0=relu_t[r][:, :T], in1=relu_t[r][:, :T],
                                                            op=ALU.mult),
                          reads=[("relu", r)], writes=Rres(n))
                proj_fm(w_ff1[l], 0, NKC, 0, DFF, lambda kc: hv[:, kc, :], hres, T, ev_ff1)

                F0 = hb[:, :].bitcast(F32).rearrange("p (c t) -> p c t", c=8)
                F1 = F1r[:, :].rearrange("p (c t) -> p c t", c=8)

                def Fv(n):
                    return F0[:, n, :T] if n < 8 else F1[:, n - 8, :T]

                def Fres(n):
                    return ["h"] if n < 8 else [("F1", n)]
                for c in range(8):
                    banks = [nxt("bank", state["nbank"]), nxt("bank", state["nbank"])]
                    for g in range(4):
                        wb_, wres = load_w(wview(w_ff2[l], g * 2048, NKC, c * 256, 256), NKC, 256)
                        for nn in range(2):
                            def f(e, wb_=wb_, nn=nn, g=g, bank=banks[nn]):
                                inst = None
                                for kc in range(NKC):
                                    inst = e.matmul(ps[bank][:, :T], lhsT=wb_[:, kc, nn * 128:(nn + 1) * 128],
                                                    rhs=Rs(g * 16 + kc)[:, :T], start=(g == 0 and kc == 0),
                                                    stop=(g == 3 and kc == NKC - 1))
                                return inst
                            P.add("pe", f, reads=[wres] + Rres(g * 16, 16), writes=[("ps", banks[nn])])
                    for nn in range(2):
                        n = c * 2 + nn
                        P.add("act", lambda e, n=n, bank=banks[nn]: e.activation(out=Fv(n), in_=ps[bank][:, :T],
                                                                                 func=AF.Identity),
                              reads=[("ps", banks[nn])], writes=Fres(n) + [("F", n)])
                rms_stats(lambda kc: Fv(kc), lambda kc: [("F", kc)], T)
                for n in range(NKC):
                    t = nxt("tmp", 3)
                    P.add("dve", lambda e, n=n, t=t: e.scalar_tensor_tensor(
                        out=tmpf[t][:, :T], in0=Fv(n), scalar=vecs[:, l, 5, n, cv:cv + 1], in1=rstd[:, :T],
                        op0=ALU.mult, op1=ALU.mult),
                        reads=[("F", n), "rstd"] + vres(l), writes=[("tmp", t)])
                    P.add(PO, lambda e, n=n, t=t: e.tensor_tensor(out=x[:, n, :T], in0=x[:, n, :T], in1=tmpf[t][:, :T],
                                                                      op=ALU.add),
                          reads=[("tmp", t), "x"] + Fres(n), writes=["x"])

            def sample_attention(l, T, Vt):
                G4 = group
                kvl = kv_loc.ap()
                kvl_res = [("kvl", i) for i in range(9)]
                for pr in range(8):
                    dma("sp", kvl[:, pr * 256:(pr + 1) * 256], Rs(SL_K + pr)[:, :256],
                        reads=Rres(SL_K + pr), writes=[("kvl", pr)])
                dma("sp", kvl[:, 2048:4096], Rs(SL_V, 8)[:, 0:2048],
                    reads=[("Vt", tb, b) for tb in range(2) for b in range(4)], writes=[("kvl", 8)])
                if G4 > 1:
                    groups = [list(range(g0, g0 + G4)) for g0 in range(0, n_cores, G4)]
                    P.add("pool", lambda e: e.collective_compute("AllGather", ALU.bypass, groups,
                                                                 [kv_loc.ap().opt()], [kv_all.ap().opt()]),
                          reads=kvl_res, writes=["kv_all"], dma_sem=ksem("cc"), inc=1)
                    src = kv_all.ap()
                    src_res = ["kv_all"]
                else:
                    src = kv_loc.ap()
                    src_res = kvl_res
                kva = src.rearrange("(r p) c -> p r c", p=128)
                kall_res = [("Kall", r) for r in range(G4)]
                vall_res = [("Vall", r) for r in range(G4)]
                for r in range(G4):
                    dma("sp", Kall[:, :, r * 256:(r + 1) * 256], kva[:, r, 0:2048].rearrange("p (c t) -> p c t", c=8),
                        reads=src_res, writes=[("Kall", r)], key=("kall", r))
                    dma("sp", Vall[:, 2 * r:2 * r + 2, :], kva[:, r, 2048:4096].rearrange("p (t f) -> p t f", t=2),
                        reads=src_res, writes=[("Vall", r)], key=("vall", r))
                kva_res = kall_res + vall_res
                dma("pool", Kc[:], ckT[l].rearrange("(c p) k -> p c k", p=128), writes=["Kc"])
                dma("pool", Vc[:], cvv[l].rearrange("(kb p) f -> p kb f", p=128), writes=["Vc"])
                dma("sp", rpbx[:, :, :], rpb[l], writes=["rpbx"])
                state["nbank"] = 6
                state["bank"] = 0
                NKB = 2 * G4
                nblk = NKB + 4
                for hd in range(NH):
                    pr, hh = hd // 2, hd % 2
                    pb = hh * 64
                    Bvs = []
                    for hf in range(2):
                        EPv = EP[0][:, :].rearrange("p (x c) -> p x c", c=31)
                        P.add("dve", lambda e, hd=hd, hf=hf, EPv=EPv: e.tensor_tensor(
                            out=EPv, in0=Es[:, hf * 32:(hf + 1) * 32].unsqueeze(2).to_broadcast([16, 32, 31]),
                            in1=rpbx[:, hd, :].unsqueeze(1).to_broadcast([16, 32, 31]), op=ALU.mult),
                            reads=["Es", "rpbx"], writes=[("EP", 0)])
                        RSv = RSp[0][:, :].rearrange("p (x m) -> p x m", m=PK)
                        for q4 in range(2):
                            bank = nxt("bank", state["nbank"])
                            P.add("pe", lambda e, q4=q4, bank=bank: e.matmul(
                                ps[bank][0:64, 0:496], lhsT=ones[0:16, 0:64], rhs=EP[0][:, q4 * 496:(q4 + 1) * 496],
                                start=True, stop=True),
                                reads=[("EP", 0), "ones"], writes=[("ps", bank)])
                            P.add("act", lambda e, q4=q4, bank=bank, RSv=RSv: e.activation(
                                out=RSv[:, q4 * 16:(q4 + 1) * 16, 0:31],
                                in_=ps[bank][0:64, 0:496].rearrange("p (x c) -> p x c", c=31), func=AF.Identity),
                                reads=[("ps", bank)], writes=[("RSp", 0)])
                        d3h = d3.ap()[hf]
                        wr = bass.AP(d3h.tensor, d3h.offset + 64, [[LROW2 + 1, 64], [1, 32 * PK]])
                        dma("sp", wr, RSp[0][:, :], reads=[("RSp", 0)], writes=[("d3", hf)])
                        rd = bass.AP(d3h.tensor, d3h.offset + 64 + 15, [[LROW2, 64], [1, 32 * PK]])
                        dma("sp", Bpad[hf][pb:pb + 64, :], rd, reads=[("d3", hf)], writes=[("Bpad", hf)])
                        Bvs.append(Bpad[hf][pb:pb + 64, :].rearrange("p (x m) -> p x m", m=PK))
                    pi = nxt("pT", 2) % NPT
                    for kb2 in range(nblk // 2):
                        sbank = nxt("bank", state["nbank"])

                        def fs(e, kb2=kb2, sbank=sbank, pr=pr, pb=pb, Bvs=Bvs):
                            inst = None
                            for j in range(2):
                                kb = kb2 * 2 + j
                                o = ps[sbank][:, j * 256:(j + 1) * 256]
                                if kb < NKB:
                                    e.matmul(o, lhsT=Kall[pb:pb + 64, pr, kb * 128:(kb + 1) * 128],
                                             rhs=Rs(SL_Q + pr)[pb:pb + 64, 0:256], start=True, stop=False)
                                    e.matmul(o, lhsT=id2[pb:pb + 64, :], rhs=cmts[pb:pb + 64, :], start=False, stop=False)
                                    for ql in range(4):
                                        for k2 in range(2):
                                            x0 = (ql % 2) * 16 + 2 * kb + k2
                                            inst = e.matmul(ps[sbank][:, j * 256 + ql * 64:j * 256 + (ql + 1) * 64],
                                                            lhsT=sel2[pb:pb + 64, k2 * 128:(k2 + 1) * 128],
                                                            rhs=Bvs[ql // 2][:, x0, 0:64],
                                                            start=False, stop=(ql == 3 and k2 == 1))
                                else:
                                    kc_ = kb - NKB
                                    inst = e.matmul(o, lhsT=Kc[pb:pb + 64, pr, kc_ * 128:(kc_ + 1) * 128],
                                                    rhs=Rs(SL_Q + pr)[pb:pb + 64, 0:256], start=True, stop=True)
                            return inst
                        P.add("pe", fs, reads=kall_res + ["Kc", "id2", "sel2", "cmt", ("Bpad", 0), ("Bpad", 1)] + Rres(SL_Q + pr),
                              writes=[("ps", sbank)])
                        P.add("act", lambda e, kb2=kb2, sbank=sbank, pi=pi: e.activation(
                            out=pT[pi][:, kb2 * 512:(kb2 + 1) * 512], in_=ps[sbank][:, :], func=AF.Exp, scale=0.125),
                            reads=[("ps", sbank)], writes=[("pT", pi, kb2)])
                    obank, dbank = 6, 7

                    def fpv(e, pr=pr, hh=hh, pi=pi, obank=obank, dbank=dbank):
                        inst = None
                        for kb in range(nblk):
                            lhs = Vall[:, kb, pr * 128:(pr + 1) * 128] if kb < NKB else Vc[:, kb - NKB, pr * 128:(pr + 1) * 128]
                            e.matmul(ps[obank][:, hh * 256:(hh + 1) * 256], lhsT=lhs,
                                     rhs=pT[pi][:, kb * 256:(kb + 1) * 256], start=(kb == 0), stop=(kb == nblk - 1))
                            inst = e.matmul(ps[dbank][:, hh * 256:(hh + 1) * 256], lhsT=ones[:],
                                            rhs=pT[pi][:, kb * 256:(kb + 1) * 256], start=(kb == 0), stop=(kb == nblk - 1))
                        return inst
                    P.add("pe", fpv, reads=[("pT", pi, k) for k in range(nblk // 2)] + vall_res + ["Vc", "ones"],
                          writes=[("ps", obank), ("ps", dbank)])
                    if hh == 1:
                        t = nxt("tmp", 3)
                        P.add("dve", lambda e, dbank=dbank, t=t: e.reciprocal(out=tmpf[t][:], in_=ps[dbank][:]),
                              reads=[("ps", dbank)], writes=[("tmp", t)])
                        for h2 in range(2):
                            p2 = h2 * 64
                            P.add("dve", lambda e, pr=pr, h2=h2, p2=p2, obank=obank, t=t: e.tensor_tensor(
                                out=Rs(SL_YB + pr)[p2:p2 + 64, 0:256],
                                in0=ps[obank][p2:p2 + 64, h2 * 256:(h2 + 1) * 256],
                                in1=tmpf[t][p2:p2 + 64, h2 * 256:(h2 + 1) * 256], op=ALU.mult),
                                reads=[("ps", obank), ("tmp", t)], writes=[("YBh", pr, h2, 0)])


            for l in range(L):
                layer(l, pname, col0, T)
            i = dma("sp", yout[:, col0:col0 + T].rearrange("(c p) t -> p c t", p=128), x[:, :, :T],
                    reads=["x"], writes=[("yout", pname)])
            out_dmas.append(i)

        for (pname, col0, T) in passes:
            run_pass(pname, col0, T)

        P.emit(out_dmas)
    return nc


def _rope_tables(row0):
    half = 16
    freqs = (10000.0 ** (-np.arange(half, dtype=np.float32) / half)).astype(np.float32)
    t = np.arange(256)
    prow = (row0 + t // 64).astype(np.float32)
    pcol = (t % 64).astype(np.float32)
    C = np.zeros((128, 256), np.float32)
    S = np.zeros((128, 256), np.float32)
    for p in range(128):
        d = p % 64
        pos = prow if d < 32 else pcol
        ang = pos * freqs[d % 16]
        C[p] = np.cos(ang.astype(np.float32))
        S[p] = np.sin(ang.astype(np.float32))
    return C, S


def _perm_T():
    M = np.zeros((128, 128), np.float32)
    for m in range(128):
        if m % 32 < 16:
            M[m + 16, m] = -1.0
        else:
            M[m - 16, m] = 1.0
    return M


def _E_table(row0, rows=16, kh=8):
    E = np.zeros((16, 64), np.float32)
    for ql in range(4):
        qr = row0 + ql
        rs = int(np.clip(qr - kh // 2, 0, rows - kh))
        for kr in range(16):
            xi = ql * 16 + kr
            if rs <= kr < rs + kh:
                E[kr - qr + 7, xi] = 8.0
            else:
                E[15, xi] = 8.0
    return E


def _cmt():
    cols = np.arange(64)
    cstart = np.clip(cols - 8, 0, 48)
    m = np.full((64, 4, 64), NEG, np.float32)
    for qc in range(64):
        m[cstart[qc]:cstart[qc] + 16, :, qc] = 0.0
    return np.concatenate([m.reshape(64, 256)] * 2, 0)


def make_inputs(inp, core, n_layers=DEPTH, n_cores=8):
    f = lambda a: np.ascontiguousarray(np.asarray(a, dtype=np.float32))
    L = n_layers
    b = core // 4
    r = core % 4
    xs = np.asarray(inp["x_sample"])[b, r * 256:(r + 1) * 256]
    xp = np.asarray(inp["x_prompt"])[4 * core:4 * core + 4].reshape(1024, D)
    xin = f(np.concatenate([xs, xp], 0).T)
    cvT = f(np.stack([np.asarray(inp["c_ctx"]), np.asarray(inp["c"])[b]], 1))
    gvec = np.stack([np.asarray(inp[k])[:L].reshape(L, 16, 128) for k in
                     ("g_pre_mix", "g_post_mix", "g_pre_ffn", "g_post_ffn")], 1)
    C, S = _rope_tables(4 * r)
    m = {
        "xin": xin, "cvT": cvT,
        "w_mod": inp["w_mod"][:L], "b_modT": f(np.asarray(inp["b_mod"])[:L].reshape(L, 96, 128).transpose(0, 2, 1)),
        "gvec": f(gvec.transpose(0, 3, 1, 2)),
        "w_in": inp["w_in"][:L], "w_pa": inp["w_pa"][:L], "w_pb": inp["w_pb"][:L], "w_out": inp["w_out"][:L],
        "w_ff1": inp["w_ff1"][:L], "w_ff2": inp["w_ff2"][:L],
        "lng": f(np.asarray(inp["sgu_ln_g"])[:L]), "lnb": f(np.asarray(inp["sgu_ln_b"])[:L]),
        "sgu_wT": f(np.asarray(inp["sgu_w"])[:L].transpose(0, 3, 1, 2).reshape(L, 128, 1024)),
        "sgu_b": f(np.asarray(inp["sgu_b"])[:L].reshape(L, 1, 1024)),
        "ropeC": C, "ropeS": S, "permT": _perm_T(), "Emat": _E_table(4 * r), "cmt": _cmt(),
        "rpb": f(np.concatenate([np.asarray(inp["na_rpb"])[:L][..., ::-1].transpose(0, 2, 1, 3),
                                 np.full((L, 1, NH, 31), NEG / 8.0, np.float32)], 1)),
        "ckT": f(np.asarray(inp["cache_ctx_k"])[b, :L].transpose(0, 1, 3, 2).reshape(L, 1024, 512)),
        "cvv": f(np.asarray(inp["cache_ctx_v"])[b, :L].transpose(0, 2, 1, 3).reshape(L, 512, 1024)),
        "ident2": f(np.concatenate([np.concatenate([np.eye(64), np.eye(64)], 1)] * 2, 0)),
        "selmat": f(np.concatenate([np.concatenate([np.eye(64), np.zeros((64, 64)), np.zeros((64, 64)), np.eye(64)], 1)] * 2, 0)),
    }
    return m


_NC_CACHE = {}


def kernel(**inputs):
    inp = {k: np.asarray(v) for k, v in inputs.items()}
    if "full" not in _NC_CACHE:
        _NC_CACHE["full"] = build()
    nc = _NC_CACHE["full"]
    in_maps = [make_inputs(inp, c) for c in range(8)]
    res = run_bass_kernel_spmd(nc, in_maps, core_ids=list(range(8)))
    B, SEQ = 32, 256
    y_prompt = np.zeros((B, SEQ, D), np.float32)
    y_sample = np.zeros((2, 1024, D), np.float32)
    sk = np.zeros((B, DEPTH, NH, SEQ, 64), np.float32)
    sv = np.zeros((B, DEPTH, NH, SEQ, 64), np.float32)
    for c in range(8):
        r = res.results[c]
        yo = np.asarray(r["yout"]).T
        y_sample[c // 4, (c % 4) * 256:(c % 4 + 1) * 256] = yo[:256]
        y_prompt[4 * c:4 * c + 4] = yo[256:].reshape(4, SEQ, D)
        k = np.asarray(r["kst"])
        sk[4 * c:4 * c + 4] = k.reshape(DEPTH, NH, 64, 4, SEQ).transpose(3, 0, 1, 4, 2)
        v = np.asarray(r["vst"])
        sv[4 * c:4 * c + 4] = v.reshape(DEPTH, 4, SEQ, NH, 64).transpose(1, 0, 3, 2, 4)
    return (y_prompt, y_sample, sk, sv)
```

```python
import numpy as np
from contextlib import ExitStack
import concourse.bass as bass
import concourse.mybir as mybir
from concourse.bass_utils import run_bass_kernel_spmd

F32 = mybir.dt.float32
BF16 = mybir.dt.bfloat16
AF = mybir.ActivationFunctionType
ALU = mybir.AluOpType

D = 2048
NKC = 16
DEPTH = 4
NH = 16
AW = 1024
DFF = 8192
INC = 9216
EPS = 1e-6
NEG = -30000.0
PK = 94
LROW = 64 + 64 * PK + 64
LROW2 = 64 + 32 * PK + 64
import os
DBG = bool(os.environ.get("KDBG"))
NSLOT = 80
NWB = 3
WBE = 4096


class Op:
    __slots__ = ("eng", "fn", "deps", "signal", "token", "dma_sem", "inc")

    def __init__(self, eng, fn, dma_sem):
        self.eng = eng
        self.fn = fn
        self.deps = set()
        self.signal = False
        self.token = None
        self.dma_sem = dma_sem
        self.inc = 16


class Prog:
    ENGS = ("pe", "act", "dve", "pool", "sp")

    def __init__(self, nc, stack):
        self.nc = nc
        self.stack = stack
        self.ops = []
        self.lastw = {}
        self.readers = {}
        self.dma_cnt = {}
        self.last_barrier = -1
        self.nsem = 0
        self.esem = {e: self.new_sem("e_" + e) for e in self.ENGS}

    def new_sem(self, name):
        self.nsem += 1
        return self.stack.enter_context(self.nc.semaphore(name))

    def add(self, eng, fn, reads=(), writes=(), dma_sem=None, inc=16):
        i = len(self.ops)
        op = Op(eng, fn, dma_sem)
        op.inc = inc
        ops = self.ops
        for r in reads:
            w = self.lastw.get(r)
            if w is not None:
                wo = ops[w]
                if not (wo.eng == eng and eng == "pe" and wo.dma_sem is None):
                    op.deps.add(w)
            if isinstance(r, tuple) and r[0] == "ps":
                for rd in self.readers.get(r, ()):
                    if ops[rd].eng != eng:
                        op.deps.add(rd)
        for r in writes:
            w = self.lastw.get(r)
            if w is not None:
                wo = ops[w]
                if wo.dma_sem is not None or wo.eng != eng or dma_sem is not None:
                    op.deps.add(w)
            for rd in self.readers.get(r, ()):
                ro = ops[rd]
                if ro.dma_sem is not None or ro.eng != eng or dma_sem is not None:
                    op.deps.add(rd)
        for r in reads:
            self.readers.setdefault(r, []).append(i)
        for r in writes:
            self.lastw[r] = i
            self.readers[r] = []
        for d in op.deps:
            ops[d].signal = True
        if dma_sem is not None:
            c = self.dma_cnt.get(dma_sem, 0) + 1
            self.dma_cnt[dma_sem] = c
            op.token = (dma_sem, inc * c)
            op.signal = True
        ops.append(op)
        return i

    def barrier(self):
        last = {}
        dmas = []
        for i, op in enumerate(self.ops):
            if op.dma_sem is not None:
                if i > self.last_barrier:
                    dmas.append(i)
            elif op.fn is not None:
                last[op.eng] = i
        deps = set(last.values()) | set(dmas)
        for e in self.ENGS:
            op = Op(e, None, None)
            op.deps = set(d for d in deps if not (self.ops[d].eng == e and self.ops[d].dma_sem is None))
            for d in op.deps:
                self.ops[d].signal = True
            self.ops.append(op)
        self.last_barrier = len(self.ops)
        self.lastw = {k: v for k, v in self.lastw.items() if isinstance(k, tuple) and k[0] in ("W", "ps")
                      or k in ("ones", "scb", "cf", "bmod", "gv") or (isinstance(k, tuple) and k[0] in ("mod", "vecs"))}
        self.readers = {k: v for k, v in self.readers.items() if k in self.lastw}

    def emit(self, final_waits):
        nc = self.nc
        import os as _os
        _n = int(_os.environ.get("KSTOP", "0"))
        if _n:
            self.ops = self.ops[:_n]
            final_waits = [i for i in final_waits if i < _n]
            print("KSTOP", _n, "nsem", self.nsem)
        cnt = {e: 0 for e in self.ENGS}
        for op in self.ops:
            if op.dma_sem is None and op.signal:
                cnt[op.eng] += 1
                op.token = (self.esem[op.eng], cnt[op.eng])
        per = {e: [] for e in self.ENGS}
        for op in self.ops:
            per[op.eng].append(op)
        ops = self.ops

        def run(engname, eng):
            waited = {}
            for op in per[engname]:
                need = {}
                for d in op.deps:
                    sem, val = ops[d].token
                    if need.get(sem, (0, 0))[1] < val:
                        need[sem] = (d, val)
                for sem, (d, val) in sorted(need.items(), key=lambda kv: kv[1][0]):
                    if waited.get(sem, 0) < val:
                        eng.wait_ge(sem, val)
                        waited[sem] = val
                inst = op.fn(eng) if op.fn is not None else None
                if inst is not None and op.signal:
                    if op.dma_sem is not None and op.inc == 1:
                        inst.then_inc(op.dma_sem)
                    elif op.dma_sem is not None:
                        inst.then_inc(op.dma_sem, 16)
                    else:
                        inst.then_inc(self.esem[engname], 1)
            if engname == "sp":
                for i in final_waits:
                    sem, val = ops[i].token
                    eng.wait_ge(sem, val)

        with nc.Block() as block:
            @block.tensor
            def _(e):
                run("pe", e)

            @block.scalar
            def _(e):
                run("act", e)

            @block.vector
            def _(e):
                run("dve", e)

            @block.gpsimd
            def _(e):
                run("pool", e)

            @block.sync
            def _(e):
                run("sp", e)


def build(n_layers=DEPTH, passes=(("A", 256, 512), ("S", 0, 256), ("B", 768, 512)),
          n_tok=1280, group=4, n_cores=8):
    nc = bass.Bass("TRN2", target_bir_lowering=False)
    L = n_layers

    def din(name, shape, dt=F32):
        return nc.dram_tensor(name, list(shape), dt, kind="ExternalInput")

    xin = din("xin", [D, n_tok]).ap()
    cvT = din("cvT", [D, 2]).ap()
    w_mod = din("w_mod", [L, D, 6 * D]).ap()
    b_modT = din("b_modT", [L, 128, 96]).ap()
    gvec = din("gvec", [L, 128, 4, 16]).ap()
    w_in = din("w_in", [L, D, INC]).ap()
    w_pa = din("w_pa", [L, AW, D]).ap()
    w_pb = din("w_pb", [L, AW, D]).ap()
    w_out = din("w_out", [L, D, D]).ap()
    w_ff1 = din("w_ff1", [L, D, DFF]).ap()
    w_ff2 = din("w_ff2", [L, DFF, D]).ap()
    lng = din("lng", [L, AW]).ap()
    lnb = din("lnb", [L, AW]).ap()
    sgu_wT = din("sgu_wT", [L, 128, 8 * 128]).ap()
    sgu_b = din("sgu_b", [L, 1, AW]).ap()
    ropeC = din("ropeC", [128, 256]).ap()
    ropeS = din("ropeS", [128, 256]).ap()
    permT = din("permT", [128, 128]).ap()
    Emat = din("Emat", [16, 64]).ap()
    cmt = din("cmt", [128, 256]).ap()
    rpb = din("rpb", [L, 16, NH, 31]).ap()
    ckT = din("ckT", [L, 1024, 512]).ap()
    cvv = din("cvv", [L, 512, 1024]).ap()
    ident2 = din("ident2", [128, 128]).ap()
    selmat = din("selmat", [128, 256]).ap()

    yout = nc.dram_tensor("yout", [D, n_tok], F32, kind="ExternalOutput").ap()
    kst = nc.dram_tensor("kst", [L, 1024, 1024], F32, kind="ExternalOutput").ap()
    vst = nc.dram_tensor("vst", [L, 1024, 1024], F32, kind="ExternalOutput").ap()

    has_S = any(p[0] == "S" for p in passes)
    kv_loc = nc.dram_tensor("kv_loc", [128, 4096], BF16)
    kv_all = nc.dram_tensor("kv_all", [group * 128, 4096], BF16)
    d3 = nc.dram_tensor("d3", [2, 68 * 32 * PK], BF16)

    with ExitStack() as st:
        P = Prog(nc, st)

        def sb(name, shape, dt):
            return st.enter_context(nc.sbuf_tensor(name, list(shape), dt))

        def dsem(name):
            return P.new_sem(name)

        ARENA = 88064
        arena = sb("arena", [128, ARENA], BF16)
        Wb = sb("W", [128, NWB * WBE], BF16)
        ones = sb("ones", [128, 128], BF16)
        mod = sb("mod", [128, L, 96, 2], F32)
        vecs = sb("vecs", [128, L, 6, 16, 2], F32)
        bmod = sb("bmod", [128, L, 96], F32)
        gv = sb("gv", [128, L, 4, 16], F32)
        scb = sb("scb", [128, NKC, 2], BF16)
        cf = sb("cf", [128, NKC, 2], F32)
        lnst = sb("lnst", [128, 4, 16], F32)
        ps = [st.enter_context(nc.psum_tensor("ps%d" % i, [128, 512], F32)) for i in range(8)]

        sems = {}

        def ksem(k):
            if k not in sems:
                sems[k] = P.new_sem("d%d" % len(sems))
            return sems[k]

        def dma(eng, out, in_, reads=(), writes=(), key=None):
            k = key if key is not None else writes[0]
            return P.add(eng, lambda e: e.dma_start(out=out, in_=in_), reads, writes, dma_sem=ksem(k))

        state = {"mod_emitted": 0, "nbank": 8, "bank": 0, "wb": 0, "sq": 0, "tmp": 0, "kst": 0, "vst": 0, "relu": 0, "pT": 0}
        out_dmas = []

        def nxt(key, n):
            v = state[key]
            state[key] = (v + 1) % n
            return v

        P.add("pool", lambda e: e.memset(ones[:], 1.0), writes=["ones"])
        dma("sp", cf[:], cvT.rearrange("(c p) j -> p c j", p=128), writes=["cf"])
        dma("sp", bmod[:], b_modT.rearrange("l p n -> p l n"), writes=["bmod"])
        dma("sp", gv[:], gvec.rearrange("l p a c -> p l a c"), writes=["gv"])
        P.add("act", lambda e: e.activation(out=scb[:], in_=cf[:], func=AF.Silu), reads=["cf"], writes=["scb"])

        def load_w(src_ap, nk, nc_):
            b = nxt("wb", NWB)
            dst = Wb[:, b * WBE:b * WBE + nk * nc_].rearrange("p (k n) -> p k n", k=nk)
            dma("pool", dst, src_ap, writes=[("W", b)])
            return dst, ("W", b)

        def wview(w2d, r0, nk, c0, nc_):
            return w2d[r0:r0 + nk * 128, c0:c0 + nc_].rearrange("(k p) n -> p k n", p=128)

        def emit_mod(l):
            bank = nxt("bank", state["nbank"])
            pm = ps[bank][:, 0:192].rearrange("p (n j) -> p n j", j=2)
            for blk in range(48):
                wb_, wres = load_w(wview(w_mod[l], 0, NKC, blk * 256, 256), NKC, 256)

                def f(e, wb_=wb_, blk=blk, pm=pm):
                    inst = None
                    for nn in range(2):
                        for kc in range(NKC):
                            inst = e.matmul(pm[:, blk * 2 + nn, :], lhsT=wb_[:, kc, nn * 128:(nn + 1) * 128],
                                            rhs=scb[:, kc, :], start=(kc == 0), stop=(kc == NKC - 1))
                    return inst
                P.add("pe", f, reads=[wres, "scb"], writes=[("ps", bank)])
            P.add("dve", lambda e, l=l, pm=pm: e.tensor_tensor(
                out=mod[:, l], in0=pm, in1=bmod[:, l].unsqueeze(2).to_broadcast([128, 96, 2]), op=ALU.add),
                reads=[("ps", bank), "bmod"], writes=[("mod", l)])
            def gb(l, a):
                return gv[:, l, a].unsqueeze(2).to_broadcast([128, 16, 2])
            P.add("dve", lambda e, l=l: e.scalar_tensor_tensor(
                out=vecs[:, l, 0], in0=mod[:, l, 16:32, :], scalar=1.0, in1=gb(l, 0), op0=ALU.add, op1=ALU.mult),
                reads=[("mod", l), "gv"], writes=[("vecs", l, 0)])
            P.add("dve", lambda e, l=l: e.tensor_copy(out=vecs[:, l, 1], in_=mod[:, l, 0:16, :]),
                  reads=[("mod", l)], writes=[("vecs", l, 1)])
            P.add("dve", lambda e, l=l: e.tensor_tensor(out=vecs[:, l, 2], in0=mod[:, l, 32:48, :], in1=gb(l, 1),
                                                        op=ALU.mult),
                  reads=[("mod", l), "gv"], writes=[("vecs", l, 2)])
            P.add("dve", lambda e, l=l: e.scalar_tensor_tensor(
                out=vecs[:, l, 3], in0=mod[:, l, 64:80, :], scalar=1.0, in1=gb(l, 2), op0=ALU.add, op1=ALU.mult),
                reads=[("mod", l), "gv"], writes=[("vecs", l, 3)])
            P.add("dve", lambda e, l=l: e.tensor_copy(out=vecs[:, l, 4], in_=mod[:, l, 48:64, :]),
                  reads=[("mod", l)], writes=[("vecs", l, 4)])
            P.add("dve", lambda e, l=l: e.tensor_tensor(out=vecs[:, l, 5], in0=mod[:, l, 80:96, :], in1=gb(l, 3),
                                                        op=ALU.mult),
                  reads=[("mod", l), "gv"], writes=[("vecs", l, 5)])

        emit_mod(0)
        state["mod_emitted"] = 1

        def vres(l):
            return [("vecs", l, a) for a in range(6)]

        def run_pass(pname, col0, T):
            isS = pname == "S"
            SW = T
            cur = [0]

            def carve(nel, dt=BF16, parts=128):
                nb = nel * (2 if dt == F32 else 1)
                nb = (nb + 15) // 16 * 16
                v = arena[0:parts, cur[0]:cur[0] + nb]
                cur[0] += nb
                assert cur[0] <= ARENA, (pname, cur[0])
                return v.bitcast(F32) if dt == F32 else v
            x = carve(NKC * SW, F32).rearrange("p (c t) -> p c t", c=NKC)
            hb = carve(NKC * SW)
            R = carve(64 * SW)
            F1r = carve(8 * SW, F32)
            lngb = carve(AW, F32)
            lnbb = carve(AW, F32)
            wsT = carve(8 * 128)
            bsrow = carve(AW, BF16, 1)
            sq = [carve(SW) for i in range(2)]
            rstd = carve(SW, F32)
            tmpf = [carve(512, F32) for i in range(3)]
            relu_t = [carve(SW) for i in range(2)]
            if not isS:
                pT = [carve(512) for i in range(2)]
                kstg = [carve(512, F32) for i in range(2)]
                vstg = [carve(256, F32) for i in range(2)]
            else:
                pT = [carve(12 * 256) for i in range(1)]
                ropeCs = carve(256, F32)
                ropeSs = carve(256, F32)
                permTs = carve(128)
                Es = carve(64, F32, 16)
                cmts = carve(256, BF16, 128)
                id2 = carve(128, BF16, 128)
                sel2 = carve(256, BF16, 128)
                rpbx = carve(NH * 31, F32, 16).rearrange("p (h c) -> p h c", c=31)
                EP = [carve(32 * 31, BF16, 16)]
                RSp = [carve(32 * PK, BF16, 64)]
                Bpad = [carve(32 * PK, BF16, 128) for i in range(2)]
                Kall = carve(8 * group * 256).rearrange("p (c k) -> p c k", c=8)
                Vall = carve(group * 2 * 1024).rearrange("p (t f) -> p t f", f=1024)
                Kc = carve(8 * 512).rearrange("p (c k) -> p c k", c=8)
                Vc = carve(4 * 1024).rearrange("p (t f) -> p t f", f=1024)
                qb16 = [carve(256) for i in range(2)]
            NPT = len(pT)
            DBG and print("ARENA", pname, cur[0], ARENA)

            def Rs(slot, n=1):
                return R[:, slot * SW:(slot + n) * SW]

            def Rres(slot, n=1):
                return [("R", s_) for s_ in range(slot, slot + n)]

            def hview(T):
                return hb[:, :].rearrange("p (c t) -> p c t", c=NKC)[:, :, :T]

            P.barrier()
            dma("sp", x[:, :, :T], xin[:, col0:col0 + T].rearrange("(c p) t -> p c t", p=128), writes=["x"])
            if isS:
                dma("sp", ropeCs[:], ropeC, writes=["ropeC"])
                dma("sp", ropeSs[:], ropeS, writes=["ropeS"])
                dma("sp", Es[:], Emat, writes=["Es"])
                dma("pool", permTs[:], permT, writes=["permT"])
                dma("pool", cmts[:], cmt, writes=["cmt"])
                dma("pool", id2[:], ident2, writes=["id2"])
                dma("pool", sel2[:], selmat, writes=["sel2"])
                P.add("pool", lambda e: e.memset(RSp[0][:], 0.0), writes=[("RSp", 0)])
                for i in range(2):
                    P.add("pool", lambda e, i=i: e.memset(Bpad[i][:], 0.0), writes=[("Bpad", i)])
                d3a = d3.ap()
                for i in range(2):
                    dma("sp", d3a[i, :].rearrange("(p m) -> p m", p=68)[0:64, :], Bpad[0][0:64, :],
                        reads=[("Bpad", 0)], writes=[("d3", i)], key=("d3z", i))
                    dma("sp", d3a[i, :].rearrange("(p m) -> p m", p=68)[64:68, :], Bpad[0][0:4, :],
                        reads=[("Bpad", 0)], writes=[("d3", i)], key=("d3z", i))

            def rms_stats(src_fn, src_res, T):
                bank = nxt("bank", state["nbank"])
                for kc in range(NKC):
                    s = nxt("sq", 2)
                    P.add("act", lambda e, kc=kc, s=s: e.activation(out=sq[s][:, :T], in_=src_fn(kc), func=AF.Square),
                          reads=src_res(kc), writes=[("sq", s)])
                    P.add("pe", lambda e, kc=kc, s=s: e.matmul(ps[bank][:, :T], lhsT=ones[:], rhs=sq[s][:, :T],
                                                               start=(kc == 0), stop=(kc == NKC - 1)),
                          reads=[("sq", s), "ones"], writes=[("ps", bank)])
                P.add("act", lambda e: e.activation(out=rstd[:, :T], in_=ps[bank][:, :T], func=AF.Sqrt,
                                                    scale=1.0 / D, bias=EPS),
                      reads=[("ps", bank)], writes=["rstd"])
                P.add("dve", lambda e: e.reciprocal(out=rstd[:, :T], in_=rstd[:, :T]), reads=["rstd"], writes=["rstd"])

            def norm_mod(l, cv, T, ai, bi):
                rms_stats(lambda kc: x[:, kc, :T], lambda kc: ["x"], T)
                hv = hview(T)
                for kc in range(NKC):
                    t = nxt("tmp", 3)
                    P.add("dve", lambda e, kc=kc, t=t: e.scalar_tensor_tensor(
                        out=tmpf[t][:, :T], in0=x[:, kc, :T], scalar=vecs[:, l, ai, kc, cv:cv + 1], in1=rstd[:, :T],
                        op0=ALU.mult, op1=ALU.mult),
                        reads=["x", "rstd"] + vres(l), writes=[("tmp", t)])
                    P.add("act", lambda e, kc=kc, t=t: e.activation(
                        out=hv[:, kc, :], in_=tmpf[t][:, :T], func=AF.Identity, bias=vecs[:, l, bi, kc, cv:cv + 1],
                        scale=1.0),
                        reads=[("tmp", t)] + vres(l), writes=["h"])

            def proj_fm(w2d, r0, nk, c0, ncols, rhs_fn, rhs_res, T, evac, nb0=0):
                bc = WBE // nk
                bc = min(bc, ncols)
                for b in range(ncols // bc):
                    wb_, wres = load_w(wview(w2d, r0, nk, c0 + b * bc, bc), nk, bc)
                    for nn in range(bc // 128):
                        bank = nxt("bank", state["nbank"])

                        def f(e, wb_=wb_, nn=nn, bank=bank):
                            inst = None
                            for kc in range(nk):
                                inst = e.matmul(ps[bank][:, :T], lhsT=wb_[:, kc, nn * 128:(nn + 1) * 128],
                                                rhs=rhs_fn(kc), start=(kc == 0), stop=(kc == nk - 1))
                            return inst
                        P.add("pe", f, reads=[wres] + rhs_res, writes=[("ps", bank)])
                        evac(nb0 + b * (bc // 128) + nn, bank)

            def proj_tm(w2d, c0, ncols, T, evac):
                hv = hview(T)
                for b in range(ncols // 256):
                    wb_, wres = load_w(wview(w2d, 0, NKC, c0 + b * 256, 256), NKC, 256)
                    for tb in range(T // 128):
                        bank = nxt("bank", state["nbank"])

                        def f(e, wb_=wb_, tb=tb, bank=bank):
                            inst = None
                            for kc in range(NKC):
                                inst = e.matmul(ps[bank][:, :256], lhsT=hv[:, kc, tb * 128:(tb + 1) * 128],
                                                rhs=wb_[:, kc, :], start=(kc == 0), stop=(kc == NKC - 1))
                            return inst
                        P.add("pe", f, reads=[wres, "h"], writes=[("ps", bank)])
                        evac(b, tb, bank)

            SL_U, SL_Q, SL_K, SL_V, SL_YB, SL_SGA, SL_VT, SL_SGB = 0, 8, 16, 24, 32, 40, 56, 8

            def layer(l, pname, col0, T):
                cv = 1 if pname == "S" else 0
                PO = "dve"
                NTB = T // 128
                hv = hview(T)
                dma("sp", lngb[:], lng[l:l + 1, :].partition_broadcast(128), writes=["lngb"])
                dma("sp", lnbb[:], lnb[l:l + 1, :].partition_broadcast(128), writes=["lnbb"])
                dma("pool", wsT[:], sgu_wT[l], writes=["wsT"])
                dma("pool", bsrow[:], sgu_b[l], writes=["bsrow"])

                norm_mod(l, cv, T, 0, 1)
                hres = ["h"]
                wl = w_in[l]

                DBG and print("MARK", len(P.ops), '# ---- au -> u = gelu ----')
                def ev_u(n, bank):
                    P.add("act", lambda e: e.activation(out=Rs(SL_U + n)[:, :T], in_=ps[bank][:, :T],
                                                        func=AF.Gelu_apprx_tanh),
                          reads=[("ps", bank)], writes=Rres(SL_U + n))
                proj_fm(wl, 0, NKC, 0, 1024, lambda kc: hv[:, kc, :], hres, T, ev_u)

                DBG and print("MARK", len(P.ops), '# ---- av -> token-major gelu -> VT (bf16) ----')
                VT = Rs(SL_VT, 8).rearrange("p (t f) -> p t f", f=1024)

                def ev_av(b, tb, bank):
                    P.add("act", lambda e: e.activation(out=VT[:, tb, b * 256:(b + 1) * 256], in_=ps[bank][:, :256],
                                                        func=AF.Gelu_apprx_tanh),
                          reads=[("ps", bank)], writes=[("VT", tb, b)])
                proj_tm(wl, 1024, 1024, T, ev_av)

                DBG and print("MARK", len(P.ops), '# ---- LayerNorm over features (token-major) ----')
                for tb in range(NTB):
                    allvt = [("VT", tb, b) for b in range(4)]
                    for c2 in range(2):
                        P.add("dve", lambda e, tb=tb, c2=c2: e.bn_stats(lnst[:, tb, c2 * 6:(c2 + 1) * 6],
                                                                         VT[:, tb, c2 * 512:(c2 + 1) * 512]),
                              reads=allvt, writes=[("lnst", tb, c2)])
                    P.add("dve", lambda e, tb=tb: e.bn_aggr(lnst[:, tb, 12:14], lnst[:, tb, 0:12]),
                          reads=[("lnst", tb, 0), ("lnst", tb, 1)], writes=[("lnmv", tb)])
                    P.add("act", lambda e, tb=tb: e.activation(out=lnst[:, tb, 14:15], in_=lnst[:, tb, 13:14],
                                                               func=AF.Sqrt, scale=1.0, bias=EPS),
                          reads=[("lnmv", tb)], writes=[("lnr", tb)])
                    P.add("dve", lambda e, tb=tb: e.reciprocal(out=lnst[:, tb, 15:16], in_=lnst[:, tb, 14:15]),
                          reads=[("lnr", tb)], writes=[("lnr2", tb)])
                    t = nxt("tmp", 3)
                    t2 = nxt("tmp", 3)
                    for c2 in range(2):
                        sl = slice(c2 * 512, (c2 + 1) * 512)
                        tt = t if c2 == 0 else t2
                        P.add("dve", lambda e, tb=tb, sl=sl, tt=tt: e.tensor_scalar(
                            out=tmpf[tt][:], in0=VT[:, tb, sl], scalar1=lnst[:, tb, 12:13], scalar2=lnst[:, tb, 15:16],
                            op0=ALU.subtract, op1=ALU.mult),
                            reads=allvt + [("lnmv", tb), ("lnr2", tb)], writes=[("tmp", tt)])
                        P.add("dve", lambda e, sl=sl, tt=tt: e.tensor_tensor(out=tmpf[tt][:], in0=tmpf[tt][:],
                                                                               in1=lngb[:, sl], op=ALU.mult),
                              reads=[("tmp", tt), "lngb"], writes=[("tmp", tt)])
                        P.add("dve", lambda e, tb=tb, sl=sl, tt=tt: e.tensor_tensor(out=VT[:, tb, sl], in0=tmpf[tt][:],
                                                                                      in1=lnbb[:, sl], op=ALU.add),
                              reads=[("tmp", tt), "lnbb"], writes=[("VTn", tb, c2)] + [("VT", tb, 2 * c2), ("VT", tb, 2 * c2 + 1)])

                DBG and print("MARK", len(P.ops), '# ---- q, k (feature-major) ----')
                if pname == "S":
                    def ev_rope(dst_slot):
                        def ev(n, bank):
                            qi = nxt("relu", 2)
                            P.add("act", lambda e: e.activation(out=qb16[qi][:, :T], in_=ps[bank][:, :T], func=AF.Identity),
                                  reads=[("ps", bank)], writes=[("qb16", qi)])
                            bank2 = nxt("bank", state["nbank"])
                            P.add("pe", lambda e: e.matmul(ps[bank2][:, :T], lhsT=permTs[:], rhs=qb16[qi][:, :T],
                                                           start=True, stop=True),
                                  reads=[("qb16", qi), "permT"], writes=[("ps", bank2)])
                            t = nxt("tmp", 3)
                            t2 = nxt("tmp", 3)
                            P.add("dve", lambda e: e.tensor_tensor(out=tmpf[t][:, :T], in0=ps[bank][:, :T],
                                                                   in1=ropeCs[:, :T], op=ALU.mult),
                                  reads=[("ps", bank), "ropeC"], writes=[("tmp", t)])
                            P.add("dve", lambda e: e.tensor_tensor(out=tmpf[t2][:, :T], in0=ps[bank2][:, :T],
                                                                   in1=ropeSs[:, :T], op=ALU.mult),
                                  reads=[("ps", bank2), "ropeS"], writes=[("tmp", t2)])
                            P.add("dve", lambda e: e.tensor_tensor(out=Rs(dst_slot + n)[:, :T], in0=tmpf[t][:, :T],
                                                                    in1=tmpf[t2][:, :T], op=ALU.add),
                                  reads=[("tmp", t), ("tmp", t2)], writes=Rres(dst_slot + n))
                        return ev
                    proj_fm(wl, 0, NKC, 2048, 1024, lambda kc: hv[:, kc, :], hres, T, ev_rope(SL_Q))
                    proj_fm(wl, 0, NKC, 3072, 1024, lambda kc: hv[:, kc, :], hres, T, ev_rope(SL_K))
                else:
                    def ev_q(n, bank):
                        P.add("act", lambda e: e.activation(out=Rs(SL_Q + n)[:, :T], in_=ps[bank][:, :T], func=AF.Identity),
                              reads=[("ps", bank)], writes=Rres(SL_Q + n))
                    proj_fm(wl, 0, NKC, 2048, 1024, lambda kc: hv[:, kc, :], hres, T, ev_q)

                    def ev_k(n, bank):
                        kb_ = nxt("kst", 2)
                        P.add("act", lambda e: e.activation(out=kstg[kb_][:, :T], in_=ps[bank][:, :T], func=AF.Identity),
                              reads=[("ps", bank)], writes=[("kstg", kb_)])
                        P.add("dve", lambda e: e.tensor_copy(out=Rs(SL_K + n)[:, :T], in_=ps[bank][:, :T]),
                              reads=[("ps", bank)], writes=Rres(SL_K + n))
                        i = dma("sp", kst[l, n * 128:(n + 1) * 128, col0 - 256:col0 - 256 + T], kstg[kb_][:, :T],
                                reads=[("kstg", kb_)], writes=[("kstg_out", kb_)])
                        out_dmas.append(i)
                    proj_fm(wl, 0, NKC, 3072, 1024, lambda kc: hv[:, kc, :], hres, T, ev_k)

                DBG and print("MARK", len(P.ops), '# ---- v (token-major) ----')
                Vt = Rs(SL_V, 8).rearrange("p (t f) -> p t f", f=1024)

                def ev_v(b, tb, bank):
                    if pname == "S":
                        P.add("act", lambda e: e.activation(out=Vt[:, tb, b * 256:(b + 1) * 256], in_=ps[bank][:, :256],
                                                            func=AF.Identity),
                              reads=[("ps", bank)], writes=[("Vt", tb, b)])
                        return
                    vb_ = nxt("vst", 2)
                    P.add("act", lambda e: e.activation(out=vstg[vb_][:], in_=ps[bank][:, :256], func=AF.Identity),
                          reads=[("ps", bank)], writes=[("vstg", vb_)])
                    P.add("dve", lambda e: e.tensor_copy(out=Vt[:, tb, b * 256:(b + 1) * 256], in_=ps[bank][:, :256]),
                          reads=[("ps", bank)], writes=[("Vt", tb, b)])
                    r0 = col0 - 256 + tb * 128
                    i = dma("sp", vst[l, r0:r0 + 128, b * 256:(b + 1) * 256], vstg[vb_][:],
                            reads=[("vstg", vb_)], writes=[("vstg_out", vb_)])
                    out_dmas.append(i)
                proj_tm(wl, 4096, 1024, T, ev_v)

                DBG and print("MARK", len(P.ops), '# ---- gate A (sigmoid) ----')
                def ev_sga(n, bank):
                    P.add("act", lambda e: e.activation(out=Rs(SL_SGA + n)[:, :T], in_=ps[bank][:, :T], func=AF.Sigmoid),
                          reads=[("ps", bank)], writes=Rres(SL_SGA + n))
                proj_fm(wl, 0, NKC, 5120, 2048, lambda kc: hv[:, kc, :], hres, T, ev_sga)

                if state["mod_emitted"] == l + 1 and l + 1 < L:
                    emit_mod(l + 1)
                    state["mod_emitted"] = l + 2
                DBG and print("MARK", len(P.ops), '# ---- SGU spatial mixing')
                for g in range(8):
                    bank = nxt("bank", state["nbank"])

                    def f(e, g=g, bank=bank):
                        inst = None
                        for tb in range(NTB):
                            e.matmul(ps[bank][:, tb * 128:(tb + 1) * 128], lhsT=VT[:, tb, g * 128:(g + 1) * 128],
                                     rhs=wsT[:, g * 128:(g + 1) * 128], start=True, stop=False)
                            inst = e.matmul(ps[bank][:, tb * 128:(tb + 1) * 128], lhsT=ones[0:1, :],
                                            rhs=bsrow[0:1, g * 128:(g + 1) * 128], start=False, stop=True)
                        return inst
                    P.add("pe", f, reads=[("VTn", tb, c2) for tb in range(NTB) for c2 in range(2)] + ["wsT", "bsrow", "ones"],
                          writes=[("ps", bank)])
                    P.add("dve", lambda e, g=g, bank=bank: e.tensor_tensor(out=Rs(SL_U + g)[:, :T], in0=ps[bank][:, :T],
                                                                          in1=Rs(SL_U + g)[:, :T], op=ALU.mult),
                          reads=[("ps", bank)] + Rres(SL_U + g), writes=Rres(SL_U + g))

                DBG and print("MARK", len(P.ops), '# ---- attention ----')
                if pname != "S":
                    for s in range(T // 256):
                        for pr in range(8):
                            obank = nxt("bank", state["nbank"])
                            dbank = nxt("bank", state["nbank"])
                            for hh in range(2):
                                pb = hh * 64
                                sbank = nxt("bank", state["nbank"])
                                pi = nxt("pT", 2) % NPT

                                def fqk(e, s=s, pr=pr, pb=pb, sbank=sbank):
                                    inst = None
                                    for kb in range(2):
                                        inst = e.matmul(
                                            ps[sbank][:, kb * 256:(kb + 1) * 256],
                                            lhsT=Rs(SL_K + pr)[pb:pb + 64, s * 256 + kb * 128:s * 256 + (kb + 1) * 128],
                                            rhs=Rs(SL_Q + pr)[pb:pb + 64, s * 256:(s + 1) * 256], start=True, stop=True)
                                    return inst
                                P.add("pe", fqk, reads=Rres(SL_K + pr) + Rres(SL_Q + pr), writes=[("ps", sbank)])
                                P.add("act", lambda e, sbank=sbank, pi=pi: e.activation(
                                    out=pT[pi][:, 0:512], in_=ps[sbank][:, :], func=AF.Exp, scale=0.125),
                                    reads=[("ps", sbank)], writes=[("pT", pi, 0)])

                                def fpv(e, s=s, pr=pr, hh=hh, pi=pi, obank=obank, dbank=dbank):
                                    inst = None
                                    for kb in range(2):
                                        tbk = s * 2 + kb
                                        e.matmul(ps[obank][:, hh * 256:(hh + 1) * 256],
                                                 lhsT=Vt[:, tbk, pr * 128:(pr + 1) * 128],
                                                 rhs=pT[pi][:, kb * 256:(kb + 1) * 256], start=(kb == 0), stop=(kb == 1))
                                        inst = e.matmul(ps[dbank][:, hh * 256:(hh + 1) * 256], lhsT=ones[:],
                                                        rhs=pT[pi][:, kb * 256:(kb + 1) * 256], start=(kb == 0),
                                                        stop=(kb == 1))
                                    return inst
                                P.add("pe", fpv,
                                      reads=[("pT", pi, 0), "ones"] + [("Vt", s * 2 + kb, b) for kb in range(2) for b in range(4)],
                                      writes=[("ps", obank), ("ps", dbank)])
                            t = nxt("tmp", 3)
                            P.add("dve", lambda e, dbank=dbank, t=t: e.reciprocal(out=tmpf[t][:], in_=ps[dbank][:]),
                                  reads=[("ps", dbank)], writes=[("tmp", t)])
                            for hh in range(2):
                                pb = hh * 64
                                P.add("dve", lambda e, s=s, pr=pr, hh=hh, pb=pb, obank=obank, t=t: e.tensor_tensor(
                                    out=Rs(SL_YB + pr)[pb:pb + 64, s * 256:(s + 1) * 256],
                                    in0=ps[obank][pb:pb + 64, hh * 256:(hh + 1) * 256],
                                    in1=tmpf[t][pb:pb + 64, hh * 256:(hh + 1) * 256], op=ALU.mult),
                                    reads=[("ps", obank), ("tmp", t)], writes=[("YBh", pr, hh, s)])
                    yb_res = lambda pr: [("YBh", pr, hh, s) for hh in range(2) for s in range(T // 256)]
                else:
                    sample_attention(l, T, Vt)
                    state["nbank"] = 8
                    yb_res = lambda pr: [("YBh", pr, hh, 0) for hh in range(2)]

                DBG and print("MARK", len(P.ops), '# ---- gate B (sigmoid) into Q/K slots ----')
                def ev_sgb(n, bank):
                    P.add("act", lambda e: e.activation(out=Rs(SL_SGB + n)[:, :T], in_=ps[bank][:, :T], func=AF.Sigmoid),
                          reads=[("ps", bank)], writes=Rres(SL_SGB + n) + ([("YBdummy",)] if False else []))
                proj_fm(wl, 0, NKC, 7168, 2048, lambda kc: hv[:, kc, :], hres, T, ev_sgb)

                DBG and print("MARK", len(P.ops), '# ---- pa / pb projections')
                for qt in range(4):
                    wa_, war_ = load_w(wview(w_pa[l], 0, 8, qt * 512, 512), 8, 512)
                    wb2_, wbr_ = load_w(wview(w_pb[l], 0, 8, qt * 512, 512), 8, 512)
                    for nn in range(4):
                        n = qt * 4 + nn
                        c = nn * 128
                        ba = nxt("bank", state["nbank"])
                        bb = nxt("bank", state["nbank"])

                        def fa(e, wa_=wa_, c=c, ba=ba):
                            inst = None
                            for kc in range(8):
                                inst = e.matmul(ps[ba][:, :T], lhsT=wa_[:, kc, c:c + 128], rhs=Rs(SL_U + kc)[:, :T],
                                                start=(kc == 0), stop=(kc == 7))
                            return inst
                        P.add("pe", fa, reads=[war_] + Rres(SL_U, 8), writes=[("ps", ba)])

                        def fb(e, wb2_=wb2_, c=c, bb=bb):
                            inst = None
                            for kc in range(8):
                                inst = e.matmul(ps[bb][:, :T], lhsT=wb2_[:, kc, c:c + 128], rhs=Rs(SL_YB + kc)[:, :T],
                                                start=(kc == 0), stop=(kc == 7))
                            return inst
                        P.add("pe", fb, reads=[wbr_] + [r for pr in range(8) for r in yb_res(pr)], writes=[("ps", bb)])
                        t = nxt("tmp", 3)
                        t2 = nxt("tmp", 3)
                        P.add("dve", lambda e, n=n, ba=ba, t=t: e.tensor_tensor(
                            out=tmpf[t][:, :T], in0=ps[ba][:, :T], in1=Rs(SL_SGA + n)[:, :T], op=ALU.mult),
                            reads=[("ps", ba)] + Rres(SL_SGA + n), writes=[("tmp", t)])
                        P.add("dve", lambda e, n=n, bb=bb, t2=t2: e.tensor_tensor(
                            out=tmpf[t2][:, :T], in0=ps[bb][:, :T], in1=Rs(SL_SGB + n)[:, :T], op=ALU.mult),
                            reads=[("ps", bb)] + Rres(SL_SGB + n), writes=[("tmp", t2)])
                        P.add(PO, lambda e, n=n, t=t, t2=t2: e.tensor_tensor(
                            out=Rs(SL_SGA + n)[:, :T], in0=tmpf[t][:, :T], in1=tmpf[t2][:, :T], op=ALU.add),
                            reads=[("tmp", t), ("tmp", t2)], writes=Rres(SL_SGA + n))

                DBG and print("MARK", len(P.ops), '# ---- w_out -> MO')
                MO = Rs(0, 32).bitcast(F32).rearrange("p (c t) -> p c t", c=NKC)
                mo_res = lambda n: Rres(2 * n, 2)

                def ev_mo(n, bank):
                    P.add("act", lambda e: e.activation(out=MO[:, n, :T], in_=ps[bank][:, :T], func=AF.Identity),
                          reads=[("ps", bank)], writes=mo_res(n))
                proj_fm(w_out[l], 0, NKC, 0, D, lambda kc: Rs(SL_SGA + kc)[:, :T], Rres(SL_SGA, 16), T, ev_mo)
                rms_stats(lambda kc: MO[:, kc, :T], mo_res, T)
                for n in range(NKC):
                    t = nxt("tmp", 3)
                    P.add("dve", lambda e, n=n, t=t: e.scalar_tensor_tensor(
                        out=tmpf[t][:, :T], in0=MO[:, n, :T], scalar=vecs[:, l, 2, n, cv:cv + 1], in1=rstd[:, :T],
                        op0=ALU.mult, op1=ALU.mult),
                        reads=mo_res(n) + ["rstd"] + vres(l), writes=[("tmp", t)])
                    P.add(PO, lambda e, n=n, t=t: e.tensor_tensor(out=x[:, n, :T], in0=x[:, n, :T], in1=tmpf[t][:, :T],
                                                                      op=ALU.add),
                          reads=[("tmp", t), "x"], writes=["x"])

                DBG and print("MARK", len(P.ops), '# ---- FFN ----')
                norm_mod(l, cv, T, 3, 4)

                def ev_ff1(n, bank):
                    r = nxt("relu", 2)
                    P.add("act", lambda e: e.activation(out=relu_t[r][:, :T], in_=ps[bank][:, :T], func=AF.Relu),
                          reads=[("ps", bank)], writes=[("relu", r)])
                    P.add(PO, lambda e: e.tensor_tensor(out=Rs(n)[:, :T], in0=relu_t[r][:, :T], in1=relu_t[r][:, :T],
                                                            op=ALU.mult),
                          reads=[("relu", r)], writes=Rres(n))
                proj_fm(w_ff1[l], 0, NKC, 0, DFF, lambda kc: hv[:, kc, :], hres, T, ev_ff1)

                F0 = hb[:, :].bitcast(F32).rearrange("p (c t) -> p c t", c=8)
                F1 = F1r[:, :].rearrange("p (c t) -> p c t", c=8)

                def Fv(n):
                    return F0[:, n, :T] if n < 8 else F1[:, n - 8, :T]

                def Fres(n):
                    return ["h"] if n < 8 else [("F1", n)]
                for c in range(8):
                    banks = [nxt("bank", state["nbank"]), nxt("bank", state["nbank"])]
                    for g in range(4):
                        wb_, wres = load_w(wview(w_ff2[l], g * 2048, NKC, c * 256, 256), NKC, 256)
                        for nn in range(2):
                            def f(e, wb_=wb_, nn=nn, g=g, bank=banks[nn]):
                                inst = None
                                for kc in range(NKC):
                                    inst = e.matmul(ps[bank][:, :T], lhsT=wb_[:, kc, nn * 128:(nn + 1) * 128],
                                                    rhs=Rs(g * 16 + kc)[:, :T], start=(g == 0 and kc == 0),
                                                    stop=(g == 3 and kc == NKC - 1))
                                return inst
                            P.add("pe", f, reads=[wres] + Rres(g * 16, 16), writes=[("ps", banks[nn])])
                    for nn in range(2):
                        n = c * 2 + nn
                        P.add("act", lambda e, n=n, bank=banks[nn]: e.activation(out=Fv(n), in_=ps[bank][:, :T],
                                                                                 func=AF.Identity),
                              reads=[("ps", banks[nn])], writes=Fres(n) + [("F", n)])
                rms_stats(lambda kc: Fv(kc), lambda kc: [("F", kc)], T)
                for n in range(NKC):
                    t = nxt("tmp", 3)
                    P.add("dve", lambda e, n=n, t=t: e.scalar_tensor_tensor(
                        out=tmpf[t][:, :T], in0=Fv(n), scalar=vecs[:, l, 5, n, cv:cv + 1], in1=rstd[:, :T],
                        op0=ALU.mult, op1=ALU.mult),
                        reads=[("F", n), "rstd"] + vres(l), writes=[("tmp", t)])
                    P.add(PO, lambda e, n=n, t=t: e.tensor_tensor(out=x[:, n, :T], in0=x[:, n, :T], in1=tmpf[t][:, :T],
                                                                      op=ALU.add),
                          reads=[("tmp", t), "x"] + Fres(n), writes=["x"])

            def sample_attention(l, T, Vt):
                G4 = group
                kvl = kv_loc.ap()
                kvl_res = [("kvl", i) for i in range(9)]
                for pr in range(8):
                    dma("sp", kvl[:, pr * 256:(pr + 1) * 256], Rs(SL_K + pr)[:, :256],
                        reads=Rres(SL_K + pr), writes=[("kvl", pr)])
                dma("sp", kvl[:, 2048:4096], Rs(SL_V, 8)[:, 0:2048],
                    reads=[("Vt", tb, b) for tb in range(2) for b in range(4)], writes=[("kvl", 8)])
                if G4 > 1:
                    groups = [list(range(g0, g0 + G4)) for g0 in range(0, n_cores, G4)]
                    P.add("pool", lambda e: e.collective_compute("AllGather", ALU.bypass, groups,
                                                                 [kv_loc.ap().opt()], [kv_all.ap().opt()]),
                          reads=kvl_res, writes=["kv_all"], dma_sem=ksem("cc"), inc=1)
                    src = kv_all.ap()
                    src_res = ["kv_all"]
                else:
                    src = kv_loc.ap()
                    src_res = kvl_res
                kva = src.rearrange("(r p) c -> p r c", p=128)
                kall_res = [("Kall", r) for r in range(G4)]
                vall_res = [("Vall", r) for r in range(G4)]
                for r in range(G4):
                    dma("sp", Kall[:, :, r * 256:(r + 1) * 256], kva[:, r, 0:2048].rearrange("p (c t) -> p c t", c=8),
                        reads=src_res, writes=[("Kall", r)], key=("kall", r))
                    dma("sp", Vall[:, 2 * r:2 * r + 2, :], kva[:, r, 2048:4096].rearrange("p (t f) -> p t f", t=2),
                        reads=src_res, writes=[("Vall", r)], key=("vall", r))
                kva_res = kall_res + vall_res
                dma("pool", Kc[:], ckT[l].rearrange("(c p) k -> p c k", p=128), writes=["Kc"])
                dma("pool", Vc[:], cvv[l].rearrange("(kb p) f -> p kb f", p=128), writes=["Vc"])
                dma("sp", rpbx[:, :, :], rpb[l], writes=["rpbx"])
                state["nbank"] = 6
                state["bank"] = 0
                NKB = 2 * G4
                nblk = NKB + 4
                for hd in range(NH):
                    pr, hh = hd // 2, hd % 2
                    pb = hh * 64
                    Bvs = []
                    for hf in range(2):
                        EPv = EP[0][:, :].rearrange("p (x c) -> p x c", c=31)
                        P.add("dve", lambda e, hd=hd, hf=hf, EPv=EPv: e.tensor_tensor(
                            out=EPv, in0=Es[:, hf * 32:(hf + 1) * 32].unsqueeze(2).to_broadcast([16, 32, 31]),
                            in1=rpbx[:, hd, :].unsqueeze(1).to_broadcast([16, 32, 31]), op=ALU.mult),
                            reads=["Es", "rpbx"], writes=[("EP", 0)])
                        RSv = RSp[0][:, :].rearrange("p (x m) -> p x m", m=PK)
                        for q4 in range(2):
                            bank = nxt("bank", state["nbank"])
                            P.add("pe", lambda e, q4=q4, bank=bank: e.matmul(
                                ps[bank][0:64, 0:496], lhsT=ones[0:16, 0:64], rhs=EP[0][:, q4 * 496:(q4 + 1) * 496],
                                start=True, stop=True),
                                reads=[("EP", 0), "ones"], writes=[("ps", bank)])
                            P.add("act", lambda e, q4=q4, bank=bank, RSv=RSv: e.activation(
                                out=RSv[:, q4 * 16:(q4 + 1) * 16, 0:31],
                                in_=ps[bank][0:64, 0:496].rearrange("p (x c) -> p x c", c=31), func=AF.Identity),
                                reads=[("ps", bank)], writes=[("RSp", 0)])
                        d3h = d3.ap()[hf]
                        wr = bass.AP(d3h.tensor, d3h.offset + 64, [[LROW2 + 1, 64], [1, 32 * PK]])
                        dma("sp", wr, RSp[0][:, :], reads=[("RSp", 0)], writes=[("d3", hf)])
                        rd = bass.AP(d3h.tensor, d3h.offset + 64 + 15, [[LROW2, 64], [1, 32 * PK]])
                        dma("sp", Bpad[hf][pb:pb + 64, :], rd, reads=[("d3", hf)], writes=[("Bpad", hf)])
                        Bvs.append(Bpad[hf][pb:pb + 64, :].rearrange("p (x m) -> p x m", m=PK))
                    pi = nxt("pT", 2) % NPT
                    for kb2 in range(nblk // 2):
                        sbank = nxt("bank", state["nbank"])

                        def fs(e, kb2=kb2, sbank=sbank, pr=pr, pb=pb, Bvs=Bvs):
                            inst = None
                            for j in range(2):
                                kb = kb2 * 2 + j
                                o = ps[sbank][:, j * 256:(j + 1) * 256]
                                if kb < NKB:
                                    e.matmul(o, lhsT=Kall[pb:pb + 64, pr, kb * 128:(kb + 1) * 128],
                                             rhs=Rs(SL_Q + pr)[pb:pb + 64, 0:256], start=True, stop=False)
                                    e.matmul(o, lhsT=id2[pb:pb + 64, :], rhs=cmts[pb:pb + 64, :], start=False, stop=False)
                                    for ql in range(4):
                                        for k2 in range(2):
                                            x0 = (ql % 2) * 16 + 2 * kb + k2
                                            inst = e.matmul(ps[sbank][:, j * 256 + ql * 64:j * 256 + (ql + 1) * 64],
                                                            lhsT=sel2[pb:pb + 64, k2 * 128:(k2 + 1) * 128],
                                                            rhs=Bvs[ql // 2][:, x0, 0:64],
                                                            start=False, stop=(ql == 3 and k2 == 1))
                                else:
                                    kc_ = kb - NKB
                                    inst = e.matmul(o, lhsT=Kc[pb:pb + 64, pr, kc_ * 128:(kc_ + 1) * 128],
                                                    rhs=Rs(SL_Q + pr)[pb:pb + 64, 0:256], start=True, stop=True)
                            return inst
                        P.add("pe", fs, reads=kall_res + ["Kc", "id2", "sel2", "cmt", ("Bpad", 0), ("Bpad", 1)] + Rres(SL_Q + pr),
                              writes=[("ps", sbank)])
                        P.add("act", lambda e, kb2=kb2, sbank=sbank, pi=pi: e.activation(
                            out=pT[pi][:, kb2 * 512:(kb2 + 1) * 512], in_=ps[sbank][:, :], func=AF.Exp, scale=0.125),
                            reads=[("ps", sbank)], writes=[("pT", pi, kb2)])
                    obank, dbank = 6, 7

                    def fpv(e, pr=pr, hh=hh, pi=pi, obank=obank, dbank=dbank):
                        inst = None
                        for kb in range(nblk):
                            lhs = Vall[:, kb, pr * 128:(pr + 1) * 128] if kb < NKB else Vc[:, kb - NKB, pr * 128:(pr + 1) * 128]
                            e.matmul(ps[obank][:, hh * 256:(hh + 1) * 256], lhsT=lhs,
                                     rhs=pT[pi][:, kb * 256:(kb + 1) * 256], start=(kb == 0), stop=(kb == nblk - 1))
                            inst = e.matmul(ps[dbank][:, hh * 256:(hh + 1) * 256], lhsT=ones[:],
                                            rhs=pT[pi][:, kb * 256:(kb + 1) * 256], start=(kb == 0), stop=(kb == nblk - 1))
                        return inst
                    P.add("pe", fpv, reads=[("pT", pi, k) for k in range(nblk // 2)] + vall_res + ["Vc", "ones"],
                          writes=[("ps", obank), ("ps", dbank)])
                    if hh == 1:
                        t = nxt("tmp", 3)
                        P.add("dve", lambda e, dbank=dbank, t=t: e.reciprocal(out=tmpf[t][:], in_=ps[dbank][:]),
                              reads=[("ps", dbank)], writes=[("tmp", t)])
                        for h2 in range(2):
                            p2 = h2 * 64
                            P.add("dve", lambda e, pr=pr, h2=h2, p2=p2, obank=obank, t=t: e.tensor_tensor(
                                out=Rs(SL_YB + pr)[p2:p2 + 64, 0:256],
                                in0=ps[obank][p2:p2 + 64, h2 * 256:(h2 + 1) * 256],
                                in1=tmpf[t][p2:p2 + 64, h2 * 256:(h2 + 1) * 256], op=ALU.mult),
                                reads=[("ps", obank), ("tmp", t)], writes=[("YBh", pr, h2, 0)])


            for l in range(L):
                layer(l, pname, col0, T)
            i = dma("sp", yout[:, col0:col0 + T].rearrange("(c p) t -> p c t", p=128), x[:, :, :T],
                    reads=["x"], writes=[("yout", pname)])
            out_dmas.append(i)

        for (pname, col0, T) in passes:
            run_pass(pname, col0, T)

        P.emit(out_dmas)
    return nc


def _rope_tables(row0):
    half = 16
    freqs = (10000.0 ** (-np.arange(half, dtype=np.float32) / half)).astype(np.float32)
    t = np.arange(256)
    prow = (row0 + t // 64).astype(np.float32)
    pcol = (t % 64).astype(np.float32)
    C = np.zeros((128, 256), np.float32)
    S = np.zeros((128, 256), np.float32)
    for p in range(128):
        d = p % 64
        pos = prow if d < 32 else pcol
        ang = pos * freqs[d % 16]
        C[p] = np.cos(ang.astype(np.float32))
        S[p] = np.sin(ang.astype(np.float32))
    return C, S


def _perm_T():
    M = np.zeros((128, 128), np.float32)
    for m in range(128):
        if m % 32 < 16:
            M[m + 16, m] = -1.0
        else:
            M[m - 16, m] = 1.0
    return M


def _E_table(row0, rows=16, kh=8):
    E = np.zeros((16, 64), np.float32)
    for ql in range(4):
        qr = row0 + ql
        rs = int(np.clip(qr - kh // 2, 0, rows - kh))
        for kr in range(16):
            xi = ql * 16 + kr
            if rs <= kr < rs + kh:
                E[kr - qr + 7, xi] = 8.0
            else:
                E[15, xi] = 8.0
    return E


def _cmt():
    cols = np.arange(64)
    cstart = np.clip(cols - 8, 0, 48)
    m = np.full((64, 4, 64), NEG, np.float32)
    for qc in range(64):
        m[cstart[qc]:cstart[qc] + 16, :, qc] = 0.0
    return np.concatenate([m.reshape(64, 256)] * 2, 0)


def make_inputs(inp, core, n_layers=DEPTH, n_cores=8):
    f = lambda a: np.ascontiguousarray(np.asarray(a, dtype=np.float32))
    L = n_layers
    b = core // 4
    r = core % 4
    xs = np.asarray(inp["x_sample"])[b, r * 256:(r + 1) * 256]
    xp = np.asarray(inp["x_prompt"])[4 * core:4 * core + 4].reshape(1024, D)
    xin = f(np.concatenate([xs, xp], 0).T)
    cvT = f(np.stack([np.asarray(inp["c_ctx"]), np.asarray(inp["c"])[b]], 1))
    gvec = np.stack([np.asarray(inp[k])[:L].reshape(L, 16, 128) for k in
                     ("g_pre_mix", "g_post_mix", "g_pre_ffn", "g_post_ffn")], 1)
    C, S = _rope_tables(4 * r)
    m = {
        "xin": xin, "cvT": cvT,
        "w_mod": inp["w_mod"][:L], "b_modT": f(np.asarray(inp["b_mod"])[:L].reshape(L, 96, 128).transpose(0, 2, 1)),
        "gvec": f(gvec.transpose(0, 3, 1, 2)),
        "w_in": inp["w_in"][:L], "w_pa": inp["w_pa"][:L], "w_pb": inp["w_pb"][:L], "w_out": inp["w_out"][:L],
        "w_ff1": inp["w_ff1"][:L], "w_ff2": inp["w_ff2"][:L],
        "lng": f(np.asarray(inp["sgu_ln_g"])[:L]), "lnb": f(np.asarray(inp["sgu_ln_b"])[:L]),
        "sgu_wT": f(np.asarray(inp["sgu_w"])[:L].transpose(0, 3, 1, 2).reshape(L, 128, 1024)),
        "sgu_b": f(np.asarray(inp["sgu_b"])[:L].reshape(L, 1, 1024)),
        "ropeC": C, "ropeS": S, "permT": _perm_T(), "Emat": _E_table(4 * r), "cmt": _cmt(),
        "rpb": f(np.concatenate([np.asarray(inp["na_rpb"])[:L][..., ::-1].transpose(0, 2, 1, 3),
                                 np.full((L, 1, NH, 31), NEG / 8.0, np.float32)], 1)),
        "ckT": f(np.asarray(inp["cache_ctx_k"])[b, :L].transpose(0, 1, 3, 2).reshape(L, 1024, 512)),
        "cvv": f(np.asarray(inp["cache_ctx_v"])[b, :L].transpose(0, 2, 1, 3).reshape(L, 512, 1024)),
        "ident2": f(np.concatenate([np.concatenate([np.eye(64), np.eye(64)], 1)] * 2, 0)),
        "selmat": f(np.concatenate([np.concatenate([np.eye(64), np.zeros((64, 64)), np.zeros((64, 64)), np.eye(64)], 1)] * 2, 0)),
    }
    return m


_NC_CACHE = {}


def kernel(**inputs):
    inp = {k: np.asarray(v) for k, v in inputs.items()}
    if "full" not in _NC_CACHE:
        _NC_CACHE["full"] = build()
    nc = _NC_CACHE["full"]
    in_maps = [make_inputs(inp, c) for c in range(8)]
    res = run_bass_kernel_spmd(nc, in_maps, core_ids=list(range(8)))
    B, SEQ = 32, 256
    y_prompt = np.zeros((B, SEQ, D), np.float32)
    y_sample = np.zeros((2, 1024, D), np.float32)
    sk = np.zeros((B, DEPTH, NH, SEQ, 64), np.float32)
    sv = np.zeros((B, DEPTH, NH, SEQ, 64), np.float32)
    for c in range(8):
        r = res.results[c]
        yo = np.asarray(r["yout"]).T
        y_sample[c // 4, (c % 4) * 256:(c % 4 + 1) * 256] = yo[:256]
        y_prompt[4 * c:4 * c + 4] = yo[256:].reshape(4, SEQ, D)
        k = np.asarray(r["kst"])
        sk[4 * c:4 * c + 4] = k.reshape(DEPTH, NH, 64, 4, SEQ).transpose(3, 0, 1, 4, 2)
        v = np.asarray(r["vst"])
        sv[4 * c:4 * c + 4] = v.reshape(DEPTH, 4, SEQ, NH, 64).transpose(1, 0, 3, 2, 4)
    return (y_prompt, y_sample, sk, sv)
```

```python
import numpy as np
from contextlib import ExitStack
import concourse.bass as bass
import concourse.mybir as mybir
from concourse.bass_utils import run_bass_kernel_spmd

F32 = mybir.dt.float32
BF16 = mybir.dt.bfloat16
AF = mybir.ActivationFunctionType
ALU = mybir.AluOpType

D = 2048
NKC = 16
DEPTH = 4
NH = 16
AW = 1024
DFF = 8192
INC = 9216
EPS = 1e-6
NEG = -30000.0
PK = 94
LROW = 64 + 64 * PK + 64
LROW2 = 64 + 32 * PK + 64
import os
DBG = bool(os.environ.get("KDBG"))
NSLOT = 80
NWB = 3
WBE = 4096


class Op:
    __slots__ = ("eng", "fn", "deps", "signal", "token", "dma_sem", "inc")

    def __init__(self, eng, fn, dma_sem):
        self.eng = eng
        self.fn = fn
        self.deps = set()
        self.signal = False
        self.token = None
        self.dma_sem = dma_sem
        self.inc = 16


class Prog:
    ENGS = ("pe", "act", "dve", "pool", "sp")

    def __init__(self, nc, stack):
        self.nc = nc
        self.stack = stack
        self.ops = []
        self.lastw = {}
        self.readers = {}
        self.dma_cnt = {}
        self.last_barrier = -1
        self.nsem = 0
        self.esem = {e: self.new_sem("e_" + e) for e in self.ENGS}

    def new_sem(self, name):
        self.nsem += 1
        return self.stack.enter_context(self.nc.semaphore(name))

    def add(self, eng, fn, reads=(), writes=(), dma_sem=None, inc=16):
        i = len(self.ops)
        op = Op(eng, fn, dma_sem)
        op.inc = inc
        ops = self.ops
        for r in reads:
            w = self.lastw.get(r)
            if w is not None:
                wo = ops[w]
                if not (wo.eng == eng and eng == "pe" and wo.dma_sem is None):
                    op.deps.add(w)
            if isinstance(r, tuple) and r[0] == "ps":
                for rd in self.readers.get(r, ()):
                    if ops[rd].eng != eng:
                        op.deps.add(rd)
        for r in writes:
            w = self.lastw.get(r)
            if w is not None:
                wo = ops[w]
                if wo.dma_sem is not None or wo.eng != eng or dma_sem is not None:
                    op.deps.add(w)
            for rd in self.readers.get(r, ()):
                ro = ops[rd]
                if ro.dma_sem is not None or ro.eng != eng or dma_sem is not None:
                    op.deps.add(rd)
        for r in reads:
            self.readers.setdefault(r, []).append(i)
        for r in writes:
            self.lastw[r] = i
            self.readers[r] = []
        for d in op.deps:
            ops[d].signal = True
        if dma_sem is not None:
            c = self.dma_cnt.get(dma_sem, 0) + 1
            self.dma_cnt[dma_sem] = c
            op.token = (dma_sem, inc * c)
            op.signal = True
        ops.append(op)
        return i

    def barrier(self):
        last = {}
        dmas = []
        for i, op in enumerate(self.ops):
            if op.dma_sem is not None:
                if i > self.last_barrier:
                    dmas.append(i)
            elif op.fn is not None:
                last[op.eng] = i
        deps = set(last.values()) | set(dmas)
        for e in self.ENGS:
            op = Op(e, None, None)
            op.deps = set(d for d in deps if not (self.ops[d].eng == e and self.ops[d].dma_sem is None))
            for d in op.deps:
                self.ops[d].signal = True
            self.ops.append(op)
        self.last_barrier = len(self.ops)
        self.lastw = {k: v for k, v in self.lastw.items() if isinstance(k, tuple) and k[0] in ("W", "ps")
                      or k in ("ones", "scb", "cf", "bmod", "gv") or (isinstance(k, tuple) and k[0] in ("mod", "vecs"))}
        self.readers = {k: v for k, v in self.readers.items() if k in self.lastw}

    def emit(self, final_waits):
        nc = self.nc
        import os as _os
        _n = int(_os.environ.get("KSTOP", "0"))
        if _n:
            self.ops = self.ops[:_n]
            final_waits = [i for i in final_waits if i < _n]
            print("KSTOP", _n, "nsem", self.nsem)
        cnt = {e: 0 for e in self.ENGS}
        for op in self.ops:
            if op.dma_sem is None and op.signal:
                cnt[op.eng] += 1
                op.token = (self.esem[op.eng], cnt[op.eng])
        per = {e: [] for e in self.ENGS}
        for op in self.ops:
            per[op.eng].append(op)
        ops = self.ops

        def run(engname, eng):
            waited = {}
            for op in per[engname]:
                need = {}
                for d in op.deps:
                    sem, val = ops[d].token
                    if need.get(sem, (0, 0))[1] < val:
                        need[sem] = (d, val)
                for sem, (d, val) in sorted(need.items(), key=lambda kv: kv[1][0]):
                    if waited.get(sem, 0) < val:
                        eng.wait_ge(sem, val)
                        waited[sem] = val
                inst = op.fn(eng) if op.fn is not None else None
                if inst is not None and op.signal:
                    if op.dma_sem is not None and op.inc == 1:
                        inst.then_inc(op.dma_sem)
                    elif op.dma_sem is not None:
                        inst.then_inc(op.dma_sem, 16)
                    else:
                        inst.then_inc(self.esem[engname], 1)
            if engname == "sp":
                for i in final_waits:
                    sem, val = ops[i].token
                    eng.wait_ge(sem, val)

        with nc.Block() as block:
            @block.tensor
            def _(e):
                run("pe", e)

            @block.scalar
            def _(e):
                run("act", e)

            @block.vector
            def _(e):
                run("dve", e)

            @block.gpsimd
            def _(e):
                run("pool", e)

            @block.sync
            def _(e):
                run("sp", e)


def build(n_layers=DEPTH, passes=(("S", 0, 256), ("A", 256, 512), ("B", 768, 512)),
          n_tok=1280, group=4, n_cores=8):
    nc = bass.Bass("TRN2", target_bir_lowering=False)
    L = n_layers

    def din(name, shape, dt=F32):
        return nc.dram_tensor(name, list(shape), dt, kind="ExternalInput")

    xin = din("xin", [D, n_tok]).ap()
    cvT = din("cvT", [D, 2]).ap()
    w_mod = din("w_mod", [L, D, 6 * D]).ap()
    b_modT = din("b_modT", [L, 128, 96]).ap()
    gvec = din("gvec", [L, 128, 4, 16]).ap()
    w_in = din("w_in", [L, D, INC]).ap()
    w_pa = din("w_pa", [L, AW, D]).ap()
    w_pb = din("w_pb", [L, AW, D]).ap()
    w_out = din("w_out", [L, D, D]).ap()
    w_ff1 = din("w_ff1", [L, D, DFF]).ap()
    w_ff2 = din("w_ff2", [L, DFF, D]).ap()
    lng = din("lng", [L, AW]).ap()
    lnb = din("lnb", [L, AW]).ap()
    sgu_wT = din("sgu_wT", [L, 128, 8 * 128]).ap()
    sgu_b = din("sgu_b", [L, 1, AW]).ap()
    ropeC = din("ropeC", [128, 256]).ap()
    ropeS = din("ropeS", [128, 256]).ap()
    permT = din("permT", [128, 128]).ap()
    Emat = din("Emat", [16, 64]).ap()
    cmt = din("cmt", [128, 256]).ap()
    rpb = din("rpb", [L, 16, NH, 31]).ap()
    ckT = din("ckT", [L, 1024, 512]).ap()
    cvv = din("cvv", [L, 512, 1024]).ap()
    ident2 = din("ident2", [128, 128]).ap()
    selmat = din("selmat", [128, 256]).ap()

    yout = nc.dram_tensor("yout", [D, n_tok], F32, kind="ExternalOutput").ap()
    kst = nc.dram_tensor("kst", [L, 1024, 1024], F32, kind="ExternalOutput").ap()
    vst = nc.dram_tensor("vst", [L, 1024, 1024], F32, kind="ExternalOutput").ap()

    has_S = any(p[0] == "S" for p in passes)
    kv_loc = nc.dram_tensor("kv_loc", [128, 4096], BF16)
    kv_all = nc.dram_tensor("kv_all", [group * 128, 4096], BF16)
    d3 = nc.dram_tensor("d3", [2, 68 * 32 * PK], BF16)

    with ExitStack() as st:
        P = Prog(nc, st)

        def sb(name, shape, dt):
            return st.enter_context(nc.sbuf_tensor(name, list(shape), dt))

        def dsem(name):
            return P.new_sem(name)

        ARENA = 88064
        arena = sb("arena", [128, ARENA], BF16)
        Wb = sb("W", [128, NWB * WBE], BF16)
        ones = sb("ones", [128, 128], BF16)
        mod = sb("mod", [128, L, 96, 2], F32)
        vecs = sb("vecs", [128, L, 6, 16, 2], F32)
        bmod = sb("bmod", [128, L, 96], F32)
        gv = sb("gv", [128, L, 4, 16], F32)
        scb = sb("scb", [128, NKC, 2], BF16)
        cf = sb("cf", [128, NKC, 2], F32)
        lnst = sb("lnst", [128, 4, 16], F32)
        ps = [st.enter_context(nc.psum_tensor("ps%d" % i, [128, 512], F32)) for i in range(8)]

        sems = {}

        def ksem(k):
            if k not in sems:
                sems[k] = P.new_sem("d%d" % len(sems))
            return sems[k]

        def dma(eng, out, in_, reads=(), writes=(), key=None):
            k = key if key is not None else writes[0]
            return P.add(eng, lambda e: e.dma_start(out=out, in_=in_), reads, writes, dma_sem=ksem(k))

        state = {"nbank": 8, "bank": 0, "wb": 0, "sq": 0, "tmp": 0, "kst": 0, "vst": 0, "relu": 0, "pT": 0}
        out_dmas = []

        def nxt(key, n):
            v = state[key]
            state[key] = (v + 1) % n
            return v

        P.add("pool", lambda e: e.memset(ones[:], 1.0), writes=["ones"])
        dma("sp", cf[:], cvT.rearrange("(c p) j -> p c j", p=128), writes=["cf"])
        dma("sp", bmod[:], b_modT.rearrange("l p n -> p l n"), writes=["bmod"])
        dma("sp", gv[:], gvec.rearrange("l p a c -> p l a c"), writes=["gv"])
        P.add("act", lambda e: e.activation(out=scb[:], in_=cf[:], func=AF.Silu), reads=["cf"], writes=["scb"])

        def load_w(src_ap, nk, nc_):
            b = nxt("wb", NWB)
            dst = Wb[:, b * WBE:b * WBE + nk * nc_].rearrange("p (k n) -> p k n", k=nk)
            dma("pool", dst, src_ap, writes=[("W", b)])
            return dst, ("W", b)

        def wview(w2d, r0, nk, c0, nc_):
            return w2d[r0:r0 + nk * 128, c0:c0 + nc_].rearrange("(k p) n -> p k n", p=128)

        for l in range(L):
            bank = nxt("bank", state["nbank"])
            pm = ps[bank][:, 0:192].rearrange("p (n j) -> p n j", j=2)
            for blk in range(48):
                wb_, wres = load_w(wview(w_mod[l], 0, NKC, blk * 256, 256), NKC, 256)

                def f(e, wb_=wb_, blk=blk, pm=pm):
                    inst = None
                    for nn in range(2):
                        for kc in range(NKC):
                            inst = e.matmul(pm[:, blk * 2 + nn, :], lhsT=wb_[:, kc, nn * 128:(nn + 1) * 128],
                                            rhs=scb[:, kc, :], start=(kc == 0), stop=(kc == NKC - 1))
                    return inst
                P.add("pe", f, reads=[wres, "scb"], writes=[("ps", bank)])
            P.add("dve", lambda e, l=l, pm=pm: e.tensor_tensor(
                out=mod[:, l], in0=pm, in1=bmod[:, l].unsqueeze(2).to_broadcast([128, 96, 2]), op=ALU.add),
                reads=[("ps", bank), "bmod"], writes=[("mod", l)])
            def gb(l, a):
                return gv[:, l, a].unsqueeze(2).to_broadcast([128, 16, 2])
            P.add("dve", lambda e, l=l: e.scalar_tensor_tensor(
                out=vecs[:, l, 0], in0=mod[:, l, 16:32, :], scalar=1.0, in1=gb(l, 0), op0=ALU.add, op1=ALU.mult),
                reads=[("mod", l), "gv"], writes=[("vecs", l, 0)])
            P.add("dve", lambda e, l=l: e.tensor_copy(out=vecs[:, l, 1], in_=mod[:, l, 0:16, :]),
                  reads=[("mod", l)], writes=[("vecs", l, 1)])
            P.add("dve", lambda e, l=l: e.tensor_tensor(out=vecs[:, l, 2], in0=mod[:, l, 32:48, :], in1=gb(l, 1),
                                                        op=ALU.mult),
                  reads=[("mod", l), "gv"], writes=[("vecs", l, 2)])
            P.add("dve", lambda e, l=l: e.scalar_tensor_tensor(
                out=vecs[:, l, 3], in0=mod[:, l, 64:80, :], scalar=1.0, in1=gb(l, 2), op0=ALU.add, op1=ALU.mult),
                reads=[("mod", l), "gv"], writes=[("vecs", l, 3)])
            P.add("dve", lambda e, l=l: e.tensor_copy(out=vecs[:, l, 4], in_=mod[:, l, 48:64, :]),
                  reads=[("mod", l)], writes=[("vecs", l, 4)])
            P.add("dve", lambda e, l=l: e.tensor_tensor(out=vecs[:, l, 5], in0=mod[:, l, 80:96, :], in1=gb(l, 3),
                                                        op=ALU.mult),
                  reads=[("mod", l), "gv"], writes=[("vecs", l, 5)])

        def vres(l):
            return [("vecs", l, a) for a in range(6)]

        def run_pass(pname, col0, T):
            isS = pname == "S"
            SW = T
            cur = [0]

            def carve(nel, dt=BF16, parts=128):
                nb = nel * (2 if dt == F32 else 1)
                nb = (nb + 15) // 16 * 16
                v = arena[0:parts, cur[0]:cur[0] + nb]
                cur[0] += nb
                assert cur[0] <= ARENA, (pname, cur[0])
                return v.bitcast(F32) if dt == F32 else v
            x = carve(NKC * SW, F32).rearrange("p (c t) -> p c t", c=NKC)
            hb = carve(NKC * SW)
            R = carve(64 * SW)
            F1r = carve(8 * SW, F32)
            lngb = carve(AW, F32)
            lnbb = carve(AW, F32)
            wsT = carve(8 * 128)
            bsrow = carve(AW, BF16, 1)
            sq = [carve(SW) for i in range(2)]
            rstd = carve(SW, F32)
            tmpf = [carve(512, F32) for i in range(3)]
            relu_t = [carve(SW) for i in range(2)]
            if not isS:
                pT = [carve(512) for i in range(2)]
                kstg = [carve(512, F32) for i in range(2)]
                vstg = [carve(256, F32) for i in range(2)]
            else:
                pT = [carve(12 * 256) for i in range(1)]
                ropeCs = carve(256, F32)
                ropeSs = carve(256, F32)
                permTs = carve(128)
                Es = carve(64, F32, 16)
                cmts = carve(256, BF16, 128)
                id2 = carve(128, BF16, 128)
                sel2 = carve(256, BF16, 128)
                rpbx = carve(NH * 31, F32, 16).rearrange("p (h c) -> p h c", c=31)
                EP = [carve(32 * 31, BF16, 16)]
                RSp = [carve(32 * PK, BF16, 64)]
                Bpad = [carve(32 * PK, BF16, 128) for i in range(2)]
                Kall = carve(8 * group * 256).rearrange("p (c k) -> p c k", c=8)
                Vall = carve(group * 2 * 1024).rearrange("p (t f) -> p t f", f=1024)
                Kc = carve(8 * 512).rearrange("p (c k) -> p c k", c=8)
                Vc = carve(4 * 1024).rearrange("p (t f) -> p t f", f=1024)
                qb16 = [carve(256) for i in range(2)]
            NPT = len(pT)
            DBG and print("ARENA", pname, cur[0], ARENA)

            def Rs(slot, n=1):
                return R[:, slot * SW:(slot + n) * SW]

            def Rres(slot, n=1):
                return [("R", s_) for s_ in range(slot, slot + n)]

            def hview(T):
                return hb[:, :].rearrange("p (c t) -> p c t", c=NKC)[:, :, :T]

            P.barrier()
            dma("sp", x[:, :, :T], xin[:, col0:col0 + T].rearrange("(c p) t -> p c t", p=128), writes=["x"])
            if isS:
                dma("sp", ropeCs[:], ropeC, writes=["ropeC"])
                dma("sp", ropeSs[:], ropeS, writes=["ropeS"])
                dma("sp", Es[:], Emat, writes=["Es"])
                dma("pool", permTs[:], permT, writes=["permT"])
                dma("pool", cmts[:], cmt, writes=["cmt"])
                dma("pool", id2[:], ident2, writes=["id2"])
                dma("pool", sel2[:], selmat, writes=["sel2"])
                P.add("pool", lambda e: e.memset(RSp[0][:], 0.0), writes=[("RSp", 0)])
                for i in range(2):
                    P.add("pool", lambda e, i=i: e.memset(Bpad[i][:], 0.0), writes=[("Bpad", i)])
                d3a = d3.ap()
                for i in range(2):
                    dma("sp", d3a[i, :].rearrange("(p m) -> p m", p=68)[0:64, :], Bpad[0][0:64, :],
                        reads=[("Bpad", 0)], writes=[("d3", i)], key=("d3z", i))
                    dma("sp", d3a[i, :].rearrange("(p m) -> p m", p=68)[64:68, :], Bpad[0][0:4, :],
                        reads=[("Bpad", 0)], writes=[("d3", i)], key=("d3z", i))

            def rms_stats(src_fn, src_res, T):
                bank = nxt("bank", state["nbank"])
                for kc in range(NKC):
                    s = nxt("sq", 2)
                    P.add("act", lambda e, kc=kc, s=s: e.activation(out=sq[s][:, :T], in_=src_fn(kc), func=AF.Square),
                          reads=src_res(kc), writes=[("sq", s)])
                    P.add("pe", lambda e, kc=kc, s=s: e.matmul(ps[bank][:, :T], lhsT=ones[:], rhs=sq[s][:, :T],
                                                               start=(kc == 0), stop=(kc == NKC - 1)),
                          reads=[("sq", s), "ones"], writes=[("ps", bank)])
                P.add("act", lambda e: e.activation(out=rstd[:, :T], in_=ps[bank][:, :T], func=AF.Sqrt,
                                                    scale=1.0 / D, bias=EPS),
                      reads=[("ps", bank)], writes=["rstd"])
                P.add("dve", lambda e: e.reciprocal(out=rstd[:, :T], in_=rstd[:, :T]), reads=["rstd"], writes=["rstd"])

            def norm_mod(l, cv, T, ai, bi):
                rms_stats(lambda kc: x[:, kc, :T], lambda kc: ["x"], T)
                hv = hview(T)
                for kc in range(NKC):
                    t = nxt("tmp", 3)
                    P.add("dve", lambda e, kc=kc, t=t: e.scalar_tensor_tensor(
                        out=tmpf[t][:, :T], in0=x[:, kc, :T], scalar=vecs[:, l, ai, kc, cv:cv + 1], in1=rstd[:, :T],
                        op0=ALU.mult, op1=ALU.mult),
                        reads=["x", "rstd"] + vres(l), writes=[("tmp", t)])
                    P.add("act", lambda e, kc=kc, t=t: e.activation(
                        out=hv[:, kc, :], in_=tmpf[t][:, :T], func=AF.Identity, bias=vecs[:, l, bi, kc, cv:cv + 1],
                        scale=1.0),
                        reads=[("tmp", t)] + vres(l), writes=["h"])

            def proj_fm(w2d, r0, nk, c0, ncols, rhs_fn, rhs_res, T, evac, nb0=0):
                if nk != 16:
                    bc = min(WBE // nk, ncols)
                    for b in range(ncols // bc):
                        wb_, wres = load_w(wview(w2d, r0, nk, c0 + b * bc, bc), nk, bc)
                        for nn in range(bc // 128):
                            bank = nxt("bank", state["nbank"])

                            def f(e, wb_=wb_, nn=nn, bank=bank):
                                inst = None
                                for kc in range(nk):
                                    inst = e.matmul(ps[bank][:, :T], lhsT=wb_[:, kc, nn * 128:(nn + 1) * 128],
                                                    rhs=rhs_fn(kc), start=(kc == 0), stop=(kc == nk - 1))
                                return inst
                            P.add("pe", f, reads=[wres] + rhs_res, writes=[("ps", bank)])
                            evac(nb0 + b * (bc // 128) + nn, bank)
                    return
                for b in range(ncols // 512):
                    banks = [nxt("bank", state["nbank"]) for _ in range(4)]
                    for kh in range(2):
                        wb_, wres = load_w(wview(w2d, r0 + kh * 1024, 8, c0 + b * 512, 512), 8, 512)
                        for nn in range(4):
                            def f(e, wb_=wb_, nn=nn, kh=kh, bank=banks[nn]):
                                inst = None
                                for kc in range(8):
                                    inst = e.matmul(ps[bank][:, :T], lhsT=wb_[:, kc, nn * 128:(nn + 1) * 128],
                                                    rhs=rhs_fn(kh * 8 + kc), start=(kh == 0 and kc == 0),
                                                    stop=(kh == 1 and kc == 7))
                                return inst
                            P.add("pe", f, reads=[wres] + rhs_res, writes=[("ps", banks[nn])])
                            if kh == 1:
                                evac(nb0 + b * 4 + nn, banks[nn])

            def proj_tm(w2d, c0, ncols, T, evac):
                hv = hview(T)
                for b in range(ncols // 256):
                    wb_, wres = load_w(wview(w2d, 0, NKC, c0 + b * 256, 256), NKC, 256)
                    for tb in range(T // 128):
                        bank = nxt("bank", state["nbank"])

                        def f(e, wb_=wb_, tb=tb, bank=bank):
                            inst = None
                            for kc in range(NKC):
                                inst = e.matmul(ps[bank][:, :256], lhsT=hv[:, kc, tb * 128:(tb + 1) * 128],
                                                rhs=wb_[:, kc, :], start=(kc == 0), stop=(kc == NKC - 1))
                            return inst
                        P.add("pe", f, reads=[wres, "h"], writes=[("ps", bank)])
                        evac(b, tb, bank)

            SL_U, SL_Q, SL_K, SL_V, SL_YB, SL_SGA, SL_VT, SL_SGB = 0, 8, 16, 24, 32, 40, 56, 8

            def layer(l, pname, col0, T):
                cv = 1 if pname == "S" else 0
                PO = "dve"
                NTB = T // 128
                hv = hview(T)
                dma("sp", lngb[:], lng[l:l + 1, :].partition_broadcast(128), writes=["lngb"])
                dma("sp", lnbb[:], lnb[l:l + 1, :].partition_broadcast(128), writes=["lnbb"])
                dma("pool", wsT[:], sgu_wT[l], writes=["wsT"])
                dma("pool", bsrow[:], sgu_b[l], writes=["bsrow"])

                norm_mod(l, cv, T, 0, 1)
                hres = ["h"]
                wl = w_in[l]

                DBG and print("MARK", len(P.ops), '# ---- au -> u = gelu ----')
                def ev_u(n, bank):
                    P.add("act", lambda e: e.activation(out=Rs(SL_U + n)[:, :T], in_=ps[bank][:, :T],
                                                        func=AF.Gelu_apprx_tanh),
                          reads=[("ps", bank)], writes=Rres(SL_U + n))
                proj_fm(wl, 0, NKC, 0, 1024, lambda kc: hv[:, kc, :], hres, T, ev_u)

                DBG and print("MARK", len(P.ops), '# ---- av -> token-major gelu -> VT (bf16) ----')
                VT = Rs(SL_VT, 8).rearrange("p (t f) -> p t f", f=1024)

                def ev_av(b, tb, bank):
                    P.add("act", lambda e: e.activation(out=VT[:, tb, b * 256:(b + 1) * 256], in_=ps[bank][:, :256],
                                                        func=AF.Gelu_apprx_tanh),
                          reads=[("ps", bank)], writes=[("VT", tb, b)])
                proj_tm(wl, 1024, 1024, T, ev_av)

                DBG and print("MARK", len(P.ops), '# ---- LayerNorm over features (token-major) ----')
                for tb in range(NTB):
                    allvt = [("VT", tb, b) for b in range(4)]
                    for c2 in range(2):
                        P.add("dve", lambda e, tb=tb, c2=c2: e.bn_stats(lnst[:, tb, c2 * 6:(c2 + 1) * 6],
                                                                         VT[:, tb, c2 * 512:(c2 + 1) * 512]),
                              reads=allvt, writes=[("lnst", tb, c2)])
                    P.add("dve", lambda e, tb=tb: e.bn_aggr(lnst[:, tb, 12:14], lnst[:, tb, 0:12]),
                          reads=[("lnst", tb, 0), ("lnst", tb, 1)], writes=[("lnmv", tb)])
                    P.add("act", lambda e, tb=tb: e.activation(out=lnst[:, tb, 14:15], in_=lnst[:, tb, 13:14],
                                                               func=AF.Sqrt, scale=1.0, bias=EPS),
                          reads=[("lnmv", tb)], writes=[("lnr", tb)])
                    P.add("dve", lambda e, tb=tb: e.reciprocal(out=lnst[:, tb, 15:16], in_=lnst[:, tb, 14:15]),
                          reads=[("lnr", tb)], writes=[("lnr2", tb)])
                    t = nxt("tmp", 3)
                    t2 = nxt("tmp", 3)
                    for c2 in range(2):
                        sl = slice(c2 * 512, (c2 + 1) * 512)
                        tt = t if c2 == 0 else t2
                        P.add("dve", lambda e, tb=tb, sl=sl, tt=tt: e.tensor_scalar(
                            out=tmpf[tt][:], in0=VT[:, tb, sl], scalar1=lnst[:, tb, 12:13], scalar2=lnst[:, tb, 15:16],
                            op0=ALU.subtract, op1=ALU.mult),
                            reads=allvt + [("lnmv", tb), ("lnr2", tb)], writes=[("tmp", tt)])
                        P.add("dve", lambda e, sl=sl, tt=tt: e.tensor_tensor(out=tmpf[tt][:], in0=tmpf[tt][:],
                                                                               in1=lngb[:, sl], op=ALU.mult),
                              reads=[("tmp", tt), "lngb"], writes=[("tmp", tt)])
                        P.add("dve", lambda e, tb=tb, sl=sl, tt=tt: e.tensor_tensor(out=VT[:, tb, sl], in0=tmpf[tt][:],
                                                                                      in1=lnbb[:, sl], op=ALU.add),
                              reads=[("tmp", tt), "lnbb"], writes=[("VTn", tb, c2)] + [("VT", tb, 2 * c2), ("VT", tb, 2 * c2 + 1)])

                DBG and print("MARK", len(P.ops), '# ---- q, k (feature-major) ----')
                if pname == "S":
                    def ev_rope(dst_slot):
                        def ev(n, bank):
                            qi = nxt("relu", 2)
                            P.add("act", lambda e: e.activation(out=qb16[qi][:, :T], in_=ps[bank][:, :T], func=AF.Identity),
                                  reads=[("ps", bank)], writes=[("qb16", qi)])
                            bank2 = nxt("bank", state["nbank"])
                            P.add("pe", lambda e: e.matmul(ps[bank2][:, :T], lhsT=permTs[:], rhs=qb16[qi][:, :T],
                                                           start=True, stop=True),
                                  reads=[("qb16", qi), "permT"], writes=[("ps", bank2)])
                            t = nxt("tmp", 3)
                            t2 = nxt("tmp", 3)
                            P.add("dve", lambda e: e.tensor_tensor(out=tmpf[t][:, :T], in0=ps[bank][:, :T],
                                                                   in1=ropeCs[:, :T], op=ALU.mult),
                                  reads=[("ps", bank), "ropeC"], writes=[("tmp", t)])
                            P.add("dve", lambda e: e.tensor_tensor(out=tmpf[t2][:, :T], in0=ps[bank2][:, :T],
                                                                   in1=ropeSs[:, :T], op=ALU.mult),
                                  reads=[("ps", bank2), "ropeS"], writes=[("tmp", t2)])
                            P.add("dve", lambda e: e.tensor_tensor(out=Rs(dst_slot + n)[:, :T], in0=tmpf[t][:, :T],
                                                                    in1=tmpf[t2][:, :T], op=ALU.add),
                                  reads=[("tmp", t), ("tmp", t2)], writes=Rres(dst_slot + n))
                        return ev
                    proj_fm(wl, 0, NKC, 2048, 1024, lambda kc: hv[:, kc, :], hres, T, ev_rope(SL_Q))
                    proj_fm(wl, 0, NKC, 3072, 1024, lambda kc: hv[:, kc, :], hres, T, ev_rope(SL_K))
                else:
                    def ev_q(n, bank):
                        P.add("act", lambda e: e.activation(out=Rs(SL_Q + n)[:, :T], in_=ps[bank][:, :T], func=AF.Identity),
                              reads=[("ps", bank)], writes=Rres(SL_Q + n))
                    proj_fm(wl, 0, NKC, 2048, 1024, lambda kc: hv[:, kc, :], hres, T, ev_q)

                    def ev_k(n, bank):
                        kb_ = nxt("kst", 2)
                        P.add("act", lambda e: e.activation(out=kstg[kb_][:, :T], in_=ps[bank][:, :T], func=AF.Identity),
                              reads=[("ps", bank)], writes=[("kstg", kb_)])
                        P.add("dve", lambda e: e.tensor_copy(out=Rs(SL_K + n)[:, :T], in_=ps[bank][:, :T]),
                              reads=[("ps", bank)], writes=Rres(SL_K + n))
                        i = dma("sp", kst[l, n * 128:(n + 1) * 128, col0 - 256:col0 - 256 + T], kstg[kb_][:, :T],
                                reads=[("kstg", kb_)], writes=[("kstg_out", kb_)])
                        out_dmas.append(i)
                    proj_fm(wl, 0, NKC, 3072, 1024, lambda kc: hv[:, kc, :], hres, T, ev_k)

                DBG and print("MARK", len(P.ops), '# ---- v (token-major) ----')
                Vt = Rs(SL_V, 8).rearrange("p (t f) -> p t f", f=1024)

                def ev_v(b, tb, bank):
                    if pname == "S":
                        P.add("act", lambda e: e.activation(out=Vt[:, tb, b * 256:(b + 1) * 256], in_=ps[bank][:, :256],
                                                            func=AF.Identity),
                              reads=[("ps", bank)], writes=[("Vt", tb, b)])
                        return
                    vb_ = nxt("vst", 2)
                    P.add("act", lambda e: e.activation(out=vstg[vb_][:], in_=ps[bank][:, :256], func=AF.Identity),
                          reads=[("ps", bank)], writes=[("vstg", vb_)])
                    P.add("dve", lambda e: e.tensor_copy(out=Vt[:, tb, b * 256:(b + 1) * 256], in_=ps[bank][:, :256]),
                          reads=[("ps", bank)], writes=[("Vt", tb, b)])
                    r0 = col0 - 256 + tb * 128
                    i = dma("sp", vst[l, r0:r0 + 128, b * 256:(b + 1) * 256], vstg[vb_][:],
                            reads=[("vstg", vb_)], writes=[("vstg_out", vb_)])
                    out_dmas.append(i)
                proj_tm(wl, 4096, 1024, T, ev_v)

                DBG and print("MARK", len(P.ops), '# ---- gate A (sigmoid) ----')
                def ev_sga(n, bank):
                    P.add("act", lambda e: e.activation(out=Rs(SL_SGA + n)[:, :T], in_=ps[bank][:, :T], func=AF.Sigmoid),
                          reads=[("ps", bank)], writes=Rres(SL_SGA + n))
                proj_fm(wl, 0, NKC, 5120, 2048, lambda kc: hv[:, kc, :], hres, T, ev_sga)

                DBG and print("MARK", len(P.ops), '# ---- SGU spatial mixing')
                for g in range(8):
                    bank = nxt("bank", state["nbank"])

                    def f(e, g=g, bank=bank):
                        inst = None
                        for tb in range(NTB):
                            e.matmul(ps[bank][:, tb * 128:(tb + 1) * 128], lhsT=VT[:, tb, g * 128:(g + 1) * 128],
                                     rhs=wsT[:, g * 128:(g + 1) * 128], start=True, stop=False)
                            inst = e.matmul(ps[bank][:, tb * 128:(tb + 1) * 128], lhsT=ones[0:1, :],
                                            rhs=bsrow[0:1, g * 128:(g + 1) * 128], start=False, stop=True)
                        return inst
                    P.add("pe", f, reads=[("VTn", tb, c2) for tb in range(NTB) for c2 in range(2)] + ["wsT", "bsrow", "ones"],
                          writes=[("ps", bank)])
                    P.add("dve", lambda e, g=g, bank=bank: e.tensor_tensor(out=Rs(SL_U + g)[:, :T], in0=ps[bank][:, :T],
                                                                          in1=Rs(SL_U + g)[:, :T], op=ALU.mult),
                          reads=[("ps", bank)] + Rres(SL_U + g), writes=Rres(SL_U + g))

                DBG and print("MARK", len(P.ops), '# ---- attention ----')
                if pname != "S":
                    for s in range(T // 256):
                        for pr in range(8):
                            obank = nxt("bank", state["nbank"])
                            dbank = nxt("bank", state["nbank"])
                            for hh in range(2):
                                pb = hh * 64
                                sbank = nxt("bank", state["nbank"])
                                pi = nxt("pT", 2) % NPT

                                def fqk(e, s=s, pr=pr, pb=pb, sbank=sbank):
                                    inst = None
                                    for kb in range(2):
                                        inst = e.matmul(
                                            ps[sbank][:, kb * 256:(kb + 1) * 256],
                                            lhsT=Rs(SL_K + pr)[pb:pb + 64, s * 256 + kb * 128:s * 256 + (kb + 1) * 128],
                                            rhs=Rs(SL_Q + pr)[pb:pb + 64, s * 256:(s + 1) * 256], start=True, stop=True)
                                    return inst
                                P.add("pe", fqk, reads=Rres(SL_K + pr) + Rres(SL_Q + pr), writes=[("ps", sbank)])
                                P.add("act", lambda e, sbank=sbank, pi=pi: e.activation(
                                    out=pT[pi][:, 0:512], in_=ps[sbank][:, :], func=AF.Exp, scale=0.125),
                                    reads=[("ps", sbank)], writes=[("pT", pi, 0)])

                                def fpv(e, s=s, pr=pr, hh=hh, pi=pi, obank=obank, dbank=dbank):
                                    inst = None
                                    for kb in range(2):
                                        tbk = s * 2 + kb
                                        e.matmul(ps[obank][:, hh * 256:(hh + 1) * 256],
                                                 lhsT=Vt[:, tbk, pr * 128:(pr + 1) * 128],
                                                 rhs=pT[pi][:, kb * 256:(kb + 1) * 256], start=(kb == 0), stop=(kb == 1))
                                        inst = e.matmul(ps[dbank][:, hh * 256:(hh + 1) * 256], lhsT=ones[:],
                                                        rhs=pT[pi][:, kb * 256:(kb + 1) * 256], start=(kb == 0),
                                                        stop=(kb == 1))
                                    return inst
                                P.add("pe", fpv,
                                      reads=[("pT", pi, 0), "ones"] + [("Vt", s * 2 + kb, b) for kb in range(2) for b in range(4)],
                                      writes=[("ps", obank), ("ps", dbank)])
                            t = nxt("tmp", 3)
                            P.add("dve", lambda e, dbank=dbank, t=t: e.reciprocal(out=tmpf[t][:], in_=ps[dbank][:]),
                                  reads=[("ps", dbank)], writes=[("tmp", t)])
                            for hh in range(2):
                                pb = hh * 64
                                P.add("dve", lambda e, s=s, pr=pr, hh=hh, pb=pb, obank=obank, t=t: e.tensor_tensor(
                                    out=Rs(SL_YB + pr)[pb:pb + 64, s * 256:(s + 1) * 256],
                                    in0=ps[obank][pb:pb + 64, hh * 256:(hh + 1) * 256],
                                    in1=tmpf[t][pb:pb + 64, hh * 256:(hh + 1) * 256], op=ALU.mult),
                                    reads=[("ps", obank), ("tmp", t)], writes=[("YBh", pr, hh, s)])
                    yb_res = lambda pr: [("YBh", pr, hh, s) for hh in range(2) for s in range(T // 256)]
                else:
                    sample_attention(l, T, Vt)
                    state["nbank"] = 8
                    yb_res = lambda pr: [("YBh", pr, hh, 0) for hh in range(2)]

                DBG and print("MARK", len(P.ops), '# ---- gate B (sigmoid) into Q/K slots ----')
                def ev_sgb(n, bank):
                    P.add("act", lambda e: e.activation(out=Rs(SL_SGB + n)[:, :T], in_=ps[bank][:, :T], func=AF.Sigmoid),
                          reads=[("ps", bank)], writes=Rres(SL_SGB + n) + ([("YBdummy",)] if False else []))
                proj_fm(wl, 0, NKC, 7168, 2048, lambda kc: hv[:, kc, :], hres, T, ev_sgb)

                DBG and print("MARK", len(P.ops), '# ---- pa / pb projections')
                for qt in range(4):
                    wa_, war_ = load_w(wview(w_pa[l], 0, 8, qt * 512, 512), 8, 512)
                    wb2_, wbr_ = load_w(wview(w_pb[l], 0, 8, qt * 512, 512), 8, 512)
                    for nn in range(4):
                        n = qt * 4 + nn
                        c = nn * 128
                        ba = nxt("bank", state["nbank"])
                        bb = nxt("bank", state["nbank"])

                        def fa(e, wa_=wa_, c=c, ba=ba):
                            inst = None
                            for kc in range(8):
                                inst = e.matmul(ps[ba][:, :T], lhsT=wa_[:, kc, c:c + 128], rhs=Rs(SL_U + kc)[:, :T],
                                                start=(kc == 0), stop=(kc == 7))
                            return inst
                        P.add("pe", fa, reads=[war_] + Rres(SL_U, 8), writes=[("ps", ba)])

                        def fb(e, wb2_=wb2_, c=c, bb=bb):
                            inst = None
                            for kc in range(8):
                                inst = e.matmul(ps[bb][:, :T], lhsT=wb2_[:, kc, c:c + 128], rhs=Rs(SL_YB + kc)[:, :T],
                                                start=(kc == 0), stop=(kc == 7))
                            return inst
                        P.add("pe", fb, reads=[wbr_] + [r for pr in range(8) for r in yb_res(pr)], writes=[("ps", bb)])
                        t = nxt("tmp", 3)
                        t2 = nxt("tmp", 3)
                        P.add("dve", lambda e, n=n, ba=ba, t=t: e.tensor_tensor(
                            out=tmpf[t][:, :T], in0=ps[ba][:, :T], in1=Rs(SL_SGA + n)[:, :T], op=ALU.mult),
                            reads=[("ps", ba)] + Rres(SL_SGA + n), writes=[("tmp", t)])
                        P.add("dve", lambda e, n=n, bb=bb, t2=t2: e.tensor_tensor(
                            out=tmpf[t2][:, :T], in0=ps[bb][:, :T], in1=Rs(SL_SGB + n)[:, :T], op=ALU.mult),
                            reads=[("ps", bb)] + Rres(SL_SGB + n), writes=[("tmp", t2)])
                        P.add(PO, lambda e, n=n, t=t, t2=t2: e.tensor_tensor(
                            out=Rs(SL_SGA + n)[:, :T], in0=tmpf[t][:, :T], in1=tmpf[t2][:, :T], op=ALU.add),
                            reads=[("tmp", t), ("tmp", t2)], writes=Rres(SL_SGA + n))

                DBG and print("MARK", len(P.ops), '# ---- w_out -> MO')
                MO = Rs(0, 32).bitcast(F32).rearrange("p (c t) -> p c t", c=NKC)
                mo_res = lambda n: Rres(2 * n, 2)

                def ev_mo(n, bank):
                    P.add("act", lambda e: e.activation(out=MO[:, n, :T], in_=ps[bank][:, :T], func=AF.Identity),
                          reads=[("ps", bank)], writes=mo_res(n))
                proj_fm(w_out[l], 0, NKC, 0, D, lambda kc: Rs(SL_SGA + kc)[:, :T], Rres(SL_SGA, 16), T, ev_mo)
                rms_stats(lambda kc: MO[:, kc, :T], mo_res, T)
                for n in range(NKC):
                    t = nxt("tmp", 3)
                    P.add("dve", lambda e, n=n, t=t: e.scalar_tensor_tensor(
                        out=tmpf[t][:, :T], in0=MO[:, n, :T], scalar=vecs[:, l, 2, n, cv:cv + 1], in1=rstd[:, :T],
                        op0=ALU.mult, op1=ALU.mult),
                        reads=mo_res(n) + ["rstd"] + vres(l), writes=[("tmp", t)])
                    P.add(PO, lambda e, n=n, t=t: e.tensor_tensor(out=x[:, n, :T], in0=x[:, n, :T], in1=tmpf[t][:, :T],
                                                                      op=ALU.add),
                          reads=[("tmp", t), "x"], writes=["x"])

                DBG and print("MARK", len(P.ops), '# ---- FFN ----')
                norm_mod(l, cv, T, 3, 4)

                def ev_ff1(n, bank):
                    r = nxt("relu", 2)
                    P.add("act", lambda e: e.activation(out=relu_t[r][:, :T], in_=ps[bank][:, :T], func=AF.Relu),
                          reads=[("ps", bank)], writes=[("relu", r)])
                    P.add(PO, lambda e: e.tensor_tensor(out=Rs(n)[:, :T], in0=relu_t[r][:, :T], in1=relu_t[r][:, :T],
                                                            op=ALU.mult),
                          reads=[("relu", r)], writes=Rres(n))
                proj_fm(w_ff1[l], 0, NKC, 0, DFF, lambda kc: hv[:, kc, :], hres, T, ev_ff1)

                F0 = hb[:, :].bitcast(F32).rearrange("p (c t) -> p c t", c=8)
                F1 = F1r[:, :].rearrange("p (c t) -> p c t", c=8)

                def Fv(n):
                    return F0[:, n, :T] if n < 8 else F1[:, n - 8, :T]

                def Fres(n):
                    return ["h"] if n < 8 else [("F1", n)]
                for c in range(8):
                    banks = [nxt("bank", state["nbank"]), nxt("bank", state["nbank"])]
                    for g in range(4):
                        wb_, wres = load_w(wview(w_ff2[l], g * 2048, NKC, c * 256, 256), NKC, 256)
                        for nn in range(2):
                            def f(e, wb_=wb_, nn=nn, g=g, bank=banks[nn]):
                                inst = None
                                for kc in range(NKC):
                                    inst = e.matmul(ps[bank][:, :T], lhsT=wb_[:, kc, nn * 128:(nn + 1) * 128],
                                                    rhs=Rs(g * 16 + kc)[:, :T], start=(g == 0 and kc == 0),
                                                    stop=(g == 3 and kc == NKC - 1))
                                return inst
                            P.add("pe", f, reads=[wres] + Rres(g * 16, 16), writes=[("ps", banks[nn])])
                    for nn in range(2):
                        n = c * 2 + nn
                        P.add("act", lambda e, n=n, bank=banks[nn]: e.activation(out=Fv(n), in_=ps[bank][:, :T],
                                                                                 func=AF.Identity),
                              reads=[("ps", banks[nn])], writes=Fres(n) + [("F", n)])
                rms_stats(lambda kc: Fv(kc), lambda kc: [("F", kc)], T)
                for n in range(NKC):
                    t = nxt("tmp", 3)
                    P.add("dve", lambda e, n=n, t=t: e.scalar_tensor_tensor(
                        out=tmpf[t][:, :T], in0=Fv(n), scalar=vecs[:, l, 5, n, cv:cv + 1], in1=rstd[:, :T],
                        op0=ALU.mult, op1=ALU.mult),
                        reads=[("F", n), "rstd"] + vres(l), writes=[("tmp", t)])
                    P.add(PO, lambda e, n=n, t=t: e.tensor_tensor(out=x[:, n, :T], in0=x[:, n, :T], in1=tmpf[t][:, :T],
                                                                      op=ALU.add),
                          reads=[("tmp", t), "x"] + Fres(n), writes=["x"])

            def sample_attention(l, T, Vt):
                G4 = group
                kvl = kv_loc.ap()
                kvl_res = [("kvl", i) for i in range(9)]
                for pr in range(8):
                    dma("sp", kvl[:, pr * 256:(pr + 1) * 256], Rs(SL_K + pr)[:, :256],
                        reads=Rres(SL_K + pr), writes=[("kvl", pr)])
                dma("sp", kvl[:, 2048:4096], Rs(SL_V, 8)[:, 0:2048],
                    reads=[("Vt", tb, b) for tb in range(2) for b in range(4)], writes=[("kvl", 8)])
                if G4 > 1:
                    groups = [list(range(g0, g0 + G4)) for g0 in range(0, n_cores, G4)]
                    P.add("pool", lambda e: e.collective_compute("AllGather", ALU.bypass, groups,
                                                                 [kv_loc.ap().opt()], [kv_all.ap().opt()]),
                          reads=kvl_res, writes=["kv_all"], dma_sem=ksem("cc"), inc=1)
                    src = kv_all.ap()
                    src_res = ["kv_all"]
                else:
                    src = kv_loc.ap()
                    src_res = kvl_res
                kva = src.rearrange("(r p) c -> p r c", p=128)
                kall_res = [("Kall", r) for r in range(G4)]
                vall_res = [("Vall", r) for r in range(G4)]
                for r in range(G4):
                    dma("sp", Kall[:, :, r * 256:(r + 1) * 256], kva[:, r, 0:2048].rearrange("p (c t) -> p c t", c=8),
                        reads=src_res, writes=[("Kall", r)], key=("kall", r))
                    dma("sp", Vall[:, 2 * r:2 * r + 2, :], kva[:, r, 2048:4096].rearrange("p (t f) -> p t f", t=2),
                        reads=src_res, writes=[("Vall", r)], key=("vall", r))
                kva_res = kall_res + vall_res
                dma("pool", Kc[:], ckT[l].rearrange("(c p) k -> p c k", p=128), writes=["Kc"])
                dma("pool", Vc[:], cvv[l].rearrange("(kb p) f -> p kb f", p=128), writes=["Vc"])
                dma("sp", rpbx[:, :, :], rpb[l], writes=["rpbx"])
                state["nbank"] = 6
                state["bank"] = 0
                NKB = 2 * G4
                nblk = NKB + 4
                for hd in range(NH):
                    pr, hh = hd // 2, hd % 2
                    pb = hh * 64
                    Bvs = []
                    for hf in range(2):
                        EPv = EP[0][:, :].rearrange("p (x c) -> p x c", c=31)
                        P.add("dve", lambda e, hd=hd, hf=hf, EPv=EPv: e.tensor_tensor(
                            out=EPv, in0=Es[:, hf * 32:(hf + 1) * 32].unsqueeze(2).to_broadcast([16, 32, 31]),
                            in1=rpbx[:, hd, :].unsqueeze(1).to_broadcast([16, 32, 31]), op=ALU.mult),
                            reads=["Es", "rpbx"], writes=[("EP", 0)])
                        RSv = RSp[0][:, :].rearrange("p (x m) -> p x m", m=PK)
                        for q4 in range(2):
                            bank = nxt("bank", state["nbank"])
                            P.add("pe", lambda e, q4=q4, bank=bank: e.matmul(
                                ps[bank][0:64, 0:496], lhsT=ones[0:16, 0:64], rhs=EP[0][:, q4 * 496:(q4 + 1) * 496],
                                start=True, stop=True),
                                reads=[("EP", 0), "ones"], writes=[("ps", bank)])
                            P.add("act", lambda e, q4=q4, bank=bank, RSv=RSv: e.activation(
                                out=RSv[:, q4 * 16:(q4 + 1) * 16, 0:31],
                                in_=ps[bank][0:64, 0:496].rearrange("p (x c) -> p x c", c=31), func=AF.Identity),
                                reads=[("ps", bank)], writes=[("RSp", 0)])
                        d3h = d3.ap()[hf]
                        wr = bass.AP(d3h.tensor, d3h.offset + 64, [[LROW2 + 1, 64], [1, 32 * PK]])
                        dma("sp", wr, RSp[0][:, :], reads=[("RSp", 0)], writes=[("d3", hf)])
                        rd = bass.AP(d3h.tensor, d3h.offset + 64 + 15, [[LROW2, 64], [1, 32 * PK]])
                        dma("sp", Bpad[hf][pb:pb + 64, :], rd, reads=[("d3", hf)], writes=[("Bpad", hf)])
                        Bvs.append(Bpad[hf][pb:pb + 64, :].rearrange("p (x m) -> p x m", m=PK))
                    pi = nxt("pT", 2) % NPT
                    for kb2 in range(nblk // 2):
                        sbank = nxt("bank", state["nbank"])

                        def fs(e, kb2=kb2, sbank=sbank, pr=pr, pb=pb, Bvs=Bvs):
                            inst = None
                            for j in range(2):
                                kb = kb2 * 2 + j
                                o = ps[sbank][:, j * 256:(j + 1) * 256]
                                if kb < NKB:
                                    e.matmul(o, lhsT=Kall[pb:pb + 64, pr, kb * 128:(kb + 1) * 128],
                                             rhs=Rs(SL_Q + pr)[pb:pb + 64, 0:256], start=True, stop=False)
                                    e.matmul(o, lhsT=id2[pb:pb + 64, :], rhs=cmts[pb:pb + 64, :], start=False, stop=False)
                                    for ql in range(4):
                                        for k2 in range(2):
                                            x0 = (ql % 2) * 16 + 2 * kb + k2
                                            inst = e.matmul(ps[sbank][:, j * 256 + ql * 64:j * 256 + (ql + 1) * 64],
                                                            lhsT=sel2[pb:pb + 64, k2 * 128:(k2 + 1) * 128],
                                                            rhs=Bvs[ql // 2][:, x0, 0:64],
                                                            start=False, stop=(ql == 3 and k2 == 1))
                                else:
                                    kc_ = kb - NKB
                                    inst = e.matmul(o, lhsT=Kc[pb:pb + 64, pr, kc_ * 128:(kc_ + 1) * 128],
                                                    rhs=Rs(SL_Q + pr)[pb:pb + 64, 0:256], start=True, stop=True)
                            return inst
                        P.add("pe", fs, reads=kall_res + ["Kc", "id2", "sel2", "cmt", ("Bpad", 0), ("Bpad", 1)] + Rres(SL_Q + pr),
                              writes=[("ps", sbank)])
                        P.add("act", lambda e, kb2=kb2, sbank=sbank, pi=pi: e.activation(
                            out=pT[pi][:, kb2 * 512:(kb2 + 1) * 512], in_=ps[sbank][:, :], func=AF.Exp, scale=0.125),
                            reads=[("ps", sbank)], writes=[("pT", pi, kb2)])
                    obank, dbank = 6, 7

                    def fpv(e, pr=pr, hh=hh, pi=pi, obank=obank, dbank=dbank):
                        inst = None
                        for kb in range(nblk):
                            lhs = Vall[:, kb, pr * 128:(pr + 1) * 128] if kb < NKB else Vc[:, kb - NKB, pr * 128:(pr + 1) * 128]
                            e.matmul(ps[obank][:, hh * 256:(hh + 1) * 256], lhsT=lhs,
                                     rhs=pT[pi][:, kb * 256:(kb + 1) * 256], start=(kb == 0), stop=(kb == nblk - 1))
                            inst = e.matmul(ps[dbank][:, hh * 256:(hh + 1) * 256], lhsT=ones[:],
                                            rhs=pT[pi][:, kb * 256:(kb + 1) * 256], start=(kb == 0), stop=(kb == nblk - 1))
                        return inst
                    P.add("pe", fpv, reads=[("pT", pi, k) for k in range(nblk // 2)] + vall_res + ["Vc", "ones"],
                          writes=[("ps", obank), ("ps", dbank)])
                    if hh == 1:
                        t = nxt("tmp", 3)
                        P.add("dve", lambda e, dbank=dbank, t=t: e.reciprocal(out=tmpf[t][:], in_=ps[dbank][:]),
                              reads=[("ps", dbank)], writes=[("tmp", t)])
                        for h2 in range(2):
                            p2 = h2 * 64
                            P.add("dve", lambda e, pr=pr, h2=h2, p2=p2, obank=obank, t=t: e.tensor_tensor(
                                out=Rs(SL_YB + pr)[p2:p2 + 64, 0:256],
                                in0=ps[obank][p2:p2 + 64, h2 * 256:(h2 + 1) * 256],
                                in1=tmpf[t][p2:p2 + 64, h2 * 256:(h2 + 1) * 256], op=ALU.mult),
                                reads=[("ps", obank), ("tmp", t)], writes=[("YBh", pr, h2, 0)])


            for l in range(L):
                layer(l, pname, col0, T)
            i = dma("sp", yout[:, col0:col0 + T].rearrange("(c p) t -> p c t", p=128), x[:, :, :T],
                    reads=["x"], writes=[("yout", pname)])
            out_dmas.append(i)

        for (pname, col0, T) in passes:
            run_pass(pname, col0, T)

        P.emit(out_dmas)
    return nc


def _rope_tables(row0):
    half = 16
    freqs = (10000.0 ** (-np.arange(half, dtype=np.float32) / half)).astype(np.float32)
    t = np.arange(256)
    prow = (row0 + t // 64).astype(np.float32)
    pcol = (t % 64).astype(np.float32)
    C = np.zeros((128, 256), np.float32)
    S = np.zeros((128, 256), np.float32)
    for p in range(128):
        d = p % 64
        pos = prow if d < 32 else pcol
        ang = pos * freqs[d % 16]
        C[p] = np.cos(ang.astype(np.float32))
        S[p] = np.sin(ang.astype(np.float32))
    return C, S


def _perm_T():
    M = np.zeros((128, 128), np.float32)
    for m in range(128):
        if m % 32 < 16:
            M[m + 16, m] = -1.0
        else:
            M[m - 16, m] = 1.0
    return M


def _E_table(row0, rows=16, kh=8):
    E = np.zeros((16, 64), np.float32)
    for ql in range(4):
        qr = row0 + ql
        rs = int(np.clip(qr - kh // 2, 0, rows - kh))
        for kr in range(16):
            xi = ql * 16 + kr
            if rs <= kr < rs + kh:
                E[kr - qr + 7, xi] = 8.0
            else:
                E[15, xi] = 8.0
    return E


def _cmt():
    cols = np.arange(64)
    cstart = np.clip(cols - 8, 0, 48)
    m = np.full((64, 4, 64), NEG, np.float32)
    for qc in range(64):
        m[cstart[qc]:cstart[qc] + 16, :, qc] = 0.0
    return np.concatenate([m.reshape(64, 256)] * 2, 0)


def make_inputs(inp, core, n_layers=DEPTH, n_cores=8):
    f = lambda a: np.ascontiguousarray(np.asarray(a, dtype=np.float32))
    L = n_layers
    b = core // 4
    r = core % 4
    xs = np.asarray(inp["x_sample"])[b, r * 256:(r + 1) * 256]
    xp = np.asarray(inp["x_prompt"])[4 * core:4 * core + 4].reshape(1024, D)
    xin = f(np.concatenate([xs, xp], 0).T)
    cvT = f(np.stack([np.asarray(inp["c_ctx"]), np.asarray(inp["c"])[b]], 1))
    gvec = np.stack([np.asarray(inp[k])[:L].reshape(L, 16, 128) for k in
                     ("g_pre_mix", "g_post_mix", "g_pre_ffn", "g_post_ffn")], 1)
    C, S = _rope_tables(4 * r)
    m = {
        "xin": xin, "cvT": cvT,
        "w_mod": inp["w_mod"][:L], "b_modT": f(np.asarray(inp["b_mod"])[:L].reshape(L, 96, 128).transpose(0, 2, 1)),
        "gvec": f(gvec.transpose(0, 3, 1, 2)),
        "w_in": inp["w_in"][:L], "w_pa": inp["w_pa"][:L], "w_pb": inp["w_pb"][:L], "w_out": inp["w_out"][:L],
        "w_ff1": inp["w_ff1"][:L], "w_ff2": inp["w_ff2"][:L],
        "lng": f(np.asarray(inp["sgu_ln_g"])[:L]), "lnb": f(np.asarray(inp["sgu_ln_b"])[:L]),
        "sgu_wT": f(np.asarray(inp["sgu_w"])[:L].transpose(0, 3, 1, 2).reshape(L, 128, 1024)),
        "sgu_b": f(np.asarray(inp["sgu_b"])[:L].reshape(L, 1, 1024)),
        "ropeC": C, "ropeS": S, "permT": _perm_T(), "Emat": _E_table(4 * r), "cmt": _cmt(),
        "rpb": f(np.concatenate([np.asarray(inp["na_rpb"])[:L][..., ::-1].transpose(0, 2, 1, 3),
                                 np.full((L, 1, NH, 31), NEG / 8.0, np.float32)], 1)),
        "ckT": f(np.asarray(inp["cache_ctx_k"])[b, :L].transpose(0, 1, 3, 2).reshape(L, 1024, 512)),
        "cvv": f(np.asarray(inp["cache_ctx_v"])[b, :L].transpose(0, 2, 1, 3).reshape(L, 512, 1024)),
        "ident2": f(np.concatenate([np.concatenate([np.eye(64), np.eye(64)], 1)] * 2, 0)),
        "selmat": f(np.concatenate([np.concatenate([np.eye(64), np.zeros((64, 64)), np.zeros((64, 64)), np.eye(64)], 1)] * 2, 0)),
    }
    return m


_NC_CACHE = {}


def kernel(**inputs):
    inp = {k: np.asarray(v) for k, v in inputs.items()}
    if "full" not in _NC_CACHE:
        _NC_CACHE["full"] = build()
    nc = _NC_CACHE["full"]
    in_maps = [make_inputs(inp, c) for c in range(8)]
    res = run_bass_kernel_spmd(nc, in_maps, core_ids=list(range(8)))
    B, SEQ = 32, 256
    y_prompt = np.zeros((B, SEQ, D), np.float32)
    y_sample = np.zeros((2, 1024, D), np.float32)
    sk = np.zeros((B, DEPTH, NH, SEQ, 64), np.float32)
    sv = np.zeros((B, DEPTH, NH, SEQ, 64), np.float32)
    for c in range(8):
        r = res.results[c]
        yo = np.asarray(r["yout"]).T
        y_sample[c // 4, (c % 4) * 256:(c % 4 + 1) * 256] = yo[:256]
        y_prompt[4 * c:4 * c + 4] = yo[256:].reshape(4, SEQ, D)
        k = np.asarray(r["kst"])
        sk[4 * c:4 * c + 4] = k.reshape(DEPTH, NH, 64, 4, SEQ).transpose(3, 0, 1, 4, 2)
        v = np.asarray(r["vst"])
        sv[4 * c:4 * c + 4] = v.reshape(DEPTH, 4, SEQ, NH, 64).transpose(1, 0, 3, 2, 4)
    return (y_prompt, y_sample, sk, sv)
```
